# Optimizing a Trainium2 kernel written in Bass

```python
import math
import jax, jax.numpy as jnp
from jax import lax
import numpy as np

D_MODEL = 1024
BATCH = 8
SEQ = 2048
DEPTH = 1

GRID_W = 64
CTX_LEN = 256
D_MIX = D_MODEL
F_WIDTH = D_MIX // 4
N_FGROUPS = 4
FGROUP_DIM = F_WIDTH // N_FGROUPS
SSD_WIDTH = D_MIX - F_WIDTH
SSD_HEAD_DIM = 64
SSD_HEADS = SSD_WIDTH // SSD_HEAD_DIM
SSD_GROUPS = 4
HEADS_PER_GROUP = SSD_HEADS // SSD_GROUPS
D_STATE = 128
BC_WIDTH = SSD_GROUPS * D_STATE
CONV_DIM = SSD_WIDTH + 2 * BC_WIDTH
D_CONV = 7
CHUNK = 128
XBC_OFF = F_WIDTH + SSD_WIDTH
IN_WIDTH = XBC_OFF + CONV_DIM + 2 * SSD_HEADS
D_FF = -(-8 * D_MODEL // (3 * 256)) * 256
N_MOD = 6
EPS = 1e-6

kernel_name = 'fourier_ssd_hybrid_prefix_block'


def rms_norm(x, g):
    xf = x.astype(jnp.float32)
    y = xf * lax.rsqrt(jnp.mean(xf * xf, axis=-1, keepdims=True) + EPS)
    return (y * g.astype(jnp.float32)).astype(x.dtype)


def modulate(h, shift, scale):
    return h * (1 + scale) + shift


def conv_rows(u, w, bias, rows, row_len):
    b, L, ch = u.shape
    half = D_CONV // 2
    up = jnp.pad(u.reshape(b, rows, row_len, ch), ((0, 0), (0, 0), (half, half), (0, 0)))
    out = bias
    for k in range(D_CONV):
        out = out + up[:, :, k:k + row_len, :] * w[:, k]
    return out.reshape(b, L, ch)


def fourier_mix(u):
    b, L, _ = u.shape
    ug = u.reshape(b, L, N_FGROUPS, FGROUP_DIM).astype(jnp.float32)
    yf = jnp.fft.fft2(ug, axes=(1, 3), norm='ortho').real
    return yf.reshape(b, L, F_WIDTH).astype(u.dtype)


def ssd_scan(xdt, dA, Bm, Cm, h0, with_output):
    b, L = xdt.shape[:2]
    nc = L // CHUNK
    xq = xdt.reshape(b, nc, CHUNK, SSD_GROUPS, HEADS_PER_GROUP, SSD_HEAD_DIM)
    aq = dA.reshape(b, nc, CHUNK, SSD_GROUPS, HEADS_PER_GROUP)
    Bq = Bm.reshape(b, nc, CHUNK, SSD_GROUPS, D_STATE)
    Cq = Cm.reshape(b, nc, CHUNK, SSD_GROUPS, D_STATE)
    a_cs = jnp.cumsum(aq, axis=2)
    a_end = a_cs[:, :, -1]
    st = jnp.einsum('bcsgn,bcsgr,bcsgrp->bcgrpn', Bq, jnp.exp(a_end[:, :, None] - a_cs), xq)

    def step(h, inp):
        s_c, a_c = inp
        return jnp.exp(a_c)[..., None, None] * h + s_c, h

    h_fin, h_in = lax.scan(step, h0, (jnp.moveaxis(st, 1, 0), jnp.moveaxis(a_end, 1, 0)))
    if not with_output:
        return None, h_fin
    h_in = jnp.moveaxis(h_in, 0, 1)
    seg = a_cs[:, :, :, None] - a_cs[:, :, None]
    lower = jnp.tril(jnp.ones((CHUNK, CHUNK), dtype=bool))[:, :, None, None]
    decay = jnp.exp(jnp.where(lower, seg, -jnp.inf))
    scores = jnp.einsum('bcqgn,bcsgn->bcqsg', Cq, Bq)
    y_diag = jnp.einsum('bcqsg,bcqsgr,bcsgrp->bcqgrp', scores, decay, xq)
    y_off = jnp.einsum('bcqgn,bcgrpn,bcqgr->bcqgrp', Cq, h_in, jnp.exp(a_cs))
    return (y_diag + y_off).reshape(b, L, SSD_HEADS, SSD_HEAD_DIM), h_fin


def ssd_direction(xs, Bm, Cm, dt_raw, dt_bias, a_log, h0, with_output):
    b, L, _ = xs.shape
    dt = jax.nn.softplus(dt_raw.astype(jnp.float32) + dt_bias.astype(jnp.float32))
    dA = dt * -jnp.exp(a_log.astype(jnp.float32))
    xdt = xs.reshape(b, L, SSD_HEADS, SSD_HEAD_DIM) * dt[..., None]
    return ssd_scan(xdt, dA, Bm.reshape(b, L, SSD_GROUPS, D_STATE),
                    Cm.reshape(b, L, SSD_GROUPS, D_STATE), h0, with_output)


def ssd_bidir(xs, Bm, Cm, dt_raw, dt_bias, a_log, h0_fwd, h0_bwd, with_output):
    flip = lambda t: jnp.flip(t, axis=1)
    y_f, h_f = ssd_direction(xs, Bm, Cm, dt_raw[..., :SSD_HEADS], dt_bias[0], a_log[0], h0_fwd, with_output)
    y_b, h_b = ssd_direction(flip(xs), flip(Bm), flip(Cm), flip(dt_raw[..., SSD_HEADS:]),
                             dt_bias[1], a_log[1], h0_bwd, with_output)
    y = y_f + flip(y_b) if with_output else None
    return y, h_f, h_b


def mixer_inputs(h, w_in, conv_w, conv_b, rows, row_len, full):
    p = h @ (w_in if full else w_in[:, XBC_OFF:])
    if full:
        u_f, z, rest = p[..., :F_WIDTH], p[..., F_WIDTH:XBC_OFF], p[..., XBC_OFF:]
    else:
        u_f, z, rest = None, None, p
    xbc = jax.nn.silu(conv_rows(rest[..., :CONV_DIM], conv_w, conv_b, rows, row_len))
    dt_raw = rest[..., CONV_DIM:]
    xs = xbc[..., :SSD_WIDTH]
    Bm = xbc[..., SSD_WIDTH:SSD_WIDTH + BC_WIDTH]
    Cm = xbc[..., SSD_WIDTH + BC_WIDTH:]
    return u_f, z, xs, Bm, Cm, dt_raw


def mixer_output(u_f, z, xs, y, d_skip, g_ssd, w_out):
    b, L, _ = xs.shape
    y = y + xs.reshape(b, L, SSD_HEADS, SSD_HEAD_DIM) * d_skip[:, None]
    y = y.reshape(b, L, SSD_WIDTH).astype(xs.dtype)
    y = rms_norm(y * jax.nn.silu(z), g_ssd)
    return jnp.concatenate([fourier_mix(u_f), y], axis=-1) @ w_out


def swiglu(h, w_gate, w_up, w_down):
    return (jax.nn.silu(h @ w_gate) * (h @ w_up)) @ w_down


def setup_inputs(seed: int = 0) -> dict:
    key = jax.random.key(seed)
    ks = jax.random.split(key, 24)
    nrm = lambda k, shape, fan_in: jax.random.normal(k, shape, jnp.float32) * fan_in ** -0.5
    gain = lambda k, n: 1.0 + 0.05 * jax.random.normal(k, (DEPTH, n), jnp.float32)
    dt0 = jnp.exp(jax.random.uniform(ks[10], (DEPTH, 2, SSD_HEADS), jnp.float32)
                  * (math.log(0.1) - math.log(0.001)) + math.log(0.001))
    return {
        'x': jax.random.normal(ks[0], (BATCH, SEQ, D_MODEL), jnp.float32),
        'c': jax.random.normal(ks[1], (BATCH, D_MODEL), jnp.float32),
        'ctx': jax.random.normal(ks[2], (BATCH, CTX_LEN, D_MODEL), jnp.float32),
        'c_ctx': jax.random.normal(ks[3], (D_MODEL,), jnp.float32),
        'w_ada': nrm(ks[4], (DEPTH, D_MODEL, N_MOD * D_MODEL), D_MODEL),
        'b_ada': 0.01 * jax.random.normal(ks[5], (DEPTH, N_MOD * D_MODEL), jnp.float32),
        'g_pre_mix': gain(ks[6], D_MODEL),
        'g_post_mix': gain(ks[7], D_MODEL),
        'g_pre_ffn': gain(ks[8], D_MODEL),
        'g_post_ffn': gain(ks[9], D_MODEL),
        'w_in': nrm(ks[11], (DEPTH, D_MODEL, IN_WIDTH), D_MODEL),
        'conv_w': nrm(ks[12], (DEPTH, CONV_DIM, D_CONV), D_CONV),
        'conv_b': 0.01 * jax.random.normal(ks[13], (DEPTH, CONV_DIM), jnp.float32),
        'dt_bias': dt0 + jnp.log(-jnp.expm1(-dt0)),
        'a_log': jnp.log(jax.random.uniform(ks[14], (DEPTH, 2, SSD_HEADS), jnp.float32, 1.0, 16.0)),
        'd_skip': 1.0 + 0.1 * jax.random.normal(ks[15], (DEPTH, SSD_HEADS), jnp.float32),
        'g_ssd': gain(ks[16], SSD_WIDTH),
        'w_out': nrm(ks[17], (DEPTH, D_MIX, D_MODEL), D_MIX),
        'w_gate': nrm(ks[18], (DEPTH, D_MODEL, D_FF), D_MODEL),
        'w_up': nrm(ks[19], (DEPTH, D_MODEL, D_FF), D_MODEL),
        'w_down': nrm(ks[20], (DEPTH, D_FF, D_MODEL), D_FF),
    }


def reference(x, c, ctx, c_ctx, w_ada, b_ada, g_pre_mix, g_post_mix, g_pre_ffn, g_post_ffn,
              w_in, conv_w, conv_b, dt_bias, a_log, d_skip, g_ssd, w_out, w_gate, w_up, w_down):
    b = x.shape[0]
    rows = x.shape[1] // GRID_W
    ctx_len = ctx.shape[1]
    h0 = jnp.zeros((b, SSD_GROUPS, HEADS_PER_GROUP, SSD_HEAD_DIM, D_STATE), jnp.float32)
    for l in range(DEPTH):
        keep_ctx = l + 1 < DEPTH
        mx = (jax.nn.silu(c) @ w_ada[l] + b_ada[l])[:, None, :]
        sm_x, cm_x, gm_x, sf_x, cf_x, gf_x = jnp.split(mx, N_MOD, axis=-1)
        mc = jax.nn.silu(c_ctx) @ w_ada[l] + b_ada[l]
        sm_c, cm_c, gm_c, sf_c, cf_c, gf_c = jnp.split(mc, N_MOD, axis=-1)

        hc = modulate(rms_norm(ctx, g_pre_mix[l]), sm_c, cm_c)
        uf_c, z_c, xs_c, B_c, C_c, dt_c = mixer_inputs(hc, w_in[l], conv_w[l], conv_b[l], 1, ctx_len, keep_ctx)
        y_c, hf_c, hb_c = ssd_bidir(xs_c, B_c, C_c, dt_c, dt_bias[l], a_log[l], h0, h0, keep_ctx)

        hx = modulate(rms_norm(x, g_pre_mix[l]), sm_x, cm_x)
        uf_x, z_x, xs_x, B_x, C_x, dt_x = mixer_inputs(hx, w_in[l], conv_w[l], conv_b[l], rows, GRID_W, True)
        y_x, _, _ = ssd_bidir(xs_x, B_x, C_x, dt_x, dt_bias[l], a_log[l], hf_c, hb_c, True)
        x = x + gm_x * rms_norm(mixer_output(uf_x, z_x, xs_x, y_x, d_skip[l], g_ssd[l], w_out[l]), g_post_mix[l])

        hx2 = modulate(rms_norm(x, g_pre_ffn[l]), sf_x, cf_x)
        x = x + gf_x * rms_norm(swiglu(hx2, w_gate[l], w_up[l], w_down[l]), g_post_ffn[l])

        if keep_ctx:
            out_c = mixer_output(uf_c, z_c, xs_c, y_c, d_skip[l], g_ssd[l], w_out[l])
            ctx = ctx + gm_c * rms_norm(out_c, g_post_mix[l])
            hc2 = modulate(rms_norm(ctx, g_pre_ffn[l]), sf_c, cf_c)
            ctx = ctx + gf_c * rms_norm(swiglu(hc2, w_gate[l], w_up[l], w_down[l]), g_post_ffn[l])
    return x
```

```python
import math
from contextlib import ExitStack

import numpy as np
import ml_dtypes

import concourse.bass as bass
import concourse.mybir as mybir
from concourse.bass_utils import run_bass_kernel_spmd

F32 = mybir.dt.float32
BF16 = mybir.dt.bfloat16
U8 = mybir.dt.uint8
AF = mybir.ActivationFunctionType
ALU = mybir.AluOpType

D = 1024
T = 2048
CT = 256
NCH = 18
INW = 2840
DFF = 2816
NFC = 22
EPS = 1e-6
ENGS = ['pe', 'act', 'dve', 'pool', 'sp']


class Op:
    __slots__ = ('eng', 'fn', 'deps', 'sig', 'sigidx', 'dma', 'dsem', 'dval')

    def __init__(self, eng, fn, dma):
        self.eng = eng
        self.fn = fn
        self.deps = {}
        self.sig = False
        self.sigidx = 0
        self.dma = dma
        self.dsem = None
        self.dval = 0


class Sched:
    def __init__(self, nc, stack):
        self.nc = nc
        self.stack = stack
        self.ops = {e: [] for e in ENGS}
        self.reg = {}
        self.dsems = {}
        self.pending = {e: None for e in ENGS}
        self.last_dma = {}

    def add(self, eng, fn, reads=(), writes=(), dma=None):
        op = Op(eng, fn, dma is not None)
        deps = op.deps
        psr = [k for k in reads if isinstance(k, tuple) and k[0] == 'ps']
        if psr:
            reads = [k for k in reads if k not in psr]
            writes = list(writes) + psr
        for k in reads:
            st = self.reg.setdefault(k, [None, []])
            if st[0] is not None:
                deps[st[0]] = 'raw'
            st[1].append(op)
        for k in writes:
            st = self.reg.setdefault(k, [None, []])
            if st[0] is not None and st[0] is not op:
                deps.setdefault(st[0], 'waw')
            for r in st[1]:
                if r is not op:
                    deps.setdefault(r, 'war')
            st[0] = op
            st[1] = []
        if self.pending[eng] is not None:
            for d in self.pending[eng]:
                deps[d] = 'raw'
            self.pending[eng] = None
        if dma is not None:
            ent = self.dsems.get(dma)
            if ent is None or ent[1] >= 224:
                self.nsem = getattr(self, 'nsem', 0) + 1
                sem = self.stack.enter_context(self.nc.semaphore('d%d' % self.nsem))
                ent = [sem, 0]
                self.dsems[dma] = ent
            ent[1] += 16
            op.dsem = ent[0]
            op.dval = ent[1]
            self.last_dma[dma] = op
        self.ops[eng].append(op)
        return op

    def barrier(self):
        last = []
        for e in ENGS:
            for o in reversed(self.ops[e]):
                if not o.dma:
                    last.append(o)
                    break
        self.pending['sp'] = list(last) + list(self.last_dma.values())
        self.reg = {}
        nop = self.add('sp', lambda e: e.nop(nofuse=True))
        import os
        for e in ENGS:
            self.pending[e] = list(last) + ([] if os.environ.get('BAR_SPONLY') else [nop])
        self.pending['sp'] = None

    @staticmethod
    def _keep(op, d, kind):
        if d.dma or op.dma:
            return True
        if d.eng != op.eng:
            return True
        if op.eng == 'pe':
            return False
        return True

    def emit(self):
        nc = self.nc
        for e in ENGS:
            for op in self.ops[e]:
                for d, kind in op.deps.items():
                    if self._keep(op, d, kind):
                        d.sig = True
        esem = {}
        for e in ENGS:
            esem[e] = self.stack.enter_context(nc.semaphore('e_' + e))
            c = 0
            for op in self.ops[e]:
                if op.sig and not op.dma:
                    c += 1
                    op.sigidx = c

        def run(e, eng):
            waited = {}
            for op in self.ops[e]:
                for d, kind in op.deps.items():
                    if not self._keep(op, d, kind):
                        continue
                    if d.dma:
                        sem, val = d.dsem, d.dval
                    else:
                        sem, val = esem[d.eng], d.sigidx
                    key = id(sem)
                    if waited.get(key, 0) >= val:
                        continue
                    eng.wait_ge(sem, val)
                    waited[key] = val
                ins = op.fn(eng)
                if op.dma:
                    ins.then_inc(op.dsem, 16)
                elif op.sig:
                    ins.then_inc(esem[e], 1)

        with nc.Block() as block:
            @block.tensor
            def _(eng):
                run('pe', eng)

            @block.scalar
            def _(eng):
                run('act', eng)

            @block.vector
            def _(eng):
                run('dve', eng)

            @block.gpsimd
            def _(eng):
                run('pool', eng)

            @block.sync
            def _(eng):
                run('sp', eng)


class Mem:
    def __init__(self, arena, size, base=0):
        self.arena = arena
        self.size = size
        self.top = base

    def alloc(self, free, dtype):
        n = 1
        for s in free:
            n *= s
        esz = 4 if dtype == F32 else 2
        nb = n * esz
        off = self.top
        self.top = (off + nb + 63) // 64 * 64
        assert self.top <= self.size, ("SBUF arena overflow", self.top, self.size)
        ap = self.arena[:, off:off + nb].bitcast(dtype)
        if len(free) == 2:
            ap = ap.rearrange("p (a b) -> p a b", a=free[0])
        elif len(free) == 3:
            ap = ap.rearrange("p (a b c) -> p a b c", a=free[0], b=free[1])
        return ap


def bc_mid(ap2, n):
    P, Fd = ap2.shape
    return ap2.rearrange("p (o f) -> p o f", o=1).to_broadcast([P, n, Fd])


def bc_last(ap2, n):
    P, A = ap2.shape
    return ap2.rearrange("p (a o) -> p a o", o=1).to_broadcast([P, A, n])


def build_nc(debug=(), stage=99):
    nc = bass.Bass("TRN2", target_bir_lowering=False)

    def din(name, shape, dt=F32):
        return nc.dram_tensor(name, list(shape), dt, kind="ExternalInput").ap()

    x_d = din("x", [T, D])
    ctx_d = din("ctx", [CT, D])
    c2_d = din("c2", [2, D])
    wada_d = din("w_ada", [D, 6 * D])
    bada_d = din("b_ada", [1, 6 * D])
    gpm_d = din("g_pre_mix", [1, D])
    gqm_d = din("g_post_mix", [1, D])
    gpf_d = din("g_pre_ffn", [1, D])
    gqf_d = din("g_post_ffn", [1, D])
    win_d = din("w_in", [D, INW])
    convw_d = din("conv_w", [1792, 7])
    convb_d = din("conv_b", [1792, 1])
    dtb_d = din("dt_bias", [24, 1])
    alog_d = din("a_log", [1, 24])
    dsk_d = din("d_skip", [1, 12])
    gssd_d = din("g_ssd", [768, 1])
    wout_d = din("w_out", [D, D])
    wg_d = din("w_gate", [D, DFF])
    wu_d = din("w_up", [D, DFF])
    wd_d = din("w_down", [DFF, D])
    dftc_d = din("dftc", [T, T], BF16)
    dfts_d = din("dfts", [T, T], BF16)
    cs64_d = din("cs64", [256, 512], BF16)
    out_d = nc.dram_tensor("out", [T, D], F32, kind="ExternalOutput").ap()
    dbg_d = {}
    for name, shape, dt_ in debug:
        dbg_d[name] = nc.dram_tensor("dbg_" + name, list(shape), dt_, kind="ExternalOutput").ap()

    with ExitStack() as st:
        ARENA = 212480
        arena = st.enter_context(nc.sbuf_tensor("arena", [128, ARENA], U8))
        M = Mem(arena, ARENA)
        banks = [st.enter_context(nc.psum_tensor("ps%d" % i, [128, 512], F32)) for i in range(8)]
        S = Sched(nc, st)

        def PB(i):
            return banks[i][:, :]

        def PBH(i):
            return banks[i][:, :].bitcast(BF16)

        def dump(name, ap, rkeys):
            if name in dbg_d:
                tgt = dbg_d[name]
                S.add('sp', lambda e: e.dma_start(out=tgt, in_=ap), reads=rkeys, writes=[('dbg', name)], dma=('dbg', name))

        def finalize():
            fin = [('dbg', n) for n in dbg_d] + [('out', i) for i in range(16)]
            S.add('sp', lambda e: e.nop(), reads=fin)
            S.emit()

        ident = M.alloc([128], BF16)
        ident32 = M.alloc([128], F32)
        triF = M.alloc([128], F32)
        triB = M.alloc([128], F32)
        ones32 = M.alloc([128], F32)
        maskF3 = M.alloc([3, 128], BF16)
        maskB3 = M.alloc([3, 128], BF16)
        Sel = M.alloc([24, 128], BF16)
        Dsk = M.alloc([12, 128], BF16)
        epsb = M.alloc([1], F32)
        off_abc = M.top
        a_bc_x = M.alloc([D], F32)
        a_bc_c = M.alloc([D], F32)
        a2_bc = M.alloc([D], F32)
        G1 = M.alloc([D], F32)
        off_g1e = M.top
        G2 = M.alloc([D], F32)
        smcol = M.alloc([8, 2], F32)
        sfcol = M.alloc([8, 2], F32)
        convw_col = M.alloc([14, 7], F32)
        convb_col = M.alloc([14], F32)
        dtb_col = M.alloc([1], F32)
        gssd_col = M.alloc([6], F32)
        nA_bc = M.alloc([24], F32)
        dsk_bc = M.alloc([12], F32)
        rstd0 = M.alloc([NCH], F32)
        ssq0 = M.alloc([NCH], F32)
        base_top = M.top

        pool_c = 'pool'
        S.add(pool_c, lambda e: e.memset(epsb, EPS), writes=['epsb'])
        S.add(pool_c, lambda e: e.memset(ident, 0.0), writes=['ident'])
        S.add(pool_c, lambda e: e.affine_select(out=ident, in_=ident, compare_op=ALU.not_equal, fill=1.0, base=0,
                                                pattern=[[-1, 128]], channel_multiplier=1), reads=['ident'], writes=['ident'])
        S.add(pool_c, lambda e: e.memset(ident32, 0.0), writes=['ident32'])
        S.add(pool_c, lambda e: e.affine_select(out=ident32, in_=ident32, compare_op=ALU.not_equal, fill=1.0, base=0,
                                                pattern=[[-1, 128]], channel_multiplier=1), reads=['ident32'], writes=['ident32'])
        S.add(pool_c, lambda e: e.memset(ones32, 1.0), writes=['ones32'])
        S.add(pool_c, lambda e: e.memset(triF, 1.0), writes=['triF'])
        S.add(pool_c, lambda e: e.affine_select(out=triF, in_=triF, compare_op=ALU.is_ge, fill=0.0, base=0,
                                                pattern=[[1, 128]], channel_multiplier=-1), reads=['triF'], writes=['triF'])
        S.add(pool_c, lambda e: e.memset(triB, 1.0), writes=['triB'])
        S.add(pool_c, lambda e: e.affine_select(out=triB, in_=triB, compare_op=ALU.is_ge, fill=0.0, base=0,
                                                pattern=[[-1, 128]], channel_multiplier=1), reads=['triB'], writes=['triB'])
        S.add(pool_c, lambda e: e.memset(maskF3, 0.0), writes=['maskF3'])
        S.add(pool_c, lambda e: e.affine_select(out=maskF3, in_=maskF3, compare_op=ALU.is_ge, fill=-30000.0, base=0,
                                                pattern=[[0, 3], [1, 128]], channel_multiplier=-1), reads=['maskF3'], writes=['maskF3'])
        S.add(pool_c, lambda e: e.memset(maskB3, 0.0), writes=['maskB3'])
        S.add(pool_c, lambda e: e.affine_select(out=maskB3, in_=maskB3, compare_op=ALU.is_ge, fill=-30000.0, base=0,
                                                pattern=[[0, 3], [-1, 128]], channel_multiplier=1), reads=['maskB3'], writes=['maskB3'])
        S.add(pool_c, lambda e: e.memset(Sel, 0.0), writes=['Sel'])
        S.add(pool_c, lambda e: e.affine_select(out=Sel, in_=Sel, compare_op=ALU.not_equal, fill=1.0, base=0,
                                                pattern=[[-1, 24], [0, 128]], channel_multiplier=1), reads=['Sel'], writes=['Sel'])
        S.add(pool_c, lambda e: e.affine_select(out=Sel, in_=Sel, compare_op=ALU.not_equal, fill=1.0, base=-24,
                                                pattern=[[-1, 24], [0, 128]], channel_multiplier=1), reads=['Sel'], writes=['Sel'])

        S.add('sp', lambda e: e.dma_start(out=convw_col, in_=convw_d.rearrange("(c p) k -> p c k", p=128)),
              writes=['convw_col'], dma='convw_col')
        S.add('sp', lambda e: e.dma_start(out=convb_col.rearrange("p (c o) -> p c o", o=1),
                                          in_=convb_d.rearrange("(c p) o -> p c o", p=128), allow_slow_non_contiguous=True),
              writes=['convb_col'], dma='convb_col')
        S.add('sp', lambda e: e.dma_start(out=dtb_col[0:24, :], in_=dtb_d), writes=['dtb_col'], dma='dtb_col')
        S.add('sp', lambda e: e.dma_start(out=gssd_col.rearrange("p (c o) -> p c o", o=1),
                                          in_=gssd_d.rearrange("(c p) o -> p c o", p=128), allow_slow_non_contiguous=True),
              writes=['gssd_col'], dma='gssd_col')
        S.add('sp', lambda e: e.dma_start(out=nA_bc, in_=alog_d.broadcast_to([128, 24])), writes=['nA_bc'], dma='nA_bc')
        S.add('sp', lambda e: e.dma_start(out=dsk_bc, in_=dsk_d.broadcast_to([128, 12])), writes=['dsk_bc'], dma='dsk_bc')

        if stage == 0.5:
            dump('a_bc_x', G1, ['Dsk', 'Sel', 'maskF3', 'maskB3', 'triF', 'triB', 'convw_col', 'convb_col', 'dtb_col', 'gssd_col', 'nA_bc'])
            finalize()
            return nc
        mark_a = M.top
        M.top = mark_a + (8 * T + 8 * CT) * 2
        c2t = M.alloc([D], F32)
        sc32 = M.alloc([D], F32)
        scT = M.alloc([8, 2], BF16)
        mrow = M.alloc([2 * D], F32)
        bada2 = M.alloc([2 * D], F32)
        wa = [M.alloc([8, 512], BF16) for _ in range(4)]
        xall = M.alloc([NCH, D], F32)
        xnp = [M.alloc([D], BF16) for _ in range(2)]
        sel0 = M.alloc([128], F32)
        sel1 = M.alloc([128], F32)

        S.add('sp', lambda e: e.dma_start(out=c2t[0:2, :], in_=c2_d), writes=['c2t'], dma='c2t')
        S.add('sp', lambda e: e.dma_start(out=bada2[0:2, :], in_=bada_d[0:1, 0:2 * D].broadcast_to([2, 2 * D])), writes=['bada2'], dma='bada2')
        S.add('pool', lambda e: e.memset(sel0[0:2, :], 1.0), writes=['sel0'])
        S.add('pool', lambda e: e.affine_select(out=sel0[0:2, :], in_=sel0[0:2, :], compare_op=ALU.is_equal, fill=0.0, base=0,
                                                pattern=[[0, 128]], channel_multiplier=1), reads=['sel0'], writes=['sel0'])
        S.add('pool', lambda e: e.memset(sel1[0:2, :], 1.0), writes=['sel1'])
        S.add('pool', lambda e: e.affine_select(out=sel1[0:2, :], in_=sel1[0:2, :], compare_op=ALU.not_equal, fill=0.0, base=0,
                                                pattern=[[0, 128]], channel_multiplier=1), reads=['sel1'], writes=['sel1'])
        S.add('act', lambda e: e.activation(out=nA_bc, in_=nA_bc, func=AF.Exp), reads=['nA_bc'], writes=['nA_bc'])
        S.add('dve', lambda e: e.tensor_scalar(out=nA_bc, in0=nA_bc, scalar1=-1.0, scalar2=None, op0=ALU.mult),
              reads=['nA_bc'], writes=['nA_bc'])
        S.add('act', lambda e: e.activation(out=sc32[0:2, :], in_=c2t[0:2, :], func=AF.Silu), reads=['c2t'], writes=['sc32'])
        for j in range(8):
            S.add('pe', lambda e, j=j: e.matmul(PB(7)[:, 2 * j:2 * j + 2], lhsT=sc32[0:2, j * 128:(j + 1) * 128],
                                                rhs=ident32[0:2, 0:2], start=True, stop=True),
                  reads=['sc32', 'ident32'], writes=[('ps', 7)])
        S.add('dve', lambda e: e.tensor_copy(out=scT.rearrange("p a b -> p (a b)"), in_=PB(7)[:, 0:16]),
              reads=[('ps', 7)], writes=['scT'])
        screp = Dsk[:, 0:8, :]
        S.add('dve', lambda e: e.tensor_copy(out=screp, in_=scT[:, :, 0:1].to_broadcast([128, 8, 128])), reads=['scT'], writes=['Dsk'])
        if stage == 0.7:
            dump('a_bc_x', G1, ['scT', 'sel0', 'sel1', 'nA_bc'])
            finalize()
            return nc
        wada_v = wada_d.rearrange("(k p) n -> p k n", p=128)
        for t in range(4):
            sl = t % 4
            S.add('pool', lambda e, t=t, sl=sl: e.dma_start(out=wa[sl], in_=wada_v[:, :, t * 512:(t + 1) * 512]),
                  writes=[('wa', sl)], dma=('wa', sl))
            pb = t % 4
            for k in range(8):
                S.add('pe', lambda e, k=k, sl=sl, pb=pb: e.matmul(PB(pb)[0:2, :], lhsT=scT[:, k, :], rhs=wa[sl][:, k, :],
                                                                  start=(k == 0), stop=(k == 7)),
                      reads=['scT', ('wa', sl)], writes=[('ps', pb)])
            S.add('dve', lambda e, t=t, pb=pb: e.tensor_tensor(out=mrow[0:2, t * 512:(t + 1) * 512], in0=PB(pb)[0:2, :],
                                                               in1=bada2[0:2, t * 512:(t + 1) * 512], op=ALU.add),
                  reads=[('ps', pb), 'bada2'], writes=[('mrow', t)])

        if stage == 0.8:
            dump('a_bc_x', mrow[:, 0:1024], [('mrow', t) for t in range(4)])
            finalize()
            return nc

        def mrow_keys(v):
            return [('mrow', 2 * v), ('mrow', 2 * v + 1)]

        acol = a_bc_x[:, 0:16].rearrange("p (a b) -> p a b", a=8)
        gpmcol = a_bc_c[:, 0:8]
        S.add('sp', lambda e: e.dma_start(out=gpmcol, in_=gpm_d[0:1, :].rearrange("o (j p) -> p (o j)", p=128),
                                          allow_slow_non_contiguous=True), writes=['gpmcol'], dma='gpmcol')
        for v in (0, 1):
            for j in range(8):
                S.add('pe', lambda e, v=v, j=j: e.matmul(PB(7)[:, 2 * j:2 * j + 2],
                                                         lhsT=mrow[0:2, v * D + j * 128:v * D + (j + 1) * 128],
                                                         rhs=ident32[0:2, 0:2], start=True, stop=True),
                      reads=mrow_keys(v) + ['ident32'], writes=[('ps', 7)])
            if v == 0:
                S.add('dve', lambda e: e.tensor_copy(out=smcol.rearrange("p a b -> p (a b)"), in_=PB(7)[:, 0:16]),
                      reads=[('ps', 7)], writes=[('col', 0)])
            else:
                S.add('dve', lambda e: e.scalar_tensor_tensor(
                    out=acol, in0=PB(7)[:, 0:16].rearrange("p (a b) -> p a b", a=8), scalar=1.0, in1=bc_last(gpmcol, 2),
                    op0=ALU.add, op1=ALU.mult), reads=[('ps', 7), 'gpmcol'], writes=['acol'])
        dump('a_bc_x', a_bc_x, ['a_bc_x'])

        if stage == 1:
            finalize()
            return nc
        M.top = mark_a

        off_r1 = M.top
        hxT = M.alloc([8, T], BF16)
        hcT = M.alloc([8, CT], BF16)
        off_xbcX = M.top
        xbcX = M.alloc([6, T], BF16)
        off_uT = M.top
        uT = M.alloc([2, T], BF16)
        off_zcm = M.top
        zcm = M.alloc([6, T], BF16)
        xbcBC = M.alloc([8, T], BF16)
        off_xbcc = M.top
        xbcc = M.alloc([14, CT], BF16)
        dtr = M.alloc([T + CT], F32)
        mark_c = M.top
        off_xt = M.top
        xt = [M.alloc([D], F32) for _ in range(4)]
        off_xs = M.top
        xs = [M.alloc([D], BF16) for _ in range(2)]
        off_xs_end = M.top
        wst = [M.alloc([8, 128], BF16) for _ in range(3)]
        dg = [M.alloc([7, 128], BF16) for _ in range(2)]
        padl = [M.alloc([8, 70], BF16) for _ in range(2)]
        padc = M.alloc([CT + 6], BF16)

        def src_tile(i):
            return ctx_d[i * 128:(i + 1) * 128, :] if i < 2 else x_d[(i - 2) * 128:(i - 1) * 128, :]

        xops = []
        for i in range(NCH):
            xops.append(S.add('sp', lambda e, i=i: e.dma_start(out=xall[:, i, :], in_=src_tile(i)), writes=[('xall', i)],
                              dma=('xall', i // 3)))
            if i % 3 == 2:
                for o in xops[-3:]:
                    o.dval = xops[-1].dval
        for i in range(NCH):
            S.add('act', lambda e, i=i: e.activation(out=xnp[i % 2], in_=xall[:, i, :], func=AF.Square, accum_out=ssq0[:, i:i + 1]),
                  reads=[('xall', i)], writes=[('xnp', i % 2), ('ssq0', i)])
        S.add('act', lambda e: e.activation(out=rstd0, in_=ssq0, func=AF.Ln, scale=1.0 / D, bias=epsb),
              reads=[('ssq0', i) for i in range(NCH)] + ['epsb'], writes=['rstd0'])
        S.add('act', lambda e: e.activation(out=rstd0, in_=rstd0, func=AF.Exp, scale=-0.5), reads=['rstd0'], writes=['rstd0'])
        dump('rstd0', rstd0, ['rstd0'])
        if stage == 1.2:
            finalize()
            return nc
        groups = [(0, 2)] + [(2 + 4 * g, 4) for g in range(4)]
        for gidx, (t0, nt) in enumerate(groups):
            if stage == 1.3 and gidx >= 1:
                break
            for ii in range(nt):
                i = t0 + ii
                sl = i % 2
                S.add('dve', lambda e, i=i, sl=sl: e.tensor_scalar(
                    out=xnp[sl], in0=xall[:, i, :], scalar1=rstd0[:, i:i + 1], scalar2=None, op0=ALU.mult),
                      reads=[('xall', i), 'rstd0'], writes=[('xnp', sl)])
                for j in range(8):
                    pb = 4 + j // 2
                    col = (j % 2) * 512 + ii * 128
                    S.add('pe', lambda e, j=j, sl=sl, pb=pb, col=col: e.transpose(
                        out=PBH(pb)[:, col:col + 128], in_=xnp[sl][:, j * 128:(j + 1) * 128], identity=ident),
                          reads=[('xnp', sl), 'ident'], writes=[('ps', pb)])
            for j in range(8):
                pb = 4 + j // 2
                c0 = (j % 2) * 512
                cix = 1 if gidx == 0 else 0
                if gidx == 0:
                    dst = hcT[:, j, :]
                    dkey = ('hcT', j)
                else:
                    dst = hxT[:, j, (gidx - 1) * 512:gidx * 512]
                    dkey = ('hxT', j, gidx - 1)
                if pb % 2 == 0:
                    S.add('act', lambda e, pb=pb, c0=c0, nt=nt, dst=dst, j=j, cix=cix: e.activation(
                        out=dst, in_=PBH(pb)[:, c0:c0 + nt * 128], func=AF.Identity, scale=acol[:, j, cix:cix + 1],
                        bias=smcol[:, j, cix:cix + 1]), reads=[('ps', pb), ('col', 0), 'acol'], writes=[dkey])
                else:
                    S.add('dve', lambda e, pb=pb, c0=c0, nt=nt, dst=dst, j=j, cix=cix: e.tensor_scalar(
                        out=dst, in0=PBH(pb)[:, c0:c0 + nt * 128], scalar1=acol[:, j, cix:cix + 1], scalar2=smcol[:, j, cix:cix + 1],
                        op0=ALU.mult, op1=ALU.add), reads=[('ps', pb), ('col', 0), 'acol'], writes=[dkey])
        if stage != 1.3:
            import os
            _dj = int(os.environ.get('DJ', '0')); _dg = int(os.environ.get('DG', '0'))
            dump('hxT', hxT[:, _dj, _dg * 512:(_dg + 1) * 512], [('hxT', _dj, _dg)])

        if stage in (1.5, 1.3):
            finalize()
            return nc
        cmax = {1.6: 2, 1.7: 8, 1.8: 22}.get(stage, 23)
        S.barrier()
        for p_ in padl:
            S.add('pool', lambda e, p_=p_: e.memset(p_, 0.0), writes=[('pad', id(p_))])
        S.add('pool', lambda e: e.memset(padc, 0.0), writes=[('padc',)])

        win_v = win_d.rearrange("(k p) n -> p k n", p=128)
        items = []
        for c in range(cmax):
            for gidx, (t0, nt) in enumerate(groups):
                if gidx == 0 and c < 8:
                    continue
                items.append((c, gidx, nt))
        seen_c = set()

        def c_mm(n):
            c, gidx, nt = items[n]
            col0 = 128 * c
            wc = 128 if c < 22 else 24
            sl = c % 3
            dsl = c % 2
            cc = c - 8
            if c not in seen_c:
                seen_c.add(c)
                S.add('pool', lambda e, sl=sl, col0=col0, wc=wc: e.dma_start(out=wst[sl][:, :, 0:wc], in_=win_v[:, :, col0:col0 + wc]),
                      writes=[('wst', sl)], dma=('wst', sl))
                if 8 <= c < 22:
                    for k in range(7):
                        S.add('dve', lambda e, k=k, dsl=dsl, cc=cc: e.tensor_scalar(
                            out=dg[dsl][:, k, :], in0=ident, scalar1=convw_col[:, cc, k:k + 1], scalar2=None, op0=ALU.mult),
                              reads=['ident', 'convw_col'], writes=[('dg', dsl)])
            ntok = nt * 128
            if gidx == 0:
                rhs_of = lambda k: hcT[:, k, :]
                rkeys = [('hcT', k) for k in range(8)]
            else:
                rhs_of = lambda k, g=gidx - 1: hxT[:, k, g * 512:(g + 1) * 512]
                rkeys = [('hxT', k, gidx - 1) for k in range(8)]
            pa = n % 2
            for k in range(8):
                S.add('pe', lambda e, k=k, pa=pa, sl=sl, wc=wc, ntok=ntok, rhs_of=rhs_of: e.matmul(
                    PB(pa)[0:wc, 0:ntok], lhsT=wst[sl][:, k, 0:wc], rhs=rhs_of(k), start=(k == 0), stop=(k == 7)),
                      reads=[('wst', sl)] + rkeys, writes=[('ps', pa)])

        def c_post(n):
            c, gidx, nt = items[n]
            ntok = nt * 128
            pa = n % 2
            conv = 8 <= c < 22
            cc = c - 8
            dsl = c % 2
            tok0 = (gidx - 1) * 512
            if c < 2:
                S.add('dve', lambda e, pa=pa, c=c, tok0=tok0: e.tensor_copy(out=uT[:, c, tok0:tok0 + 512], in_=PB(pa)),
                      reads=[('ps', pa)], writes=[('uT', c, gidx)])
            elif c < 8:
                S.add('act', lambda e, pa=pa, c=c, tok0=tok0: e.activation(out=zcm[:, c - 2, tok0:tok0 + 512], in_=PB(pa),
                                                                            func=AF.Silu),
                      reads=[('ps', pa)], writes=[('zcm', c - 2, gidx)])
            elif conv:
                pbb = 2 + pa
                if gidx == 0:
                    S.add('dve', lambda e, pa=pa: e.tensor_copy(out=padc[:, 3:3 + CT], in_=PB(pa)[:, 0:CT]),
                          reads=[('ps', pa)], writes=[('padc',)])
                    for k in range(7):
                        S.add('pe', lambda e, k=k, pbb=pbb, dsl=dsl: e.matmul(
                            PB(pbb)[:, 0:CT], lhsT=dg[dsl][:, k, :], rhs=padc[:, k:k + CT], start=(k == 0), stop=(k == 6)),
                              reads=[('dg', dsl), ('padc',)], writes=[('ps', pbb)])
                    S.add('act', lambda e, pbb=pbb, cc=cc: e.activation(out=xbcc[:, cc, :], in_=PB(pbb)[:, 0:CT], func=AF.Silu,
                                                                        bias=convb_col[:, cc:cc + 1]),
                          reads=[('ps', pbb), 'convb_col'], writes=[('xbcc', cc)])
                else:
                    pl = padl[n % 2]
                    pkey = ('pad', id(pl))
                    S.add('dve', lambda e, pa=pa, pl=pl: e.tensor_copy(
                        out=pl[:, :, 3:67], in_=PB(pa).rearrange("p (r w) -> p r w", r=8)),
                          reads=[('ps', pa)], writes=[pkey])
                    for k in range(7):
                        S.add('pe', lambda e, k=k, pbb=pbb, dsl=dsl, pl=pl: e.matmul(
                            PB(pbb).rearrange("p (r w) -> p r w", r=8), lhsT=dg[dsl][:, k, :], rhs=pl[:, :, k:k + 64],
                            start=(k == 0), stop=(k == 6)),
                              reads=[('dg', dsl), pkey], writes=[('ps', pbb)])
                    S.add('act', lambda e, pbb=pbb, cc=cc, tok0=tok0: e.activation(
                        out=(xbcX[:, cc, tok0:tok0 + 512] if cc < 6 else xbcBC[:, cc - 6, tok0:tok0 + 512]),
                        in_=PB(pbb), func=AF.Silu, bias=convb_col[:, cc:cc + 1]),
                          reads=[('ps', pbb), 'convb_col'], writes=[('xbc', cc, gidx - 1)])
            else:
                d0 = 0 if gidx == 0 else CT + tok0
                S.add('act', lambda e, pa=pa, d0=d0, ntok=ntok: e.activation(
                    out=dtr[0:24, d0:d0 + ntok], in_=PB(pa)[0:24, 0:ntok], func=AF.Identity, bias=dtb_col[0:24, :]),
                      reads=[('ps', pa), 'dtb_col'], writes=[('dtr', gidx)])

        MW = Mem(arena, off_xs, base=off_xt)
        wa2 = [MW.alloc([8, 512], BF16) for _ in range(2)]
        MB = Mem(arena, off_xs_end, base=off_xs)
        btile = MB.alloc([512], F32)
        gtile = MB.alloc([512], F32)
        ada_dst = {2: (G1, gqm_d, False, 'G1'), 4: (a2_bc, gpf_d, True, 'a2_bc'), 5: (G2, gqf_d, False, 'G2')}

        def ada_dma(j):
            t = 4 + j
            sl = j % 2
            S.add('pool', lambda e, t=t, sl=sl: e.dma_start(out=wa2[sl], in_=wada_v[:, :, t * 512:(t + 1) * 512]),
                  writes=[('xt', 2 * sl), ('xt', 2 * sl + 1), ('wa2', sl)], dma=('wa2', sl))

        def ada_mm(j):
            t = 4 + j
            v, half = t // 2, t % 2
            sl = j % 2
            bk = 4 + j % 2
            if v == 3:
                S.add('sp', lambda e, t=t: e.dma_start(out=btile[:, 0:4], in_=bada_d[0:1, t * 512:(t + 1) * 512].rearrange(
                    "o (j p) -> p (o j)", p=128), allow_slow_non_contiguous=True), writes=[('xs', 0), 'btile'], dma='btile')
                for n4 in range(4):
                    for k in range(8):
                        S.add('pe', lambda e, n4=n4, k=k, sl=sl, bk=bk: e.matmul(
                            PB(bk)[:, n4:n4 + 1], lhsT=wa2[sl][:, k, n4 * 128:(n4 + 1) * 128], rhs=screp[:, k, 0:1],
                            start=(k == 0), stop=(k == 7)), reads=[('wa2', sl)], writes=[('ps', bk)])
                S.add('dve', lambda e, half=half, bk=bk: e.tensor_tensor(out=sfcol[:, half * 4:(half + 1) * 4, 0], in0=PB(bk)[:, 0:4],
                                                                         in1=btile[:, 0:4], op=ALU.add),
                      reads=[('ps', bk), 'btile'], writes=[('sfcol', half)])
                return
            dst, gain_d, plus1, key = ada_dst[v]
            S.add('sp', lambda e, t=t: e.dma_start(out=btile, in_=bada_d[0:1, t * 512:(t + 1) * 512].broadcast_to([128, 512])),
                  writes=[('xs', 0), 'btile'], dma='btile')
            S.add('sp', lambda e, half=half, gain_d=gain_d: e.dma_start(
                out=gtile, in_=gain_d[0:1, half * 512:(half + 1) * 512].broadcast_to([128, 512])),
                  writes=[('xs', 1), 'gtile'], dma='gtile')
            for k in range(8):
                S.add('pe', lambda e, k=k, sl=sl, bk=bk: e.matmul(PB(bk), lhsT=screp[:, k, :], rhs=wa2[sl][:, k, :],
                                                                  start=(k == 0), stop=(k == 7)),
                      reads=[('wa2', sl)], writes=[('ps', bk)])
            dh = dst[:, half * 512:(half + 1) * 512]
            if plus1:
                S.add('dve', lambda e, dh=dh, bk=bk: e.scalar_tensor_tensor(out=dh, in0=PB(bk), scalar=1.0, in1=btile,
                                                                           op0=ALU.add, op1=ALU.add),
                      reads=[('ps', bk), 'btile'], writes=[(key, half)])
            else:
                S.add('dve', lambda e, dh=dh, bk=bk: e.tensor_tensor(out=dh, in0=PB(bk), in1=btile, op=ALU.add),
                      reads=[('ps', bk), 'btile'], writes=[(key, half)])
            S.add('dve', lambda e, dh=dh: e.tensor_tensor(out=dh, in0=dh, in1=gtile, op=ALU.mult),
                  reads=[(key, half), 'gtile'], writes=[(key, half)])

        ada_at = {}
        if cmax == 23:
            for j in range(8):
                ada_at[2 + 12 * j] = ('dma', j)
                ada_at[2 + 12 * j + 8] = ('mm', j)
        if items:
            c_mm(0)
        for n in range(len(items)):
            if n + 1 < len(items):
                c_mm(n + 1)
            c_post(n)
            if n in ada_at:
                kind, j = ada_at[n]
                (ada_dma if kind == 'dma' else ada_mm)(j)
        dump('G1', G1, [('G1', 0), ('G1', 1)])
        dump('uT', uT[:, 0, 0:512], [('uT', 0, 1)])
        dump('zcm', zcm[:, 0, 0:512], [('zcm', 0, 1)])
        dump('xbc', xbcX[:, 0, 0:512], [('xbc', 0, 0)])
        dump('xbcc', xbcc[:, 13, :], [('xbcc', 13)])
        dump('dtr', dtr[0:24, 0:512], [('dtr', 0), ('dtr', 1)])

        if stage == 1.9:
            finalize()
            return nc

        def evac(bank, out_ap, in_ap, rkeys, wkeys, bias=None):
            if bank % 2 == 0:
                if bias is None:
                    S.add('act', lambda e: e.activation(out=out_ap, in_=in_ap, func=AF.Identity), reads=rkeys, writes=wkeys)
                else:
                    S.add('act', lambda e: e.activation(out=out_ap, in_=in_ap, func=AF.Identity, bias=bias), reads=rkeys, writes=wkeys)
            else:
                if bias is None:
                    S.add('dve', lambda e: e.tensor_copy(out=out_ap, in_=in_ap), reads=rkeys, writes=wkeys)
                else:
                    S.add('dve', lambda e: e.tensor_scalar(out=out_ap, in0=in_ap, scalar1=bias, scalar2=None, op0=ALU.add),
                          reads=rkeys, writes=wkeys)

        S.barrier()
        ME = Mem(arena, ARENA, base=mark_c)
        A_tok = ME.alloc([16, 512], BF16)
        cs64t = ME.alloc([2, 512], BF16)
        dft = [ME.alloc([2, 1024], BF16) for _ in range(3)]
        MR1 = Mem(arena, off_xbcX, base=off_r1)
        YfT = MR1.alloc([2, T], BF16)
        x_tok = MR1.alloc([NCH, 768], BF16)
        S.add('sp', lambda e: e.dma_start(out=cs64t, in_=cs64_d.rearrange("(k p) n -> p k n", p=128)), writes=['cs64t'], dma='cs64t')
        for i in range(16):
            pb = i % 2
            for kc in range(2):
                S.add('pe', lambda e, i=i, kc=kc, pb=pb: e.matmul(PB(pb), lhsT=uT[:, kc, i * 128:(i + 1) * 128], rhs=cs64t[:, kc, :],
                                                                  start=(kc == 0), stop=(kc == 1)),
                      reads=['cs64t'], writes=[('ps', pb)])
            evac(pb, A_tok[:, i, :], PB(pb), [('ps', pb)], [('A_tok', i)])
        for kh in range(2):
            for i in range(16):
                sl = (kh * 16 + i) % 3
                S.add('sp', lambda e, i=i, sl=sl, kh=kh: e.dma_start(out=dft[sl][:, 0, :],
                                                                   in_=dftc_d[i * 128:(i + 1) * 128, kh * 1024:(kh + 1) * 1024]),
                      writes=[('dftc', sl)], dma=('dftc', sl))
                S.add('sp', lambda e, i=i, sl=sl, kh=kh: e.dma_start(out=dft[sl][:, 1, :],
                                                                   in_=dfts_d[i * 128:(i + 1) * 128, kh * 1024:(kh + 1) * 1024]),
                      writes=[('dfts', sl)], dma=('dfts', sl))
                for jc in range(2):
                    for kt in range(2):
                        bank = 4 + jc * 2 + kt
                        S.add('pe', lambda e, i=i, sl=sl, jc=jc, kt=kt, bank=bank: e.matmul(
                            PB(bank), lhsT=A_tok[:, i, jc * 128:(jc + 1) * 128], rhs=dft[sl][:, 0, kt * 512:(kt + 1) * 512],
                            start=(i == 0), stop=False), reads=[('A_tok', i), ('dftc', sl)], writes=[('ps', bank)])
                        S.add('pe', lambda e, i=i, sl=sl, jc=jc, kt=kt, bank=bank: e.matmul(
                            PB(bank), lhsT=A_tok[:, i, 256 + jc * 128:256 + (jc + 1) * 128], rhs=dft[sl][:, 1, kt * 512:(kt + 1) * 512],
                            start=False, stop=(i == 15)), reads=[('A_tok', i), ('dfts', sl)], writes=[('ps', bank)])
            for jc in range(2):
                for kt in range(2):
                    bank = 4 + jc * 2 + kt
                    c0 = kh * 1024 + kt * 512
                    evac(bank, YfT[:, jc, c0:c0 + 512], PB(bank), [('ps', bank)], [('YfT', jc, kh, kt)])
        dump('YfT', YfT[:, 0, 0:512], [('YfT', 0, 0, 0)])
        if stage == 2:
            finalize()
            return nc

        S.barrier()
        MD = Mem(arena, ARENA, base=mark_c)
        ea = MD.alloc([NCH, 24], F32)
        wstt = MD.alloc([NCH, 24], F32)
        eaend = MD.alloc([NCH, 24], F32)
        acsT_hl = MD.alloc([NCH * 128], BF16)
        uT_hl = MD.alloc([NCH * 128], BF16)
        mark_f = MD.top
        dtl = MD.alloc([NCH, 24], F32)
        dA = MD.alloc([NCH, 24], F32)
        lndt = MD.alloc([NCH, 24], F32)
        acs = MD.alloc([NCH, 24], F32)
        uu = MD.alloc([NCH, 24], F32)
        tmpw = MD.alloc([NCH, 24], F32)
        hlA = MD.alloc([NCH, 48], BF16)
        hlU = MD.alloc([NCH, 48], BF16)
        fl = lambda t: t.rearrange("p a b -> p (a b)")
        NW = NCH * 24
        for cc in range(NCH):
            S.add('pe', lambda e, cc=cc: e.matmul(PB(0)[:, cc * 24:(cc + 1) * 24], lhsT=dtr[0:24, cc * 128:(cc + 1) * 128],
                                                  rhs=ident32[0:24, 0:24], start=True, stop=True), reads=[], writes=[('ps', 0)])
        S.add('act', lambda e: e.activation(out=fl(dtl), in_=PB(0)[:, 0:NW], func=AF.Identity), reads=[('ps', 0)], writes=['dtl'])
        S.add('act', lambda e: e.activation(out=fl(tmpw), in_=fl(dtl), func=AF.Exp), reads=['dtl'], writes=['tmpw'])
        S.add('act', lambda e: e.activation(out=fl(dtl), in_=fl(tmpw), func=AF.Ln, bias=1.0), reads=['tmpw'], writes=['dtl'])
        S.add('act', lambda e: e.activation(out=fl(lndt), in_=fl(dtl), func=AF.Ln), reads=['dtl'], writes=['lndt'])
        S.add('dve', lambda e: e.tensor_tensor(out=dA, in0=dtl, in1=bc_mid(nA_bc, NCH), op=ALU.mult), reads=['dtl'], writes=['dA'])
        for cc in range(NCH):
            S.add('pe', lambda e, cc=cc: e.matmul(PB(1)[:, cc * 24:cc * 24 + 12], lhsT=triF, rhs=dA[:, cc, 0:12], start=True, stop=True),
                  reads=['dA'], writes=[('ps', 1)])
            S.add('pe', lambda e, cc=cc: e.matmul(PB(1)[:, cc * 24 + 12:cc * 24 + 24], lhsT=triB, rhs=dA[:, cc, 12:24], start=True, stop=True),
                  reads=['dA'], writes=[('ps', 1)])
            S.add('pe', lambda e, cc=cc: e.matmul(PB(2)[:, cc * 24:(cc + 1) * 24], lhsT=ones32, rhs=dA[:, cc, :], start=True, stop=True),
                  reads=['dA'], writes=[('ps', 2)])
        S.add('dve', lambda e: e.tensor_copy(out=fl(acs), in_=PB(1)[:, 0:NW]), reads=[('ps', 1)], writes=['acs'])
        S.add('act', lambda e: e.activation(out=fl(ea), in_=fl(acs), func=AF.Exp), reads=['acs'], writes=['ea'])
        S.add('act', lambda e: e.activation(out=fl(eaend), in_=PB(2)[:, 0:NW], func=AF.Exp), reads=[('ps', 2)], writes=['eaend'])
        S.add('dve', lambda e: e.tensor_tensor(out=fl(uu), in0=fl(lndt), in1=fl(acs), op=ALU.subtract), reads=['lndt', 'acs'], writes=['uu'])
        S.add('dve', lambda e: e.tensor_tensor(out=fl(tmpw), in0=PB(2)[:, 0:NW], in1=fl(uu), op=ALU.add), reads=[('ps', 2), 'uu'], writes=['tmpw'])
        S.add('act', lambda e: e.activation(out=fl(wstt), in_=fl(tmpw), func=AF.Exp), reads=['tmpw'], writes=['wstt'])
        for src, hl, key in ((acs, hlA, 'hlA'), (uu, hlU, 'hlU')):
            S.add('dve', lambda e, src=src, hl=hl: e.tensor_copy(out=hl[:, :, 0:24], in_=src), reads=['acs', 'uu'], writes=[key])
            S.add('dve', lambda e, src=src, hl=hl: e.tensor_tensor(out=hl[:, :, 24:48], in0=src, in1=hl[:, :, 0:24], op=ALU.subtract),
                  reads=['acs', 'uu', key], writes=[key])
        for hl, dstT, key in ((hlA, acsT_hl, 'hlA'), (hlU, uT_hl, 'hlU')):
            for cc in range(NCH):
                bank = 3 + cc // 8
                col = (cc % 8) * 128
                S.add('pe', lambda e, hl=hl, cc=cc, bank=bank, col=col: e.transpose(out=PBH(bank)[0:48, col:col + 128], in_=hl[:, cc, :],
                                                                                   identity=ident), reads=[key], writes=[('ps', bank)])
            for bank in (3, 4, 5):
                c0 = (bank - 3) * 8
                n = min(8, NCH - c0)
                evac(bank, dstT[0:48, c0 * 128:(c0 + n) * 128], PBH(bank)[0:48, 0:n * 128], [('ps', bank)], [key + 'T'])
        for cc in range(NCH):
            bank = 6 + cc % 2
            for j in range(6):
                srcx = xbcc[:, j, cc * 128:(cc + 1) * 128] if cc < 2 else xbcX[:, j, (cc - 2) * 128:(cc - 1) * 128]
                S.add('pe', lambda e, j=j, srcx=srcx, bank=bank: e.transpose(out=PBH(bank)[:, j * 128:(j + 1) * 128], in_=srcx, identity=ident),
                      reads=[], writes=[('ps', bank)])
            evac(bank, x_tok[:, cc, :], PBH(bank)[:, 0:768], [('ps', bank)], [('x_tok', cc)])
        dump('ea', fl(ea), ['ea'])
        dump('wstt', fl(wstt), ['wstt'])
        dump('eaend', fl(eaend), ['eaend'])
        dump('acsT', acsT_hl[0:48, 0:512], ['hlAT'])
        dump('x_tok', x_tok[:, 2, :], [('x_tok', 2)])
        if stage == 3:
            finalize()
            return nc

        S.barrier()
        MF3 = Mem(arena, ARENA, base=mark_f)
        MF4 = Mem(arena, off_zcm, base=off_uT)
        MF5 = Mem(arena, mark_c, base=off_xbcc)
        MX = Mem(arena, off_uT, base=off_xbcX)
        hinB = MX.alloc([16, 768], BF16)
        t1 = MF3.alloc([768], F32)
        t2 = MF3.alloc([768], F32)
        yv = MF3.alloc([768], F32)
        hF = MF3.alloc([768], F32)
        hB = MF3.alloc([768], F32)
        tmpS = MF3.alloc([768], F32)
        E_s = [[MF5.alloc([3, 128], BF16) for _ in range(2)] for _ in range(2)]
        Es = [MF5.alloc([3, 128], BF16) for _ in range(2)]
        Wt = [MF5.alloc([3, 128], BF16) for _ in range(2)]
        xw = [MF5.alloc([768], BF16) for _ in range(2)]
        Btk = [MF5.alloc([512], BF16) for _ in range(2)]
        yn = MF5.alloc([768], BF16)
        junky = MF5.alloc([768], BF16)
        hF_bf = MF5.alloc([768], BF16)
        ssqy = MF4.alloc([16], F32)
        rstdy = MF4.alloc([16], F32)
        TR, SC, EF, EB, Y0, Y1, ST0 = 0, 1, 2, 3, 4, 5, 6
        hv = lambda t: t.rearrange("p (h d) -> p h d", h=12)
        for h in range(12):
            S.add('dve', lambda e, h=h: e.tensor_scalar(out=Dsk[:, h, :], in0=ident, scalar1=dsk_bc[:, h:h + 1], scalar2=None,
                                                        op0=ALU.mult), reads=[], writes=['Dsk'])
        S.add('pool', lambda e: e.memset(hF, 0.0), writes=['hF'])
        S.add('pool', lambda e: e.memset(hB, 0.0), writes=['hB'])
        cnt = [0]

        def st_a(cc, d):
            slot = cnt[0] % 2
            cnt[0] += 1
            for g in range(4):
                srcb = xbcc[:, 6 + g, cc * 128:(cc + 1) * 128] if cc < 2 else xbcBC[:, g, (cc - 2) * 128:(cc - 1) * 128]
                S.add('pe', lambda e, g=g, srcb=srcb: e.transpose(out=PBH(TR)[:, g * 128:(g + 1) * 128], in_=srcb, identity=ident),
                      reads=[], writes=[('ps', TR)])
            evac(TR, Btk[slot], PBH(TR)[:, 0:512], [('ps', TR)], [('Btk', slot)])
            S.add('dve', lambda e, slot=slot, cc=cc, d=d: e.tensor_tensor(
                out=hv(xw[slot]), in0=hv(x_tok[:, cc, :]), in1=bc_last(wstt[:, cc, 12 * d:12 * d + 12], 64), op=ALU.mult),
                  reads=[], writes=[('xw', slot)])
            return slot

        def st_mm(slot, b0):
            for g in range(4):
                bank = b0 + g // 2
                col = (g % 2) * 192
                S.add('pe', lambda e, g=g, bank=bank, col=col, slot=slot: e.matmul(
                    PB(bank)[:, col:col + 192], lhsT=Btk[slot][:, g * 128:(g + 1) * 128], rhs=xw[slot][:, g * 192:(g + 1) * 192],
                    start=True, stop=True), reads=[('Btk', slot), ('xw', slot)], writes=[('ps', bank)])

        def st_rec(cc, d, hst, hkey, b0, meng):
            S.add(meng, lambda e, cc=cc, d=d, hst=hst: e.tensor_tensor(
                out=hv(tmpS), in0=hv(hst), in1=bc_last(eaend[:, cc, 12 * d:12 * d + 12], 64), op=ALU.mult),
                  reads=[hkey], writes=['tmpS'])
            for half in range(2):
                S.add('dve', lambda e, half=half, hst=hst: e.tensor_tensor(
                    out=hst[:, half * 384:(half + 1) * 384], in0=PB(b0 + half)[:, 0:384], in1=tmpS[:, half * 384:(half + 1) * 384],
                    op=ALU.add), reads=[('ps', b0 + half), 'tmpS'], writes=[hkey])

        def st_b(cc, d, hst, hkey, slot, meng='pool'):
            st_mm(slot, ST0)
            st_rec(cc, d, hst, hkey, ST0, meng)

        def st_step(cc, d, hst, hkey, meng='pool'):
            st_b(cc, d, hst, hkey, st_a(cc, d), meng)

        order_b = [1, 0] + list(range(17, 1, -1))
        slot_n = st_a(order_b[0], 1)
        for n, cc in enumerate(order_b):
            slot_c = slot_n
            b0 = ST0 if n % 2 == 0 else Y0
            st_mm(slot_c, b0)
            if n + 1 < len(order_b):
                slot_n = st_a(order_b[n + 1], 1)
            if cc >= 2:
                S.add('act', lambda e, cc=cc: e.activation(out=hinB[:, cc - 2, :], in_=hB, func=AF.Identity),
                      reads=['hB'], writes=[('hinB', cc - 2)])
            st_rec(cc, 1, hB, 'hB', b0, 'dve')
        dump('hinB0', hinB[:, 0, :], [('hinB', 0)])
        if stage == 3.5:
            finalize()
            return nc

        def E_mm(cc, g):
            for d in range(2):
                bank = EF if d == 0 else EB
                mk = maskF3 if d == 0 else maskB3
                S.add('pe', lambda e, bank=bank, mk=mk: e.matmul(PB(bank)[:, 0:384], lhsT=ident, rhs=mk.rearrange("p a b -> p (a b)"),
                                                                 start=True, stop=False), reads=[], writes=[('ps', bank)])
                for hh in range(3):
                    hd = 12 * d + 3 * g + hh
                    S.add('pe', lambda e, bank=bank, hh=hh, hd=hd, cc=cc: e.matmul(
                        PB(bank)[:, hh * 128:(hh + 1) * 128], lhsT=Sel[0:48, hd, :], rhs=acsT_hl[0:48, cc * 128:(cc + 1) * 128],
                        start=False, stop=False), reads=[], writes=[('ps', bank)])
                    S.add('pe', lambda e, bank=bank, hh=hh, hd=hd, cc=cc: e.matmul(
                        PB(bank)[:, hh * 128:(hh + 1) * 128], lhsT=uT_hl[0:48, cc * 128:(cc + 1) * 128], rhs=Sel[0:48, hd, :],
                        start=False, stop=(hh == 2)), reads=[], writes=[('ps', bank)])

        def E_exp(g):
            esl = g % 2
            for d in range(2):
                bank = EF if d == 0 else EB
                S.add('act', lambda e, bank=bank, d=d, esl=esl: e.activation(
                    out=E_s[d][esl].rearrange("p a b -> p (a b)"), in_=PB(bank)[:, 0:384], func=AF.Exp),
                      reads=[('ps', bank)], writes=[('E_s', d, esl)])

        Wd = [Wt, Es]

        def W_y(cc, g):
            esl = g % 2
            for d in range(2):
                S.add('dve', lambda e, esl=esl, g=g, d=d: e.tensor_tensor(
                    out=Wd[d][esl], in0=E_s[d][esl], in1=bc_mid(PB(SC)[:, g * 128:(g + 1) * 128], 3), op=ALU.mult),
                      reads=[('E_s', d, esl), ('ps', SC)], writes=[('Wd', d, esl)])
            for hh in range(3):
                h = 3 * g + hh
                yb, yc = (Y0, h * 64) if h < 8 else (Y1, (h - 8) * 64)
                for d in range(2):
                    S.add('pe', lambda e, esl=esl, hh=hh, h=h, yb=yb, yc=yc, cc=cc, d=d: e.matmul(
                        PB(yb)[:, yc:yc + 64], lhsT=Wd[d][esl][:, hh, :], rhs=x_tok[:, cc, h * 64:(h + 1) * 64],
                        start=(d == 0), stop=False), reads=[('Wd', d, esl)], writes=[('ps', yb)])
                S.add('pe', lambda e, h=h, yb=yb, yc=yc, cc=cc: e.matmul(
                    PB(yb)[:, yc:yc + 64], lhsT=Dsk[:, h, :], rhs=x_tok[:, cc, h * 64:(h + 1) * 64], start=False, stop=True),
                      reads=['Dsk'], writes=[('ps', yb)])

        yvs = [yv, MF4.alloc([768], F32)]

        def head(cc):
            l = cc - 2
            t0 = l * 128
            S.add('act', lambda e: e.activation(out=hF_bf, in_=hF, func=AF.Identity), reads=['hF'], writes=['hF_bf'])

            def yoff(d, tdst, hsrc, hk):
                for g in range(4):
                    bank = ST0 + g // 2
                    col = (g % 2) * 192
                    S.add('pe', lambda e, g=g, bank=bank, col=col, hsrc=hsrc, t0=t0: e.matmul(
                        PB(bank)[:, col:col + 192], lhsT=xbcBC[:, 4 + g, t0:t0 + 128], rhs=hsrc[:, g * 192:(g + 1) * 192],
                        start=True, stop=True), reads=[hk], writes=[('ps', bank)])
                for half in range(2):
                    S.add('dve', lambda e, half=half, tdst=tdst, d=d, cc=cc: e.tensor_tensor(
                        out=tdst[:, half * 384:(half + 1) * 384].rearrange("p (h d) -> p h d", h=6),
                        in0=PB(ST0 + half)[:, 0:384].rearrange("p (h d) -> p h d", h=6),
                        in1=bc_last(ea[:, cc, 12 * d + 6 * half:12 * d + 6 * half + 6], 64), op=ALU.mult),
                          reads=[('ps', ST0 + half)], writes=[('t', d)])

            yoff(1, t2, hinB[:, l, :], ('hinB', l))
            for g in range(4):
                S.add('pe', lambda e, g=g, t0=t0: e.matmul(PB(SC)[:, g * 128:(g + 1) * 128], lhsT=xbcBC[:, g, t0:t0 + 128],
                                                           rhs=xbcBC[:, 4 + g, t0:t0 + 128], start=True, stop=True),
                      reads=[], writes=[('ps', SC)])
            E_mm(cc, 0)
            E_exp(0)
            yoff(0, t1, hF_bf, 'hF_bf')
            E_mm(cc, 1)
            E_exp(1)
            S.add('pool', lambda e: e.tensor_tensor(out=t1, in0=t1, in1=t2, op=ALU.add), reads=[('t', 0), ('t', 1)], writes=[('t', 0)])

        def body(cc, mid_hook=None):
            yvc = yvs[cc % 2]
            ykey = ('yv', cc % 2)
            W_y(cc, 0)
            E_mm(cc, 2)
            E_exp(2)
            slot_f = st_a(cc, 0)
            W_y(cc, 1)
            if mid_hook is not None:
                mid_hook()
            E_mm(cc, 3)
            E_exp(3)
            W_y(cc, 2)
            st_b(cc, 0, hF, 'hF', slot_f)
            W_y(cc, 3)
            S.add('dve', lambda e: e.tensor_tensor(out=yvc[:, 0:512], in0=PB(Y0), in1=t1[:, 0:512], op=ALU.add),
                  reads=[('ps', Y0), ('t', 0)], writes=[ykey])
            S.add('dve', lambda e: e.tensor_tensor(out=yvc[:, 512:768], in0=PB(Y1)[:, 0:256], in1=t1[:, 512:768], op=ALU.add),
                  reads=[('ps', Y1), ('t', 0)], writes=[ykey])

        def tail2a(cc):
            l = cc - 2
            t0 = l * 128
            yvc = yvs[cc % 2]
            ykey = ('yv', cc % 2)
            for j in range(6):
                S.add('pe', lambda e, j=j, t0=t0: e.transpose(out=PBH(TR)[:, j * 128:(j + 1) * 128], in_=zcm[:, j, t0:t0 + 128], identity=ident),
                      reads=[], writes=[('ps', TR)])
            S.add('dve', lambda e: e.tensor_tensor(out=yvc, in0=yvc, in1=PBH(TR)[:, 0:768], op=ALU.mult), reads=[ykey, ('ps', TR)], writes=[ykey])
            S.add('act', lambda e, l=l: e.activation(out=junky, in_=yvc, func=AF.Square, accum_out=ssqy[:, l:l + 1]),
                  reads=[ykey], writes=['junky', ('ssqy', l)])
            S.add('act', lambda e, l=l: e.activation(out=rstdy[:, l:l + 1], in_=ssqy[:, l:l + 1], func=AF.Ln, scale=1.0 / 768, bias=epsb),
                  reads=[('ssqy', l)], writes=[('rstdy', l)])
            S.add('act', lambda e, l=l: e.activation(out=rstdy[:, l:l + 1], in_=rstdy[:, l:l + 1], func=AF.Exp, scale=-0.5),
                  reads=[('rstdy', l)], writes=[('rstdy', l)])
            S.add('act', lambda e, l=l: e.activation(out=yn, in_=yvc, func=AF.Copy, scale=rstdy[:, l:l + 1]),
                  reads=[ykey, ('rstdy', l)], writes=['yn'])

        def tail2b(cc):
            l = cc - 2
            for j in range(6):
                S.add('pe', lambda e, j=j: e.transpose(out=PBH(TR)[:, j * 128:(j + 1) * 128], in_=yn[:, j * 128:(j + 1) * 128], identity=ident),
                      reads=['yn'], writes=[('ps', TR)])
            evac(TR, hinB[:, l, :], PBH(TR)[:, 0:768], [('ps', TR)], [('hinB', l)])

        for cc in (0, 1):
            st_step(cc, 0, hF, 'hF')
        head(2)
        body(2)
        for cc in range(3, NCH):
            head(cc)
            tail2a(cc - 1)
            body(cc, mid_hook=(lambda c=cc - 1: tail2b(c)))
        tail2a(NCH - 1)
        tail2b(NCH - 1)
        dump('ynT0', hinB[:, 0, :], [('hinB', 0)])
        dump('ynT15', hinB[:, 15, :], [('hinB', 15)])
        if stage == 4:
            finalize()
            return nc

        S.barrier()
        MG1 = Mem(arena, off_xbcX, base=off_r1 + 2 * T * 2)
        wout = MG1.alloc([8, D], BF16)
        tt = MG1.alloc([D], F32)
        xs2 = [MG1.alloc([D], BF16) for _ in range(2)]
        junkg = MG1.alloc([D], BF16)
        MG = Mem(arena, ARENA, base=off_uT)
        xres = MG.alloc([16, D], F32)
        h2T = MG.alloc([8, T], BF16)
        xt2 = [MG.alloc([D], F32) for _ in range(2)]
        ssa = MG.alloc([16, 4], F32)
        ssb = MG.alloc([16, 2], F32)
        ssc = MG.alloc([16, 4], F32)
        S.add('pool', lambda e: e.dma_start(out=wout, in_=wout_d.rearrange("(k p) n -> p k n", p=128)), writes=['wout'], dma='wout')
        for j in range(6):
            S.add('dve', lambda e, j=j: e.tensor_scalar(out=wout[:, 2 + j, :], in0=wout[:, 2 + j, :], scalar1=gssd_col[:, j:j + 1], scalar2=None,
                                                        op0=ALU.mult), reads=['wout'], writes=['wout'])
        def g_mm(i):
            sl = i % 2
            S.add('sp', lambda e, i=i, sl=sl: e.dma_start(out=xt2[sl], in_=x_d[i * 128:(i + 1) * 128, :]), writes=[('xt2', sl)], dma=('xt2', sl))
            ob = (0, 1) if i % 2 == 0 else (2, 3)
            for nh in range(2):
                for k in range(8):
                    lt = YfT[:, k, i * 128:(i + 1) * 128] if k < 2 else hinB[:, i, (k - 2) * 128:(k - 1) * 128]
                    S.add('pe', lambda e, nh=nh, k=k, lt=lt, ob=ob: e.matmul(PB(ob[nh]), lhsT=lt, rhs=wout[:, k, nh * 512:(nh + 1) * 512],
                                                                            start=(k == 0), stop=(k == 7)),
                          reads=['wout'], writes=[('ps', ob[nh])])

        def g_s1a(i):
            ob = (0, 1) if i % 2 == 0 else (2, 3)
            for nh in range(2):
                S.add('act', lambda e, nh=nh, ob=ob, i=i: e.activation(out=junkg[:, 0:512], in_=PB(ob[nh]), func=AF.Square,
                                                                       accum_out=ssa[:, i, nh:nh + 1]),
                      reads=[('ps', ob[nh])], writes=['junkg', ('ssa', i, nh)])
            for nh in range(2):
                S.add('dve', lambda e, nh=nh, ob=ob: e.tensor_tensor(out=tt[:, nh * 512:(nh + 1) * 512], in0=PB(ob[nh]),
                                                                     in1=G1[:, nh * 512:(nh + 1) * 512], op=ALU.mult),
                      reads=[('ps', ob[nh])], writes=['tt'])
            S.add('dve', lambda e, i=i: e.tensor_tensor(out=ssa[:, i, 2:3], in0=ssa[:, i, 0:1], in1=ssa[:, i, 1:2], op=ALU.add),
                  reads=[('ssa', i, 0), ('ssa', i, 1)], writes=[('ssa', i, 2)])

        def g_s1b(i):
            sl = i % 2
            S.add('act', lambda e, i=i: e.activation(out=ssa[:, i, 3:4], in_=ssa[:, i, 2:3], func=AF.Ln, scale=1.0 / D, bias=epsb),
                  reads=[('ssa', i, 2)], writes=[('ssa', i, 3)])
            S.add('act', lambda e, i=i: e.activation(out=ssa[:, i, 3:4], in_=ssa[:, i, 3:4], func=AF.Exp, scale=-0.5),
                  reads=[('ssa', i, 3)], writes=[('ssa', i, 3)])
            S.add('dve', lambda e, i=i, sl=sl: e.scalar_tensor_tensor(out=xres[:, i, :], in0=tt, scalar=ssa[:, i, 3:4], in1=xt2[sl],
                                                                      op0=ALU.mult, op1=ALU.add),
                  reads=['tt', ('ssa', i, 3), ('xt2', sl)], writes=[('xres', i)])

        def g_s2a(i):
            S.add('act', lambda e, i=i: e.activation(out=junkg2, in_=xres[:, i, :], func=AF.Square, accum_out=ssb[:, i, 0:1]),
                  reads=[('xres', i)], writes=['junkg2', ('ssb', i, 0)])
            S.add('act', lambda e, i=i: e.activation(out=ssb[:, i, 1:2], in_=ssb[:, i, 0:1], func=AF.Ln, scale=1.0 / D, bias=epsb),
                  reads=[('ssb', i, 0)], writes=[('ssb', i, 1)])
            S.add('act', lambda e, i=i: e.activation(out=ssb[:, i, 1:2], in_=ssb[:, i, 1:2], func=AF.Exp, scale=-0.5),
                  reads=[('ssb', i, 1)], writes=[('ssb', i, 1)])

        def g_s2b(i):
            sl = i % 2
            ii = i % 4
            S.add('dve', lambda e, i=i, sl=sl: e.scalar_tensor_tensor(out=xs2[sl], in0=xres[:, i, :], scalar=ssb[:, i, 1:2], in1=a2_bc,
                                                                      op0=ALU.mult, op1=ALU.mult),
                  reads=[('xres', i), ('ssb', i, 1)], writes=[('xs2', sl)])
            for j in range(8):
                pb = 4 + j // 2
                col = (j % 2) * 512 + ii * 128
                S.add('pe', lambda e, j=j, sl=sl, pb=pb, col=col: e.transpose(out=PBH(pb)[:, col:col + 128],
                                                                             in_=xs2[sl][:, j * 128:(j + 1) * 128], identity=ident),
                      reads=[('xs2', sl)], writes=[('ps', pb)])

        junkg2 = MG.alloc([D], BF16)
        g_mm(0)
        g_mm(1)
        g_s1a(0)
        g_s1b(0)
        for i in range(16):
            if i + 2 < 16:
                g_mm(i + 2)
            if i + 1 < 16:
                g_s1a(i + 1)
            g_s2a(i)
            if i + 1 < 16:
                g_s1b(i + 1)
            g_s2b(i)
            if i % 4 == 3:
                g4 = i // 4
                for j in range(8):
                    pb = 4 + j // 2
                    c0 = (j % 2) * 512
                    evac(pb, h2T[:, j, g4 * 512:(g4 + 1) * 512], PBH(pb)[:, c0:c0 + 512], [('ps', pb)], [('h2T', j, g4)],
                         bias=sfcol[:, j, 0:1])
        dump('xres0', xres[:, 0, :], [('xres', 0)])
        dump('h2T', h2T[:, 0, 0:512], [('h2T', 0, 0)])
        if stage == 5:
            finalize()
            return nc

        S.barrier()
        MH = Mem(arena, off_uT, base=off_r1)
        hT = MH.alloc([NFC, 1024], BF16)
        wgs = [MH.alloc([8, 256], BF16) for _ in range(2)]
        wus = [MH.alloc([8, 256], BF16) for _ in range(2)]
        MHc = Mem(arena, off_g1e, base=off_abc)
        wds = [MHc.alloc([2, 512], BF16) for _ in range(4)]
        sg = [MHc.alloc([512], BF16) for _ in range(2)]
        junkh = MHc.alloc([512], BF16)
        outs = [xt2[0], xt2[1], MG.alloc([D], F32)]
        wg_v = wg_d.rearrange("(k p) n -> p k n", p=128)
        wu_v = wu_d.rearrange("(k p) n -> p k n", p=128)
        for hh in range(2):
            for fp in range(NFC // 2):
                sl = fp % 2
                S.add('pool', lambda e, fp=fp, sl=sl: e.dma_start(out=wgs[sl], in_=wg_v[:, :, fp * 256:(fp + 1) * 256]),
                      writes=[('wgs', sl)], dma=('wgs', sl))
                S.add('pool', lambda e, fp=fp, sl=sl: e.dma_start(out=wus[sl], in_=wu_v[:, :, fp * 256:(fp + 1) * 256]),
                      writes=[('wus', sl)], dma=('wus', sl))
                for fi in range(2):
                    fc = fp * 2 + fi
                    for tg in range(2):
                        tok0 = hh * 1024 + tg * 512
                        gb, ub = (0, 1) if tg == 0 else (2, 3)
                        for k in range(8):
                            S.add('pe', lambda e, k=k, sl=sl, gb=gb, tok0=tok0, fi=fi: e.matmul(
                                PB(gb), lhsT=wgs[sl][:, k, fi * 128:(fi + 1) * 128], rhs=h2T[:, k, tok0:tok0 + 512],
                                start=(k == 0), stop=(k == 7)),
                                  reads=[('wgs', sl), ('h2T', k, hh)], writes=[('ps', gb)])
                        for k in range(8):
                            S.add('pe', lambda e, k=k, sl=sl, ub=ub, tok0=tok0, fi=fi: e.matmul(
                                PB(ub), lhsT=wus[sl][:, k, fi * 128:(fi + 1) * 128], rhs=h2T[:, k, tok0:tok0 + 512],
                                start=(k == 0), stop=(k == 7)),
                                  reads=[('wus', sl), ('h2T', k, hh)], writes=[('ps', ub)])
                        S.add('act', lambda e, tg=tg, gb=gb: e.activation(out=sg[tg], in_=PB(gb), func=AF.Silu), reads=[('ps', gb)], writes=[('sg', tg)])
                        S.add('dve', lambda e, tg=tg, ub=ub, fc=fc: e.tensor_tensor(out=hT[:, fc, tg * 512:(tg + 1) * 512], in0=sg[tg], in1=PB(ub),
                                                                                   op=ALU.mult),
                              reads=[('sg', tg), ('ps', ub)], writes=[('hT', fc, tg)])
            for nh in range(2):
                for fp in range(NFC // 2):
                    sl = fp % 4
                    S.add('pool', lambda e, fp=fp, sl=sl, nh=nh: e.dma_start(
                        out=wds[sl], in_=wd_d[fp * 256:(fp + 1) * 256, nh * 512:(nh + 1) * 512].rearrange("(a p) n -> p a n", p=128)),
                          writes=[('wds', sl)], dma=('wds', sl))
                    for fi in range(2):
                        fc = fp * 2 + fi
                        for t in range(8):
                            S.add('pe', lambda e, fc=fc, sl=sl, t=t, fi=fi: e.matmul(
                                PB(t), lhsT=hT[:, fc, t * 128:(t + 1) * 128], rhs=wds[sl][:, fi, :],
                                start=(fc == 0), stop=(fc == NFC - 1)),
                                  reads=[('wds', sl), ('hT', fc, t // 4)], writes=[('ps', t)])
                for t in range(8):
                    i = hh * 8 + t
                    S.add('act', lambda e, nh=nh, t=t, i=i: e.activation(out=junkh, in_=PB(t), func=AF.Square,
                                                                         accum_out=ssc[:, i, nh:nh + 1]),
                          reads=[('ps', t)], writes=['junkh', ('ssc', i, nh)])
                    tq = h2T[:, t, hh * 1024:(hh + 1) * 1024].bitcast(F32)
                    if nh == 0:
                        S.add('dve', lambda e, t=t, tq=tq: e.tensor_tensor(out=tq, in0=PB(t), in1=G2[:, 0:512], op=ALU.mult),
                              reads=[('ps', t)], writes=[('h2T', t, hh)])
                        continue
                    ob_i = i % 3
                    ot = outs[ob_i]
                    S.add('dve', lambda e, t=t, ot=ot: e.tensor_tensor(out=ot[:, 512:1024], in0=PB(t), in1=G2[:, 512:1024], op=ALU.mult),
                          reads=[('ps', t)], writes=[('outt', ob_i)])
                    S.add('dve', lambda e, i=i: e.tensor_tensor(out=ssc[:, i, 2:3], in0=ssc[:, i, 0:1], in1=ssc[:, i, 1:2], op=ALU.add),
                          reads=[('ssc', i, 0), ('ssc', i, 1)], writes=[('ssc', i, 2)])
                    S.add('act', lambda e, i=i: e.activation(out=ssc[:, i, 3:4], in_=ssc[:, i, 2:3], func=AF.Ln, scale=1.0 / D, bias=epsb),
                          reads=[('ssc', i, 2)], writes=[('ssc', i, 3)])
                    S.add('act', lambda e, i=i: e.activation(out=ssc[:, i, 3:4], in_=ssc[:, i, 3:4], func=AF.Exp, scale=-0.5),
                          reads=[('ssc', i, 3)], writes=[('ssc', i, 3)])
                    S.add('dve', lambda e, i=i, ot=ot: e.scalar_tensor_tensor(
                        out=ot[:, 512:1024], in0=ot[:, 512:1024], scalar=ssc[:, i, 3:4], in1=xres[:, i, 512:1024],
                        op0=ALU.mult, op1=ALU.add), reads=[('outt', ob_i), ('ssc', i, 3)], writes=[('outt', ob_i)])
                    S.add('dve', lambda e, i=i, tq=tq, ot=ot: e.scalar_tensor_tensor(
                        out=ot[:, 0:512], in0=tq, scalar=ssc[:, i, 3:4], in1=xres[:, i, 0:512],
                        op0=ALU.mult, op1=ALU.add), reads=[('h2T', t, hh), ('ssc', i, 3)], writes=[('outt', ob_i)])
                    S.add('sp', lambda e, i=i, ot=ot: e.dma_start(out=out_d[i * 128:(i + 1) * 128, :], in_=ot), reads=[('outt', ob_i)],
                          writes=[('out', i)], dma=('outd', ob_i))

        finalize()
    return nc


def _consts():
    t = np.arange(T, dtype=np.float64)
    ang = 2.0 * np.pi * np.outer(t, t) / T
    dftc = np.cos(ang).astype(np.float32).astype(ml_dtypes.bfloat16)
    dfts = (-np.sin(ang)).astype(np.float32).astype(ml_dtypes.bfloat16)
    m = np.arange(64, dtype=np.float64)
    a64 = 2.0 * np.pi * np.outer(m, m) / 64
    sc = 1.0 / math.sqrt(T * 64)
    cs = np.zeros((256, 512), np.float64)
    for g in range(4):
        cs[g * 64:(g + 1) * 64, g * 64:(g + 1) * 64] = np.cos(a64) * sc
        cs[g * 64:(g + 1) * 64, 256 + g * 64:256 + (g + 1) * 64] = np.sin(a64) * sc
    return dftc, dfts, cs.astype(np.float32).astype(ml_dtypes.bfloat16)


def make_in_maps(inp):
    f = lambda a: np.ascontiguousarray(np.asarray(a, dtype=np.float32))
    dftc, dfts, cs64 = _consts()
    shared = {
        "w_ada": f(inp["w_ada"][0]), "b_ada": f(inp["b_ada"][0]).reshape(1, -1),
        "g_pre_mix": f(inp["g_pre_mix"][0]).reshape(1, -1), "g_post_mix": f(inp["g_post_mix"][0]).reshape(1, -1),
        "g_pre_ffn": f(inp["g_pre_ffn"][0]).reshape(1, -1), "g_post_ffn": f(inp["g_post_ffn"][0]).reshape(1, -1),
        "w_in": f(inp["w_in"][0]), "conv_w": f(inp["conv_w"][0]), "conv_b": f(inp["conv_b"][0]).reshape(-1, 1),
        "dt_bias": f(inp["dt_bias"][0]).reshape(24, 1), "a_log": f(inp["a_log"][0]).reshape(1, 24),
        "d_skip": f(inp["d_skip"][0]).reshape(1, 12), "g_ssd": f(inp["g_ssd"][0]).reshape(-1, 1),
        "w_out": f(inp["w_out"][0]), "w_gate": f(inp["w_gate"][0]), "w_up": f(inp["w_up"][0]), "w_down": f(inp["w_down"][0]),
        "dftc": dftc, "dfts": dfts, "cs64": cs64,
    }
    x = f(inp["x"])
    c = f(inp["c"])
    ctx = f(inp["ctx"])
    cc = f(inp["c_ctx"])
    maps = []
    for b in range(8):
        m = dict(shared)
        m["x"] = x[b]
        m["ctx"] = ctx[b]
        m["c2"] = np.ascontiguousarray(np.stack([c[b], cc], axis=0))
        maps.append(m)
    return maps


def kernel(**inputs):
    nc = build_nc()
    maps = make_in_maps(inputs)
    res = run_bass_kernel_spmd(nc, maps, core_ids=list(range(8)))
    return np.stack([np.asarray(r["out"], dtype=np.float32) for r in res.results], axis=0)
```

```python
import math
from contextlib import ExitStack

import numpy as np
import ml_dtypes

import concourse.bass as bass
import concourse.mybir as mybir
from concourse.bass_utils import run_bass_kernel_spmd

F32 = mybir.dt.float32
BF16 = mybir.dt.bfloat16
U8 = mybir.dt.uint8
AF = mybir.ActivationFunctionType
ALU = mybir.AluOpType

D = 1024
T = 2048
CT = 256
NCH = 18
INW = 2840
DFF = 2816
NFC = 22
EPS = 1e-6
ENGS = ['pe', 'act', 'dve', 'pool', 'sp']


class Op:
    __slots__ = ('eng', 'fn', 'deps', 'sig', 'sigidx', 'dma', 'dsem', 'dval')

    def __init__(self, eng, fn, dma):
        self.eng = eng
        self.fn = fn
        self.deps = {}
        self.sig = False
        self.sigidx = 0
        self.dma = dma
        self.dsem = None
        self.dval = 0


class Sched:
    def __init__(self, nc, stack):
        self.nc = nc
        self.stack = stack
        self.ops = {e: [] for e in ENGS}
        self.reg = {}
        self.dsems = {}
        self.pending = {e: None for e in ENGS}
        self.last_dma = {}

    def add(self, eng, fn, reads=(), writes=(), dma=None):
        op = Op(eng, fn, dma is not None)
        deps = op.deps
        psr = [k for k in reads if isinstance(k, tuple) and k[0] == 'ps']
        if psr:
            reads = [k for k in reads if k not in psr]
            writes = list(writes) + psr
        for k in reads:
            st = self.reg.setdefault(k, [None, []])
            if st[0] is not None:
                deps[st[0]] = 'raw'
            st[1].append(op)
        for k in writes:
            st = self.reg.setdefault(k, [None, []])
            if st[0] is not None and st[0] is not op:
                deps.setdefault(st[0], 'waw')
            for r in st[1]:
                if r is not op:
                    deps.setdefault(r, 'war')
            st[0] = op
            st[1] = []
        if self.pending[eng] is not None:
            for d in self.pending[eng]:
                deps[d] = 'raw'
            self.pending[eng] = None
        if dma is not None:
            ent = self.dsems.get(dma)
            if ent is None or ent[1] >= 224:
                self.nsem = getattr(self, 'nsem', 0) + 1
                sem = self.stack.enter_context(self.nc.semaphore('d%d' % self.nsem))
                ent = [sem, 0]
                self.dsems[dma] = ent
            ent[1] += 16
            op.dsem = ent[0]
            op.dval = ent[1]
            self.last_dma[dma] = op
        self.ops[eng].append(op)
        return op

    def barrier(self):
        last = []
        for e in ENGS:
            for o in reversed(self.ops[e]):
                if not o.dma:
                    last.append(o)
                    break
        self.pending['sp'] = list(last) + list(self.last_dma.values())
        self.reg = {}
        nop = self.add('sp', lambda e: e.nop(nofuse=True))
        import os
        for e in ENGS:
            self.pending[e] = list(last) + ([] if os.environ.get('BAR_SPONLY') else [nop])
        self.pending['sp'] = None

    @staticmethod
    def _keep(op, d, kind):
        if d.dma or op.dma:
            return True
        if d.eng != op.eng:
            return True
        if op.eng == 'pe':
            return False
        return True

    def emit(self):
        nc = self.nc
        for e in ENGS:
            for op in self.ops[e]:
                for d, kind in op.deps.items():
                    if self._keep(op, d, kind):
                        d.sig = True
        esem = {}
        for e in ENGS:
            esem[e] = self.stack.enter_context(nc.semaphore('e_' + e))
            c = 0
            for op in self.ops[e]:
                if op.sig and not op.dma:
                    c += 1
                    op.sigidx = c

        def run(e, eng):
            waited = {}
            for op in self.ops[e]:
                for d, kind in op.deps.items():
                    if not self._keep(op, d, kind):
                        continue
                    if d.dma:
                        sem, val = d.dsem, d.dval
                    else:
                        sem, val = esem[d.eng], d.sigidx
                    key = id(sem)
                    if waited.get(key, 0) >= val:
                        continue
                    eng.wait_ge(sem, val)
                    waited[key] = val
                ins = op.fn(eng)
                if op.dma:
                    ins.then_inc(op.dsem, 16)
                elif op.sig:
                    ins.then_inc(esem[e], 1)

        with nc.Block() as block:
            @block.tensor
            def _(eng):
                run('pe', eng)

            @block.scalar
            def _(eng):
                run('act', eng)

            @block.vector
            def _(eng):
                run('dve', eng)

            @block.gpsimd
            def _(eng):
                run('pool', eng)

            @block.sync
            def _(eng):
                run('sp', eng)


class Mem:
    def __init__(self, arena, size, base=0):
        self.arena = arena
        self.size = size
        self.top = base

    def alloc(self, free, dtype):
        n = 1
        for s in free:
            n *= s
        esz = 4 if dtype == F32 else 2
        nb = n * esz
        off = self.top
        self.top = (off + nb + 63) // 64 * 64
        assert self.top <= self.size, ("SBUF arena overflow", self.top, self.size)
        ap = self.arena[:, off:off + nb].bitcast(dtype)
        if len(free) == 2:
            ap = ap.rearrange("p (a b) -> p a b", a=free[0])
        elif len(free) == 3:
            ap = ap.rearrange("p (a b c) -> p a b c", a=free[0], b=free[1])
        return ap


def bc_mid(ap2, n):
    P, Fd = ap2.shape
    return ap2.rearrange("p (o f) -> p o f", o=1).to_broadcast([P, n, Fd])


def bc_last(ap2, n):
    P, A = ap2.shape
    return ap2.rearrange("p (a o) -> p a o", o=1).to_broadcast([P, A, n])


def build_nc(debug=(), stage=99):
    nc = bass.Bass("TRN2", target_bir_lowering=False)

    def din(name, shape, dt=F32):
        return nc.dram_tensor(name, list(shape), dt, kind="ExternalInput").ap()

    x_d = din("x", [T, D])
    ctx_d = din("ctx", [CT, D])
    c2_d = din("c2", [2, D])
    wada_d = din("w_ada", [D, 6 * D])
    bada_d = din("b_ada", [1, 6 * D])
    gpm_d = din("g_pre_mix", [1, D])
    gqm_d = din("g_post_mix", [1, D])
    gpf_d = din("g_pre_ffn", [1, D])
    gqf_d = din("g_post_ffn", [1, D])
    win_d = din("w_in", [D, INW])
    convw_d = din("conv_w", [1792, 7])
    convb_d = din("conv_b", [1792, 1])
    dtb_d = din("dt_bias", [24, 1])
    alog_d = din("a_log", [1, 24])
    dsk_d = din("d_skip", [1, 12])
    gssd_d = din("g_ssd", [768, 1])
    wout_d = din("w_out", [D, D])
    wg_d = din("w_gate", [D, DFF])
    wu_d = din("w_up", [D, DFF])
    wd_d = din("w_down", [DFF, D])
    dftc_d = din("dftc", [T, T], BF16)
    dfts_d = din("dfts", [T, T], BF16)
    cs64_d = din("cs64", [256, 512], BF16)
    out_d = nc.dram_tensor("out", [T, D], F32, kind="ExternalOutput").ap()
    dbg_d = {}
    for name, shape, dt_ in debug:
        dbg_d[name] = nc.dram_tensor("dbg_" + name, list(shape), dt_, kind="ExternalOutput").ap()

    with ExitStack() as st:
        ARENA = 212480
        arena = st.enter_context(nc.sbuf_tensor("arena", [128, ARENA], U8))
        M = Mem(arena, ARENA)
        banks = [st.enter_context(nc.psum_tensor("ps%d" % i, [128, 512], F32)) for i in range(8)]
        S = Sched(nc, st)

        def PB(i):
            return banks[i][:, :]

        def PBH(i):
            return banks[i][:, :].bitcast(BF16)

        def dump(name, ap, rkeys):
            if name in dbg_d:
                tgt = dbg_d[name]
                S.add('sp', lambda e: e.dma_start(out=tgt, in_=ap), reads=rkeys, writes=[('dbg', name)], dma=('dbg', name))

        def finalize():
            fin = [('dbg', n) for n in dbg_d] + [('out', i) for i in range(16)]
            S.add('sp', lambda e: e.nop(), reads=fin)
            S.emit()

        ident = M.alloc([128], BF16)
        ident32 = M.alloc([128], F32)
        triF = M.alloc([128], F32)
        triB = M.alloc([128], F32)
        ones32 = M.alloc([128], F32)
        maskF3 = M.alloc([3, 128], BF16)
        maskB3 = M.alloc([3, 128], BF16)
        Sel = M.alloc([24, 128], BF16)
        Dsk = M.alloc([12, 128], BF16)
        epsb = M.alloc([1], F32)
        off_abc = M.top
        a_bc_x = M.alloc([D], F32)
        a_bc_c = M.alloc([D], F32)
        a2_bc = M.alloc([D], F32)
        G1 = M.alloc([D], F32)
        off_g1e = M.top
        G2 = M.alloc([D], F32)
        smcol = M.alloc([8, 2], F32)
        sfcol = M.alloc([8, 2], F32)
        convw_col = M.alloc([14, 7], F32)
        convb_col = M.alloc([14], F32)
        dtb_col = M.alloc([1], F32)
        gssd_col = M.alloc([6], F32)
        nA_bc = M.alloc([24], F32)
        dsk_bc = M.alloc([12], F32)
        rstd0 = M.alloc([NCH], F32)
        ssq0 = M.alloc([NCH], F32)
        base_top = M.top

        pool_c = 'pool'
        S.add(pool_c, lambda e: e.memset(epsb, EPS), writes=['epsb'])
        S.add(pool_c, lambda e: e.memset(ident, 0.0), writes=['ident'])
        S.add(pool_c, lambda e: e.affine_select(out=ident, in_=ident, compare_op=ALU.not_equal, fill=1.0, base=0,
                                                pattern=[[-1, 128]], channel_multiplier=1), reads=['ident'], writes=['ident'])
        S.add(pool_c, lambda e: e.memset(ident32, 0.0), writes=['ident32'])
        S.add(pool_c, lambda e: e.affine_select(out=ident32, in_=ident32, compare_op=ALU.not_equal, fill=1.0, base=0,
                                                pattern=[[-1, 128]], channel_multiplier=1), reads=['ident32'], writes=['ident32'])
        S.add(pool_c, lambda e: e.memset(ones32, 1.0), writes=['ones32'])
        S.add(pool_c, lambda e: e.memset(triF, 1.0), writes=['triF'])
        S.add(pool_c, lambda e: e.affine_select(out=triF, in_=triF, compare_op=ALU.is_ge, fill=0.0, base=0,
                                                pattern=[[1, 128]], channel_multiplier=-1), reads=['triF'], writes=['triF'])
        S.add(pool_c, lambda e: e.memset(triB, 1.0), writes=['triB'])
        S.add(pool_c, lambda e: e.affine_select(out=triB, in_=triB, compare_op=ALU.is_ge, fill=0.0, base=0,
                                                pattern=[[-1, 128]], channel_multiplier=1), reads=['triB'], writes=['triB'])
        S.add(pool_c, lambda e: e.memset(maskF3, 0.0), writes=['maskF3'])
        S.add(pool_c, lambda e: e.affine_select(out=maskF3, in_=maskF3, compare_op=ALU.is_ge, fill=-30000.0, base=0,
                                                pattern=[[0, 3], [1, 128]], channel_multiplier=-1), reads=['maskF3'], writes=['maskF3'])
        S.add(pool_c, lambda e: e.memset(maskB3, 0.0), writes=['maskB3'])
        S.add(pool_c, lambda e: e.affine_select(out=maskB3, in_=maskB3, compare_op=ALU.is_ge, fill=-30000.0, base=0,
                                                pattern=[[0, 3], [-1, 128]], channel_multiplier=1), reads=['maskB3'], writes=['maskB3'])
        S.add(pool_c, lambda e: e.memset(Sel, 0.0), writes=['Sel'])
        S.add(pool_c, lambda e: e.affine_select(out=Sel, in_=Sel, compare_op=ALU.not_equal, fill=1.0, base=0,
                                                pattern=[[-1, 24], [0, 128]], channel_multiplier=1), reads=['Sel'], writes=['Sel'])
        S.add(pool_c, lambda e: e.affine_select(out=Sel, in_=Sel, compare_op=ALU.not_equal, fill=1.0, base=-24,
                                                pattern=[[-1, 24], [0, 128]], channel_multiplier=1), reads=['Sel'], writes=['Sel'])

        S.add('sp', lambda e: e.dma_start(out=convw_col, in_=convw_d.rearrange("(c p) k -> p c k", p=128)),
              writes=['convw_col'], dma='convw_col')
        S.add('sp', lambda e: e.dma_start(out=convb_col.rearrange("p (c o) -> p c o", o=1),
                                          in_=convb_d.rearrange("(c p) o -> p c o", p=128), allow_slow_non_contiguous=True),
              writes=['convb_col'], dma='convb_col')
        S.add('sp', lambda e: e.dma_start(out=dtb_col[0:24, :], in_=dtb_d), writes=['dtb_col'], dma='dtb_col')
        S.add('sp', lambda e: e.dma_start(out=gssd_col.rearrange("p (c o) -> p c o", o=1),
                                          in_=gssd_d.rearrange("(c p) o -> p c o", p=128), allow_slow_non_contiguous=True),
              writes=['gssd_col'], dma='gssd_col')
        S.add('sp', lambda e: e.dma_start(out=nA_bc, in_=alog_d.broadcast_to([128, 24])), writes=['nA_bc'], dma='nA_bc')
        S.add('sp', lambda e: e.dma_start(out=dsk_bc, in_=dsk_d.broadcast_to([128, 12])), writes=['dsk_bc'], dma='dsk_bc')

        if stage == 0.5:
            dump('a_bc_x', G1, ['Dsk', 'Sel', 'maskF3', 'maskB3', 'triF', 'triB', 'convw_col', 'convb_col', 'dtb_col', 'gssd_col', 'nA_bc'])
            finalize()
            return nc
        mark_a = M.top
        M.top = mark_a + (8 * T + 8 * CT) * 2
        c2t = M.alloc([D], F32)
        sc32 = M.alloc([D], F32)
        scT = M.alloc([8, 2], BF16)
        mrow = M.alloc([2 * D], F32)
        bada2 = M.alloc([2 * D], F32)
        wa = [M.alloc([8, 512], BF16) for _ in range(4)]
        xall = M.alloc([NCH, D], F32)
        xnp = [M.alloc([D], BF16) for _ in range(2)]
        sel0 = M.alloc([128], F32)
        sel1 = M.alloc([128], F32)

        S.add('sp', lambda e: e.dma_start(out=c2t[0:2, :], in_=c2_d), writes=['c2t'], dma='c2t')
        S.add('sp', lambda e: e.dma_start(out=bada2[0:2, :], in_=bada_d[0:1, 0:2 * D].broadcast_to([2, 2 * D])), writes=['bada2'], dma='bada2')
        S.add('pool', lambda e: e.memset(sel0[0:2, :], 1.0), writes=['sel0'])
        S.add('pool', lambda e: e.affine_select(out=sel0[0:2, :], in_=sel0[0:2, :], compare_op=ALU.is_equal, fill=0.0, base=0,
                                                pattern=[[0, 128]], channel_multiplier=1), reads=['sel0'], writes=['sel0'])
        S.add('pool', lambda e: e.memset(sel1[0:2, :], 1.0), writes=['sel1'])
        S.add('pool', lambda e: e.affine_select(out=sel1[0:2, :], in_=sel1[0:2, :], compare_op=ALU.not_equal, fill=0.0, base=0,
                                                pattern=[[0, 128]], channel_multiplier=1), reads=['sel1'], writes=['sel1'])
        S.add('act', lambda e: e.activation(out=nA_bc, in_=nA_bc, func=AF.Exp), reads=['nA_bc'], writes=['nA_bc'])
        S.add('dve', lambda e: e.tensor_scalar(out=nA_bc, in0=nA_bc, scalar1=-1.0, scalar2=None, op0=ALU.mult),
              reads=['nA_bc'], writes=['nA_bc'])
        S.add('act', lambda e: e.activation(out=sc32[0:2, :], in_=c2t[0:2, :], func=AF.Silu), reads=['c2t'], writes=['sc32'])
        for j in range(8):
            S.add('pe', lambda e, j=j: e.matmul(PB(7)[:, 2 * j:2 * j + 2], lhsT=sc32[0:2, j * 128:(j + 1) * 128],
                                                rhs=ident32[0:2, 0:2], start=True, stop=True),
                  reads=['sc32', 'ident32'], writes=[('ps', 7)])
        S.add('dve', lambda e: e.tensor_copy(out=scT.rearrange("p a b -> p (a b)"), in_=PB(7)[:, 0:16]),
              reads=[('ps', 7)], writes=['scT'])
        screp = Dsk[:, 0:8, :]
        S.add('dve', lambda e: e.tensor_copy(out=screp, in_=scT[:, :, 0:1].to_broadcast([128, 8, 128])), reads=['scT'], writes=['Dsk'])
        if stage == 0.7:
            dump('a_bc_x', G1, ['scT', 'sel0', 'sel1', 'nA_bc'])
            finalize()
            return nc
        wada_v = wada_d.rearrange("(k p) n -> p k n", p=128)
        for t in range(4):
            sl = t % 4
            S.add('pool', lambda e, t=t, sl=sl: e.dma_start(out=wa[sl], in_=wada_v[:, :, t * 512:(t + 1) * 512]),
                  writes=[('wa', sl)], dma=('wa', sl))
            pb = t % 4
            for k in range(8):
                S.add('pe', lambda e, k=k, sl=sl, pb=pb: e.matmul(PB(pb)[0:2, :], lhsT=scT[:, k, :], rhs=wa[sl][:, k, :],
                                                                  start=(k == 0), stop=(k == 7)),
                      reads=['scT', ('wa', sl)], writes=[('ps', pb)])
            S.add('dve', lambda e, t=t, pb=pb: e.tensor_tensor(out=mrow[0:2, t * 512:(t + 1) * 512], in0=PB(pb)[0:2, :],
                                                               in1=bada2[0:2, t * 512:(t + 1) * 512], op=ALU.add),
                  reads=[('ps', pb), 'bada2'], writes=[('mrow', t)])

        if stage == 0.8:
            dump('a_bc_x', mrow[:, 0:1024], [('mrow', t) for t in range(4)])
            finalize()
            return nc

        def mrow_keys(v):
            return [('mrow', 2 * v), ('mrow', 2 * v + 1)]

        acol = a_bc_x[:, 0:16].rearrange("p (a b) -> p a b", a=8)
        gpmcol = a_bc_c[:, 0:8]
        S.add('sp', lambda e: e.dma_start(out=gpmcol, in_=gpm_d[0:1, :].rearrange("o (j p) -> p (o j)", p=128),
                                          allow_slow_non_contiguous=True), writes=['gpmcol'], dma='gpmcol')
        for v in (0, 1):
            for j in range(8):
                S.add('pe', lambda e, v=v, j=j: e.matmul(PB(7)[:, 2 * j:2 * j + 2],
                                                         lhsT=mrow[0:2, v * D + j * 128:v * D + (j + 1) * 128],
                                                         rhs=ident32[0:2, 0:2], start=True, stop=True),
                      reads=mrow_keys(v) + ['ident32'], writes=[('ps', 7)])
            if v == 0:
                S.add('dve', lambda e: e.tensor_copy(out=smcol.rearrange("p a b -> p (a b)"), in_=PB(7)[:, 0:16]),
                      reads=[('ps', 7)], writes=[('col', 0)])
            else:
                S.add('dve', lambda e: e.scalar_tensor_tensor(
                    out=acol, in0=PB(7)[:, 0:16].rearrange("p (a b) -> p a b", a=8), scalar=1.0, in1=bc_last(gpmcol, 2),
                    op0=ALU.add, op1=ALU.mult), reads=[('ps', 7), 'gpmcol'], writes=['acol'])
        dump('a_bc_x', a_bc_x, ['a_bc_x'])

        if stage == 1:
            finalize()
            return nc
        M.top = mark_a

        off_r1 = M.top
        hxT = M.alloc([8, T], BF16)
        hcT = M.alloc([8, CT], BF16)
        off_xbcX = M.top
        xbcX = M.alloc([6, T], BF16)
        off_uT = M.top
        uT = M.alloc([2, T], BF16)
        off_zcm = M.top
        zcm = M.alloc([6, T], BF16)
        xbcBC = M.alloc([8, T], BF16)
        off_xbcc = M.top
        xbcc = M.alloc([14, CT], BF16)
        dtr = M.alloc([T + CT], F32)
        mark_c = M.top
        off_xt = M.top
        xt = [M.alloc([D], F32) for _ in range(4)]
        off_xs = M.top
        xs = [M.alloc([D], BF16) for _ in range(2)]
        off_xs_end = M.top
        wst = [M.alloc([8, 128], BF16) for _ in range(3)]
        dg = [M.alloc([7, 128], BF16) for _ in range(2)]
        padl = [M.alloc([8, 70], BF16) for _ in range(2)]
        padc = M.alloc([CT + 6], BF16)

        def src_tile(i):
            return ctx_d[i * 128:(i + 1) * 128, :] if i < 2 else x_d[(i - 2) * 128:(i - 1) * 128, :]

        xops = []
        for i in range(NCH):
            xops.append(S.add('sp', lambda e, i=i: e.dma_start(out=xall[:, i, :], in_=src_tile(i)), writes=[('xall', i)],
                              dma=('xall', i // 3)))
            if i % 3 == 2:
                for o in xops[-3:]:
                    o.dval = xops[-1].dval
        for i in range(NCH):
            S.add('act', lambda e, i=i: e.activation(out=xnp[i % 2], in_=xall[:, i, :], func=AF.Square, accum_out=ssq0[:, i:i + 1]),
                  reads=[('xall', i)], writes=[('xnp', i % 2), ('ssq0', i)])
        S.add('act', lambda e: e.activation(out=rstd0, in_=ssq0, func=AF.Ln, scale=1.0 / D, bias=epsb),
              reads=[('ssq0', i) for i in range(NCH)] + ['epsb'], writes=['rstd0'])
        S.add('act', lambda e: e.activation(out=rstd0, in_=rstd0, func=AF.Exp, scale=-0.5), reads=['rstd0'], writes=['rstd0'])
        dump('rstd0', rstd0, ['rstd0'])
        if stage == 1.2:
            finalize()
            return nc
        groups = [(0, 2)] + [(2 + 4 * g, 4) for g in range(4)]
        for gidx, (t0, nt) in enumerate(groups):
            if stage == 1.3 and gidx >= 1:
                break
            for ii in range(nt):
                i = t0 + ii
                sl = i % 2
                S.add('dve', lambda e, i=i, sl=sl: e.tensor_scalar(
                    out=xnp[sl], in0=xall[:, i, :], scalar1=rstd0[:, i:i + 1], scalar2=None, op0=ALU.mult),
                      reads=[('xall', i), 'rstd0'], writes=[('xnp', sl)])
                for j in range(8):
                    pb = 4 + j // 2
                    col = (j % 2) * 512 + ii * 128
                    S.add('pe', lambda e, j=j, sl=sl, pb=pb, col=col: e.transpose(
                        out=PBH(pb)[:, col:col + 128], in_=xnp[sl][:, j * 128:(j + 1) * 128], identity=ident),
                          reads=[('xnp', sl), 'ident'], writes=[('ps', pb)])
            for j in range(8):
                pb = 4 + j // 2
                c0 = (j % 2) * 512
                cix = 1 if gidx == 0 else 0
                if gidx == 0:
                    dst = hcT[:, j, :]
                    dkey = ('hcT', j)
                else:
                    dst = hxT[:, j, (gidx - 1) * 512:gidx * 512]
                    dkey = ('hxT', j, gidx - 1)
                if pb % 2 == 0:
                    S.add('act', lambda e, pb=pb, c0=c0, nt=nt, dst=dst, j=j, cix=cix: e.activation(
                        out=dst, in_=PBH(pb)[:, c0:c0 + nt * 128], func=AF.Identity, scale=acol[:, j, cix:cix + 1],
                        bias=smcol[:, j, cix:cix + 1]), reads=[('ps', pb), ('col', 0), 'acol'], writes=[dkey])
                else:
                    S.add('dve', lambda e, pb=pb, c0=c0, nt=nt, dst=dst, j=j, cix=cix: e.tensor_scalar(
                        out=dst, in0=PBH(pb)[:, c0:c0 + nt * 128], scalar1=acol[:, j, cix:cix + 1], scalar2=smcol[:, j, cix:cix + 1],
                        op0=ALU.mult, op1=ALU.add), reads=[('ps', pb), ('col', 0), 'acol'], writes=[dkey])
        if stage != 1.3:
            import os
            _dj = int(os.environ.get('DJ', '0')); _dg = int(os.environ.get('DG', '0'))
            dump('hxT', hxT[:, _dj, _dg * 512:(_dg + 1) * 512], [('hxT', _dj, _dg)])

        if stage in (1.5, 1.3):
            finalize()
            return nc
        cmax = {1.6: 2, 1.7: 8, 1.8: 22}.get(stage, 23)
        S.barrier()
        for p_ in padl:
            S.add('pool', lambda e, p_=p_: e.memset(p_, 0.0), writes=[('pad', id(p_))])
        S.add('pool', lambda e: e.memset(padc, 0.0), writes=[('padc',)])

        win_v = win_d.rearrange("(k p) n -> p k n", p=128)
        items = []
        for c in range(cmax):
            for gidx, (t0, nt) in enumerate(groups):
                if gidx == 0 and c < 8:
                    continue
                items.append((c, gidx, nt))
        seen_c = set()

        def c_mm(n):
            c, gidx, nt = items[n]
            col0 = 128 * c
            wc = 128 if c < 22 else 24
            sl = c % 3
            dsl = c % 2
            cc = c - 8
            if c not in seen_c:
                seen_c.add(c)
                S.add('pool', lambda e, sl=sl, col0=col0, wc=wc: e.dma_start(out=wst[sl][:, :, 0:wc], in_=win_v[:, :, col0:col0 + wc]),
                      writes=[('wst', sl)], dma=('wst', sl))
                if 8 <= c < 22:
                    for k in range(7):
                        S.add('dve', lambda e, k=k, dsl=dsl, cc=cc: e.tensor_scalar(
                            out=dg[dsl][:, k, :], in0=ident, scalar1=convw_col[:, cc, k:k + 1], scalar2=None, op0=ALU.mult),
                              reads=['ident', 'convw_col'], writes=[('dg', dsl)])
            ntok = nt * 128
            if gidx == 0:
                rhs_of = lambda k: hcT[:, k, :]
                rkeys = [('hcT', k) for k in range(8)]
            else:
                rhs_of = lambda k, g=gidx - 1: hxT[:, k, g * 512:(g + 1) * 512]
                rkeys = [('hxT', k, gidx - 1) for k in range(8)]
            pa = n % 2
            for k in range(8):
                S.add('pe', lambda e, k=k, pa=pa, sl=sl, wc=wc, ntok=ntok, rhs_of=rhs_of: e.matmul(
                    PB(pa)[0:wc, 0:ntok], lhsT=wst[sl][:, k, 0:wc], rhs=rhs_of(k), start=(k == 0), stop=(k == 7)),
                      reads=[('wst', sl)] + rkeys, writes=[('ps', pa)])

        def c_post(n):
            c, gidx, nt = items[n]
            ntok = nt * 128
            pa = n % 2
            conv = 8 <= c < 22
            cc = c - 8
            dsl = c % 2
            tok0 = (gidx - 1) * 512
            if c < 2:
                S.add('dve', lambda e, pa=pa, c=c, tok0=tok0: e.tensor_copy(out=uT[:, c, tok0:tok0 + 512], in_=PB(pa)),
                      reads=[('ps', pa)], writes=[('uT', c, gidx)])
            elif c < 8:
                S.add('act', lambda e, pa=pa, c=c, tok0=tok0: e.activation(out=zcm[:, c - 2, tok0:tok0 + 512], in_=PB(pa),
                                                                            func=AF.Silu),
                      reads=[('ps', pa)], writes=[('zcm', c - 2, gidx)])
            elif conv:
                pbb = 2 + pa
                if gidx == 0:
                    S.add('dve', lambda e, pa=pa: e.tensor_copy(out=padc[:, 3:3 + CT], in_=PB(pa)[:, 0:CT]),
                          reads=[('ps', pa)], writes=[('padc',)])
                    for k in range(7):
                        S.add('pe', lambda e, k=k, pbb=pbb, dsl=dsl: e.matmul(
                            PB(pbb)[:, 0:CT], lhsT=dg[dsl][:, k, :], rhs=padc[:, k:k + CT], start=(k == 0), stop=(k == 6)),
                              reads=[('dg', dsl), ('padc',)], writes=[('ps', pbb)])
                    S.add('act', lambda e, pbb=pbb, cc=cc: e.activation(out=xbcc[:, cc, :], in_=PB(pbb)[:, 0:CT], func=AF.Silu,
                                                                        bias=convb_col[:, cc:cc + 1]),
                          reads=[('ps', pbb), 'convb_col'], writes=[('xbcc', cc)])
                else:
                    pl = padl[n % 2]
                    pkey = ('pad', id(pl))
                    S.add('dve', lambda e, pa=pa, pl=pl: e.tensor_copy(
                        out=pl[:, :, 3:67], in_=PB(pa).rearrange("p (r w) -> p r w", r=8)),
                          reads=[('ps', pa)], writes=[pkey])
                    for k in range(7):
                        S.add('pe', lambda e, k=k, pbb=pbb, dsl=dsl, pl=pl: e.matmul(
                            PB(pbb).rearrange("p (r w) -> p r w", r=8), lhsT=dg[dsl][:, k, :], rhs=pl[:, :, k:k + 64],
                            start=(k == 0), stop=(k == 6)),
                              reads=[('dg', dsl), pkey], writes=[('ps', pbb)])
                    S.add('act', lambda e, pbb=pbb, cc=cc, tok0=tok0: e.activation(
                        out=(xbcX[:, cc, tok0:tok0 + 512] if cc < 6 else xbcBC[:, cc - 6, tok0:tok0 + 512]),
                        in_=PB(pbb), func=AF.Silu, bias=convb_col[:, cc:cc + 1]),
                          reads=[('ps', pbb), 'convb_col'], writes=[('xbc', cc, gidx - 1)])
            else:
                d0 = 0 if gidx == 0 else CT + tok0
                S.add('act', lambda e, pa=pa, d0=d0, ntok=ntok: e.activation(
                    out=dtr[0:24, d0:d0 + ntok], in_=PB(pa)[0:24, 0:ntok], func=AF.Identity, bias=dtb_col[0:24, :]),
                      reads=[('ps', pa), 'dtb_col'], writes=[('dtr', gidx)])

        MW = Mem(arena, off_xs, base=off_xt)
        wa2 = [MW.alloc([8, 512], BF16) for _ in range(2)]
        MB = Mem(arena, off_xs_end, base=off_xs)
        btile = MB.alloc([512], F32)
        gtile = MB.alloc([512], F32)
        ada_dst = {2: (G1, gqm_d, False, 'G1'), 4: (a2_bc, gpf_d, True, 'a2_bc'), 5: (G2, gqf_d, False, 'G2')}

        def ada_dma(j):
            t = 4 + j
            sl = j % 2
            S.add('pool', lambda e, t=t, sl=sl: e.dma_start(out=wa2[sl], in_=wada_v[:, :, t * 512:(t + 1) * 512]),
                  writes=[('xt', 2 * sl), ('xt', 2 * sl + 1), ('wa2', sl)], dma=('wa2', sl))

        def ada_mm(j):
            t = 4 + j
            v, half = t // 2, t % 2
            sl = j % 2
            bk = 4 + j % 2
            if v == 3:
                S.add('sp', lambda e, t=t: e.dma_start(out=btile[:, 0:4], in_=bada_d[0:1, t * 512:(t + 1) * 512].rearrange(
                    "o (j p) -> p (o j)", p=128), allow_slow_non_contiguous=True), writes=[('xs', 0), 'btile'], dma='btile')
                for n4 in range(4):
                    for k in range(8):
                        S.add('pe', lambda e, n4=n4, k=k, sl=sl, bk=bk: e.matmul(
                            PB(bk)[:, n4:n4 + 1], lhsT=wa2[sl][:, k, n4 * 128:(n4 + 1) * 128], rhs=screp[:, k, 0:1],
                            start=(k == 0), stop=(k == 7)), reads=[('wa2', sl)], writes=[('ps', bk)])
                S.add('dve', lambda e, half=half, bk=bk: e.tensor_tensor(out=sfcol[:, half * 4:(half + 1) * 4, 0], in0=PB(bk)[:, 0:4],
                                                                         in1=btile[:, 0:4], op=ALU.add),
                      reads=[('ps', bk), 'btile'], writes=[('sfcol', half)])
                return
            dst, gain_d, plus1, key = ada_dst[v]
            S.add('sp', lambda e, t=t: e.dma_start(out=btile, in_=bada_d[0:1, t * 512:(t + 1) * 512].broadcast_to([128, 512])),
                  writes=[('xs', 0), 'btile'], dma='btile')
            S.add('sp', lambda e, half=half, gain_d=gain_d: e.dma_start(
                out=gtile, in_=gain_d[0:1, half * 512:(half + 1) * 512].broadcast_to([128, 512])),
                  writes=[('xs', 1), 'gtile'], dma='gtile')
            for k in range(8):
                S.add('pe', lambda e, k=k, sl=sl, bk=bk: e.matmul(PB(bk), lhsT=screp[:, k, :], rhs=wa2[sl][:, k, :],
                                                                  start=(k == 0), stop=(k == 7)),
                      reads=[('wa2', sl)], writes=[('ps', bk)])
            dh = dst[:, half * 512:(half + 1) * 512]
            if plus1:
                S.add('dve', lambda e, dh=dh, bk=bk: e.scalar_tensor_tensor(out=dh, in0=PB(bk), scalar=1.0, in1=btile,
                                                                           op0=ALU.add, op1=ALU.add),
                      reads=[('ps', bk), 'btile'], writes=[(key, half)])
            else:
                S.add('dve', lambda e, dh=dh, bk=bk: e.tensor_tensor(out=dh, in0=PB(bk), in1=btile, op=ALU.add),
                      reads=[('ps', bk), 'btile'], writes=[(key, half)])
            S.add('dve', lambda e, dh=dh: e.tensor_tensor(out=dh, in0=dh, in1=gtile, op=ALU.mult),
                  reads=[(key, half), 'gtile'], writes=[(key, half)])

        ada_at = {}
        if cmax == 23:
            for j in range(8):
                ada_at[2 + 12 * j] = ('dma', j)
                ada_at[2 + 12 * j + 8] = ('mm', j)
        if items:
            c_mm(0)
        for n in range(len(items)):
            if n + 1 < len(items):
                c_mm(n + 1)
            c_post(n)
            if n in ada_at:
                kind, j = ada_at[n]
                (ada_dma if kind == 'dma' else ada_mm)(j)
        dump('G1', G1, [('G1', 0), ('G1', 1)])
        dump('uT', uT[:, 0, 0:512], [('uT', 0, 1)])
        dump('zcm', zcm[:, 0, 0:512], [('zcm', 0, 1)])
        dump('xbc', xbcX[:, 0, 0:512], [('xbc', 0, 0)])
        dump('xbcc', xbcc[:, 13, :], [('xbcc', 13)])
        dump('dtr', dtr[0:24, 0:512], [('dtr', 0), ('dtr', 1)])

        if stage == 1.9:
            finalize()
            return nc

        def evac(bank, out_ap, in_ap, rkeys, wkeys, bias=None):
            if bank % 2 == 0:
                if bias is None:
                    S.add('act', lambda e: e.activation(out=out_ap, in_=in_ap, func=AF.Identity), reads=rkeys, writes=wkeys)
                else:
                    S.add('act', lambda e: e.activation(out=out_ap, in_=in_ap, func=AF.Identity, bias=bias), reads=rkeys, writes=wkeys)
            else:
                if bias is None:
                    S.add('dve', lambda e: e.tensor_copy(out=out_ap, in_=in_ap), reads=rkeys, writes=wkeys)
                else:
                    S.add('dve', lambda e: e.tensor_scalar(out=out_ap, in0=in_ap, scalar1=bias, scalar2=None, op0=ALU.add),
                          reads=rkeys, writes=wkeys)

        S.barrier()
        ME = Mem(arena, ARENA, base=mark_c)
        A_tok = ME.alloc([16, 512], BF16)
        cs64t = ME.alloc([2, 512], BF16)
        dft = [ME.alloc([2, 1024], BF16) for _ in range(3)]
        MR1 = Mem(arena, off_xbcX, base=off_r1)
        YfT = MR1.alloc([2, T], BF16)
        x_tok = MR1.alloc([NCH, 768], BF16)
        S.add('sp', lambda e: e.dma_start(out=cs64t, in_=cs64_d.rearrange("(k p) n -> p k n", p=128)), writes=['cs64t'], dma='cs64t')
        for i in range(16):
            pb = i % 2
            for kc in range(2):
                S.add('pe', lambda e, i=i, kc=kc, pb=pb: e.matmul(PB(pb), lhsT=uT[:, kc, i * 128:(i + 1) * 128], rhs=cs64t[:, kc, :],
                                                                  start=(kc == 0), stop=(kc == 1)),
                      reads=['cs64t'], writes=[('ps', pb)])
            evac(pb, A_tok[:, i, :], PB(pb), [('ps', pb)], [('A_tok', i)])
        for kh in range(2):
            for i in range(16):
                sl = (kh * 16 + i) % 3
                S.add('sp', lambda e, i=i, sl=sl, kh=kh: e.dma_start(out=dft[sl][:, 0, :],
                                                                   in_=dftc_d[i * 128:(i + 1) * 128, kh * 1024:(kh + 1) * 1024]),
                      writes=[('dftc', sl)], dma=('dftc', sl))
                S.add('sp', lambda e, i=i, sl=sl, kh=kh: e.dma_start(out=dft[sl][:, 1, :],
                                                                   in_=dfts_d[i * 128:(i + 1) * 128, kh * 1024:(kh + 1) * 1024]),
                      writes=[('dfts', sl)], dma=('dfts', sl))
                for jc in range(2):
                    for kt in range(2):
                        bank = 4 + jc * 2 + kt
                        S.add('pe', lambda e, i=i, sl=sl, jc=jc, kt=kt, bank=bank: e.matmul(
                            PB(bank), lhsT=A_tok[:, i, jc * 128:(jc + 1) * 128], rhs=dft[sl][:, 0, kt * 512:(kt + 1) * 512],
                            start=(i == 0), stop=False), reads=[('A_tok', i), ('dftc', sl)], writes=[('ps', bank)])
                        S.add('pe', lambda e, i=i, sl=sl, jc=jc, kt=kt, bank=bank: e.matmul(
                            PB(bank), lhsT=A_tok[:, i, 256 + jc * 128:256 + (jc + 1) * 128], rhs=dft[sl][:, 1, kt * 512:(kt + 1) * 512],
                            start=False, stop=(i == 15)), reads=[('A_tok', i), ('dfts', sl)], writes=[('ps', bank)])
            for jc in range(2):
                for kt in range(2):
                    bank = 4 + jc * 2 + kt
                    c0 = kh * 1024 + kt * 512
                    evac(bank, YfT[:, jc, c0:c0 + 512], PB(bank), [('ps', bank)], [('YfT', jc, kh, kt)])
        dump('YfT', YfT[:, 0, 0:512], [('YfT', 0, 0, 0)])
        if stage == 2:
            finalize()
            return nc

        S.barrier()
        MD = Mem(arena, ARENA, base=mark_c)
        ea = MD.alloc([NCH, 24], F32)
        wstt = MD.alloc([NCH, 24], F32)
        eaend = MD.alloc([NCH, 24], F32)
        acsT_hl = MD.alloc([NCH * 128], BF16)
        uT_hl = MD.alloc([NCH * 128], BF16)
        mark_f = MD.top
        dtl = MD.alloc([NCH, 24], F32)
        dA = MD.alloc([NCH, 24], F32)
        lndt = MD.alloc([NCH, 24], F32)
        acs = MD.alloc([NCH, 24], F32)
        uu = MD.alloc([NCH, 24], F32)
        tmpw = MD.alloc([NCH, 24], F32)
        hlA = MD.alloc([NCH, 48], BF16)
        hlU = MD.alloc([NCH, 48], BF16)
        fl = lambda t: t.rearrange("p a b -> p (a b)")
        NW = NCH * 24
        for cc in range(NCH):
            S.add('pe', lambda e, cc=cc: e.matmul(PB(0)[:, cc * 24:(cc + 1) * 24], lhsT=dtr[0:24, cc * 128:(cc + 1) * 128],
                                                  rhs=ident32[0:24, 0:24], start=True, stop=True), reads=[], writes=[('ps', 0)])
        S.add('act', lambda e: e.activation(out=fl(dtl), in_=PB(0)[:, 0:NW], func=AF.Identity), reads=[('ps', 0)], writes=['dtl'])
        S.add('act', lambda e: e.activation(out=fl(tmpw), in_=fl(dtl), func=AF.Exp), reads=['dtl'], writes=['tmpw'])
        S.add('act', lambda e: e.activation(out=fl(dtl), in_=fl(tmpw), func=AF.Ln, bias=1.0), reads=['tmpw'], writes=['dtl'])
        S.add('act', lambda e: e.activation(out=fl(lndt), in_=fl(dtl), func=AF.Ln), reads=['dtl'], writes=['lndt'])
        S.add('dve', lambda e: e.tensor_tensor(out=dA, in0=dtl, in1=bc_mid(nA_bc, NCH), op=ALU.mult), reads=['dtl'], writes=['dA'])
        acs_ps = PB(1)[:, 0:NW].rearrange("p (c h) -> p c h", c=NCH)
        for c0 in range(0, NCH, 9):
            S.add('pe', lambda e, c0=c0: e.matmul(acs_ps[:, c0:c0 + 9, 0:12], lhsT=triF, rhs=dA[:, c0:c0 + 9, 0:12], start=True, stop=True),
                  reads=['dA'], writes=[('ps', 1)])
            S.add('pe', lambda e, c0=c0: e.matmul(acs_ps[:, c0:c0 + 9, 12:24], lhsT=triB, rhs=dA[:, c0:c0 + 9, 12:24], start=True, stop=True),
                  reads=['dA'], writes=[('ps', 1)])
        for c0 in range(0, NCH, 4):
            n = min(4, NCH - c0)
            S.add('pe', lambda e, c0=c0, n=n: e.matmul(PB(2)[:, c0 * 24:(c0 + n) * 24], lhsT=ones32,
                                                       rhs=dA[:, c0:c0 + n, :].rearrange("p c h -> p (c h)"), start=True, stop=True),
                  reads=['dA'], writes=[('ps', 2)])
        S.add('dve', lambda e: e.tensor_copy(out=fl(acs), in_=PB(1)[:, 0:NW]), reads=[('ps', 1)], writes=['acs'])
        S.add('act', lambda e: e.activation(out=fl(ea), in_=fl(acs), func=AF.Exp), reads=['acs'], writes=['ea'])
        S.add('act', lambda e: e.activation(out=fl(eaend), in_=PB(2)[:, 0:NW], func=AF.Exp), reads=[('ps', 2)], writes=['eaend'])
        S.add('dve', lambda e: e.tensor_tensor(out=fl(uu), in0=fl(lndt), in1=fl(acs), op=ALU.subtract), reads=['lndt', 'acs'], writes=['uu'])
        S.add('dve', lambda e: e.tensor_tensor(out=fl(tmpw), in0=PB(2)[:, 0:NW], in1=fl(uu), op=ALU.add), reads=[('ps', 2), 'uu'], writes=['tmpw'])
        S.add('act', lambda e: e.activation(out=fl(wstt), in_=fl(tmpw), func=AF.Exp), reads=['tmpw'], writes=['wstt'])
        for src, hl, key in ((acs, hlA, 'hlA'), (uu, hlU, 'hlU')):
            S.add('dve', lambda e, src=src, hl=hl: e.tensor_copy(out=hl[:, :, 0:24], in_=src), reads=['acs', 'uu'], writes=[key])
            S.add('dve', lambda e, src=src, hl=hl: e.tensor_tensor(out=hl[:, :, 24:48], in0=src, in1=hl[:, :, 0:24], op=ALU.subtract),
                  reads=['acs', 'uu', key], writes=[key])
        for hl, dstT, key in ((hlA, acsT_hl, 'hlA'), (hlU, uT_hl, 'hlU')):
            for cc in range(NCH):
                bank = 3 + cc // 8
                col = (cc % 8) * 128
                S.add('pe', lambda e, hl=hl, cc=cc, bank=bank, col=col: e.transpose(out=PBH(bank)[0:48, col:col + 128], in_=hl[:, cc, :],
                                                                                   identity=ident), reads=[key], writes=[('ps', bank)])
            for bank in (3, 4, 5):
                c0 = (bank - 3) * 8
                n = min(8, NCH - c0)
                evac(bank, dstT[0:48, c0 * 128:(c0 + n) * 128], PBH(bank)[0:48, 0:n * 128], [('ps', bank)], [key + 'T'])
        for cc in range(NCH):
            bank = 6 + cc % 2
            for j in range(6):
                srcx = xbcc[:, j, cc * 128:(cc + 1) * 128] if cc < 2 else xbcX[:, j, (cc - 2) * 128:(cc - 1) * 128]
                S.add('pe', lambda e, j=j, srcx=srcx, bank=bank: e.transpose(out=PBH(bank)[:, j * 128:(j + 1) * 128], in_=srcx, identity=ident),
                      reads=[], writes=[('ps', bank)])
            evac(bank, x_tok[:, cc, :], PBH(bank)[:, 0:768], [('ps', bank)], [('x_tok', cc)])
        dump('ea', fl(ea), ['ea'])
        dump('wstt', fl(wstt), ['wstt'])
        dump('eaend', fl(eaend), ['eaend'])
        dump('acsT', acsT_hl[0:48, 0:512], ['hlAT'])
        dump('x_tok', x_tok[:, 2, :], [('x_tok', 2)])
        if stage == 3:
            finalize()
            return nc

        S.barrier()
        MF3 = Mem(arena, ARENA, base=mark_f)
        MF4 = Mem(arena, off_zcm, base=off_uT)
        MF5 = Mem(arena, mark_c, base=off_xbcc)
        MX = Mem(arena, off_uT, base=off_xbcX)
        hinB = MX.alloc([16, 768], BF16)
        t1 = MF3.alloc([768], F32)
        t2 = MF3.alloc([768], F32)
        yv = MF3.alloc([768], F32)
        hF = MF3.alloc([768], F32)
        hB = MF3.alloc([768], F32)
        tmpS = MF3.alloc([768], F32)
        E_s = [[MF5.alloc([3, 128], BF16) for _ in range(2)] for _ in range(2)]
        Es = [MF5.alloc([3, 128], BF16) for _ in range(2)]
        Wt = [MF5.alloc([3, 128], BF16) for _ in range(2)]
        xw = [MF5.alloc([768], BF16) for _ in range(2)]
        Btk = [MF5.alloc([512], BF16) for _ in range(2)]
        yn = MF5.alloc([768], BF16)
        junky = MF5.alloc([768], BF16)
        hF_bf = MF5.alloc([768], BF16)
        ssqy = MF4.alloc([16], F32)
        rstdy = MF4.alloc([16], F32)
        TR, SC, EF, EB, Y0, Y1, ST0 = 0, 1, 2, 3, 4, 5, 6
        hv = lambda t: t.rearrange("p (h d) -> p h d", h=12)
        for h in range(12):
            S.add('dve', lambda e, h=h: e.tensor_scalar(out=Dsk[:, h, :], in0=ident, scalar1=dsk_bc[:, h:h + 1], scalar2=None,
                                                        op0=ALU.mult), reads=[], writes=['Dsk'])
        S.add('pool', lambda e: e.memset(hF, 0.0), writes=['hF'])
        S.add('pool', lambda e: e.memset(hB, 0.0), writes=['hB'])
        cnt = [0]

        def st_a(cc, d):
            slot = cnt[0] % 2
            cnt[0] += 1
            for g in range(4):
                srcb = xbcc[:, 6 + g, cc * 128:(cc + 1) * 128] if cc < 2 else xbcBC[:, g, (cc - 2) * 128:(cc - 1) * 128]
                S.add('pe', lambda e, g=g, srcb=srcb: e.transpose(out=PBH(TR)[:, g * 128:(g + 1) * 128], in_=srcb, identity=ident),
                      reads=[], writes=[('ps', TR)])
            evac(TR, Btk[slot], PBH(TR)[:, 0:512], [('ps', TR)], [('Btk', slot)])
            S.add('dve', lambda e, slot=slot, cc=cc, d=d: e.tensor_tensor(
                out=hv(xw[slot]), in0=hv(x_tok[:, cc, :]), in1=bc_last(wstt[:, cc, 12 * d:12 * d + 12], 64), op=ALU.mult),
                  reads=[], writes=[('xw', slot)])
            return slot

        def st_mm(slot, b0):
            for g in range(4):
                bank = b0 + g // 2
                col = (g % 2) * 192
                S.add('pe', lambda e, g=g, bank=bank, col=col, slot=slot: e.matmul(
                    PB(bank)[:, col:col + 192], lhsT=Btk[slot][:, g * 128:(g + 1) * 128], rhs=xw[slot][:, g * 192:(g + 1) * 192],
                    start=True, stop=True), reads=[('Btk', slot), ('xw', slot)], writes=[('ps', bank)])

        def st_rec(cc, d, hst, hkey, b0, meng):
            S.add(meng, lambda e, cc=cc, d=d, hst=hst: e.tensor_tensor(
                out=hv(tmpS), in0=hv(hst), in1=bc_last(eaend[:, cc, 12 * d:12 * d + 12], 64), op=ALU.mult),
                  reads=[hkey], writes=['tmpS'])
            for half in range(2):
                S.add('dve', lambda e, half=half, hst=hst: e.tensor_tensor(
                    out=hst[:, half * 384:(half + 1) * 384], in0=PB(b0 + half)[:, 0:384], in1=tmpS[:, half * 384:(half + 1) * 384],
                    op=ALU.add), reads=[('ps', b0 + half), 'tmpS'], writes=[hkey])

        def st_b(cc, d, hst, hkey, slot, meng='pool'):
            st_mm(slot, ST0)
            st_rec(cc, d, hst, hkey, ST0, meng)

        def st_step(cc, d, hst, hkey, meng='pool'):
            st_b(cc, d, hst, hkey, st_a(cc, d), meng)

        order_b = [1, 0] + list(range(17, 1, -1))
        slot_n = st_a(order_b[0], 1)
        for n, cc in enumerate(order_b):
            slot_c = slot_n
            b0 = ST0 if n % 2 == 0 else Y0
            st_mm(slot_c, b0)
            if n + 1 < len(order_b):
                slot_n = st_a(order_b[n + 1], 1)
            if cc >= 2:
                S.add('act', lambda e, cc=cc: e.activation(out=hinB[:, cc - 2, :], in_=hB, func=AF.Identity),
                      reads=['hB'], writes=[('hinB', cc - 2)])
            st_rec(cc, 1, hB, 'hB', b0, 'dve')
        dump('hinB0', hinB[:, 0, :], [('hinB', 0)])
        if stage == 3.5:
            finalize()
            return nc

        def E_mm(cc, g):
            for d in range(2):
                bank = EF if d == 0 else EB
                mk = maskF3 if d == 0 else maskB3
                S.add('pe', lambda e, bank=bank, mk=mk: e.matmul(PB(bank)[:, 0:384], lhsT=ident, rhs=mk.rearrange("p a b -> p (a b)"),
                                                                 start=True, stop=False), reads=[], writes=[('ps', bank)])
                for hh in range(3):
                    hd = 12 * d + 3 * g + hh
                    S.add('pe', lambda e, bank=bank, hh=hh, hd=hd, cc=cc: e.matmul(
                        PB(bank)[:, hh * 128:(hh + 1) * 128], lhsT=Sel[0:48, hd, :], rhs=acsT_hl[0:48, cc * 128:(cc + 1) * 128],
                        start=False, stop=False), reads=[], writes=[('ps', bank)])
                    S.add('pe', lambda e, bank=bank, hh=hh, hd=hd, cc=cc: e.matmul(
                        PB(bank)[:, hh * 128:(hh + 1) * 128], lhsT=uT_hl[0:48, cc * 128:(cc + 1) * 128], rhs=Sel[0:48, hd, :],
                        start=False, stop=(hh == 2)), reads=[], writes=[('ps', bank)])

        def E_exp(g):
            esl = g % 2
            for d in range(2):
                bank = EF if d == 0 else EB
                S.add('act', lambda e, bank=bank, d=d, esl=esl: e.activation(
                    out=E_s[d][esl].rearrange("p a b -> p (a b)"), in_=PB(bank)[:, 0:384], func=AF.Exp),
                      reads=[('ps', bank)], writes=[('E_s', d, esl)])

        Wd = [Wt, Es]

        def W_y(cc, g):
            esl = g % 2
            for d in range(2):
                S.add('dve', lambda e, esl=esl, g=g, d=d: e.tensor_tensor(
                    out=Wd[d][esl], in0=E_s[d][esl], in1=bc_mid(PB(SC)[:, g * 128:(g + 1) * 128], 3), op=ALU.mult),
                      reads=[('E_s', d, esl), ('ps', SC)], writes=[('Wd', d, esl)])
            for hh in range(3):
                h = 3 * g + hh
                yb, yc = (Y0, h * 64) if h < 8 else (Y1, (h - 8) * 64)
                for d in range(2):
                    S.add('pe', lambda e, esl=esl, hh=hh, h=h, yb=yb, yc=yc, cc=cc, d=d: e.matmul(
                        PB(yb)[:, yc:yc + 64], lhsT=Wd[d][esl][:, hh, :], rhs=x_tok[:, cc, h * 64:(h + 1) * 64],
                        start=(d == 0), stop=False), reads=[('Wd', d, esl)], writes=[('ps', yb)])
                S.add('pe', lambda e, h=h, yb=yb, yc=yc, cc=cc: e.matmul(
                    PB(yb)[:, yc:yc + 64], lhsT=Dsk[:, h, :], rhs=x_tok[:, cc, h * 64:(h + 1) * 64], start=False, stop=True),
                      reads=['Dsk'], writes=[('ps', yb)])

        yvs = [yv, MF4.alloc([768], F32)]

        def head(cc):
            l = cc - 2
            t0 = l * 128
            S.add('act', lambda e: e.activation(out=hF_bf, in_=hF, func=AF.Identity), reads=['hF'], writes=['hF_bf'])
            for g in range(4):
                S.add('pe', lambda e, g=g, t0=t0: e.matmul(PB(SC)[:, g * 128:(g + 1) * 128], lhsT=xbcBC[:, g, t0:t0 + 128],
                                                           rhs=xbcBC[:, 4 + g, t0:t0 + 128], start=True, stop=True),
                      reads=[], writes=[('ps', SC)])
            E_mm(cc, 0)
            E_exp(0)
            for d, tdst, hsrc, hk in ((0, t1, hF_bf, 'hF_bf'), (1, t2, hinB[:, l, :], ('hinB', l))):
                for g in range(4):
                    bank = ST0 + g // 2
                    col = (g % 2) * 192
                    S.add('pe', lambda e, g=g, bank=bank, col=col, hsrc=hsrc, t0=t0: e.matmul(
                        PB(bank)[:, col:col + 192], lhsT=xbcBC[:, 4 + g, t0:t0 + 128], rhs=hsrc[:, g * 192:(g + 1) * 192],
                        start=True, stop=True), reads=[hk], writes=[('ps', bank)])
                for half in range(2):
                    S.add('dve', lambda e, half=half, tdst=tdst, d=d, cc=cc: e.tensor_tensor(
                        out=tdst[:, half * 384:(half + 1) * 384].rearrange("p (h d) -> p h d", h=6),
                        in0=PB(ST0 + half)[:, 0:384].rearrange("p (h d) -> p h d", h=6),
                        in1=bc_last(ea[:, cc, 12 * d + 6 * half:12 * d + 6 * half + 6], 64), op=ALU.mult),
                          reads=[('ps', ST0 + half)], writes=[('t', d)])
                if d == 0:
                    E_mm(cc, 1)
                    E_exp(1)
            S.add('pool', lambda e: e.tensor_tensor(out=t1, in0=t1, in1=t2, op=ALU.add), reads=[('t', 0), ('t', 1)], writes=[('t', 0)])

        def body(cc, mid_hook=None):
            yvc = yvs[cc % 2]
            ykey = ('yv', cc % 2)
            W_y(cc, 0)
            E_mm(cc, 2)
            E_exp(2)
            slot_f = st_a(cc, 0)
            W_y(cc, 1)
            if mid_hook is not None:
                mid_hook()
            E_mm(cc, 3)
            E_exp(3)
            W_y(cc, 2)
            st_b(cc, 0, hF, 'hF', slot_f)
            W_y(cc, 3)
            S.add('dve', lambda e: e.tensor_tensor(out=yvc[:, 0:512], in0=PB(Y0), in1=t1[:, 0:512], op=ALU.add),
                  reads=[('ps', Y0), ('t', 0)], writes=[ykey])
            S.add('dve', lambda e: e.tensor_tensor(out=yvc[:, 512:768], in0=PB(Y1)[:, 0:256], in1=t1[:, 512:768], op=ALU.add),
                  reads=[('ps', Y1), ('t', 0)], writes=[ykey])

        def tail2a(cc):
            l = cc - 2
            t0 = l * 128
            yvc = yvs[cc % 2]
            ykey = ('yv', cc % 2)
            for j in range(6):
                S.add('pe', lambda e, j=j, t0=t0: e.transpose(out=PBH(TR)[:, j * 128:(j + 1) * 128], in_=zcm[:, j, t0:t0 + 128], identity=ident),
                      reads=[], writes=[('ps', TR)])
            S.add('dve', lambda e: e.tensor_tensor(out=yvc, in0=yvc, in1=PBH(TR)[:, 0:768], op=ALU.mult), reads=[ykey, ('ps', TR)], writes=[ykey])
            S.add('act', lambda e, l=l: e.activation(out=junky, in_=yvc, func=AF.Square, accum_out=ssqy[:, l:l + 1]),
                  reads=[ykey], writes=['junky', ('ssqy', l)])
            S.add('act', lambda e, l=l: e.activation(out=rstdy[:, l:l + 1], in_=ssqy[:, l:l + 1], func=AF.Ln, scale=1.0 / 768, bias=epsb),
                  reads=[('ssqy', l)], writes=[('rstdy', l)])
            S.add('act', lambda e, l=l: e.activation(out=rstdy[:, l:l + 1], in_=rstdy[:, l:l + 1], func=AF.Exp, scale=-0.5),
                  reads=[('rstdy', l)], writes=[('rstdy', l)])
            S.add('act', lambda e, l=l: e.activation(out=yn, in_=yvc, func=AF.Copy, scale=rstdy[:, l:l + 1]),
                  reads=[ykey, ('rstdy', l)], writes=['yn'])

        def tail2b(cc):
            l = cc - 2
            for j in range(6):
                S.add('pe', lambda e, j=j: e.transpose(out=PBH(TR)[:, j * 128:(j + 1) * 128], in_=yn[:, j * 128:(j + 1) * 128], identity=ident),
                      reads=['yn'], writes=[('ps', TR)])
            evac(TR, hinB[:, l, :], PBH(TR)[:, 0:768], [('ps', TR)], [('hinB', l)])

        for cc in (0, 1):
            st_step(cc, 0, hF, 'hF')
        head(2)
        body(2)
        for cc in range(3, NCH):
            head(cc)
            tail2a(cc - 1)
            body(cc, mid_hook=(lambda c=cc - 1: tail2b(c)))
        tail2a(NCH - 1)
        tail2b(NCH - 1)
        dump('ynT0', hinB[:, 0, :], [('hinB', 0)])
        dump('ynT15', hinB[:, 15, :], [('hinB', 15)])
        if stage == 4:
            finalize()
            return nc

        S.barrier()
        MG1 = Mem(arena, off_xbcX, base=off_r1 + 2 * T * 2)
        wout = MG1.alloc([8, D], BF16)
        tt = MG1.alloc([D], F32)
        xs2 = [MG1.alloc([D], BF16) for _ in range(2)]
        junkg = MG1.alloc([D], BF16)
        MG = Mem(arena, ARENA, base=off_uT)
        xres = MG.alloc([16, D], F32)
        h2T = MG.alloc([8, T], BF16)
        xt2 = [MG.alloc([D], F32) for _ in range(2)]
        ssa = MG.alloc([16, 4], F32)
        ssb = MG.alloc([16, 2], F32)
        ssc = MG.alloc([16, 4], F32)
        S.add('pool', lambda e: e.dma_start(out=wout, in_=wout_d.rearrange("(k p) n -> p k n", p=128)), writes=['wout'], dma='wout')
        for j in range(6):
            S.add('dve', lambda e, j=j: e.tensor_scalar(out=wout[:, 2 + j, :], in0=wout[:, 2 + j, :], scalar1=gssd_col[:, j:j + 1], scalar2=None,
                                                        op0=ALU.mult), reads=['wout'], writes=['wout'])
        def g_mm(i):
            sl = i % 2
            S.add('sp', lambda e, i=i, sl=sl: e.dma_start(out=xt2[sl], in_=x_d[i * 128:(i + 1) * 128, :]), writes=[('xt2', sl)], dma=('xt2', sl))
            ob = (0, 1) if i % 2 == 0 else (2, 3)
            for nh in range(2):
                for k in range(8):
                    lt = YfT[:, k, i * 128:(i + 1) * 128] if k < 2 else hinB[:, i, (k - 2) * 128:(k - 1) * 128]
                    S.add('pe', lambda e, nh=nh, k=k, lt=lt, ob=ob: e.matmul(PB(ob[nh]), lhsT=lt, rhs=wout[:, k, nh * 512:(nh + 1) * 512],
                                                                            start=(k == 0), stop=(k == 7)),
                          reads=['wout'], writes=[('ps', ob[nh])])

        def g_s1a(i):
            ob = (0, 1) if i % 2 == 0 else (2, 3)
            for nh in range(2):
                S.add('act', lambda e, nh=nh, ob=ob, i=i: e.activation(out=junkg[:, 0:512], in_=PB(ob[nh]), func=AF.Square,
                                                                       accum_out=ssa[:, i, nh:nh + 1]),
                      reads=[('ps', ob[nh])], writes=['junkg', ('ssa', i, nh)])
            for nh in range(2):
                S.add('dve', lambda e, nh=nh, ob=ob: e.tensor_tensor(out=tt[:, nh * 512:(nh + 1) * 512], in0=PB(ob[nh]),
                                                                     in1=G1[:, nh * 512:(nh + 1) * 512], op=ALU.mult),
                      reads=[('ps', ob[nh])], writes=['tt'])
            S.add('dve', lambda e, i=i: e.tensor_tensor(out=ssa[:, i, 2:3], in0=ssa[:, i, 0:1], in1=ssa[:, i, 1:2], op=ALU.add),
                  reads=[('ssa', i, 0), ('ssa', i, 1)], writes=[('ssa', i, 2)])

        def g_s1b(i):
            sl = i % 2
            S.add('act', lambda e, i=i: e.activation(out=ssa[:, i, 3:4], in_=ssa[:, i, 2:3], func=AF.Ln, scale=1.0 / D, bias=epsb),
                  reads=[('ssa', i, 2)], writes=[('ssa', i, 3)])
            S.add('act', lambda e, i=i: e.activation(out=ssa[:, i, 3:4], in_=ssa[:, i, 3:4], func=AF.Exp, scale=-0.5),
                  reads=[('ssa', i, 3)], writes=[('ssa', i, 3)])
            S.add('dve', lambda e, i=i, sl=sl: e.scalar_tensor_tensor(out=xres[:, i, :], in0=tt, scalar=ssa[:, i, 3:4], in1=xt2[sl],
                                                                      op0=ALU.mult, op1=ALU.add),
                  reads=['tt', ('ssa', i, 3), ('xt2', sl)], writes=[('xres', i)])

        def g_s2a(i):
            S.add('act', lambda e, i=i: e.activation(out=junkg2, in_=xres[:, i, :], func=AF.Square, accum_out=ssb[:, i, 0:1]),
                  reads=[('xres', i)], writes=['junkg2', ('ssb', i, 0)])
            S.add('act', lambda e, i=i: e.activation(out=ssb[:, i, 1:2], in_=ssb[:, i, 0:1], func=AF.Ln, scale=1.0 / D, bias=epsb),
                  reads=[('ssb', i, 0)], writes=[('ssb', i, 1)])
            S.add('act', lambda e, i=i: e.activation(out=ssb[:, i, 1:2], in_=ssb[:, i, 1:2], func=AF.Exp, scale=-0.5),
                  reads=[('ssb', i, 1)], writes=[('ssb', i, 1)])

        def g_s2b(i):
            sl = i % 2
            ii = i % 4
            S.add('dve', lambda e, i=i, sl=sl: e.scalar_tensor_tensor(out=xs2[sl], in0=xres[:, i, :], scalar=ssb[:, i, 1:2], in1=a2_bc,
                                                                      op0=ALU.mult, op1=ALU.mult),
                  reads=[('xres', i), ('ssb', i, 1)], writes=[('xs2', sl)])
            for j in range(8):
                pb = 4 + j // 2
                col = (j % 2) * 512 + ii * 128
                S.add('pe', lambda e, j=j, sl=sl, pb=pb, col=col: e.transpose(out=PBH(pb)[:, col:col + 128],
                                                                             in_=xs2[sl][:, j * 128:(j + 1) * 128], identity=ident),
                      reads=[('xs2', sl)], writes=[('ps', pb)])

        junkg2 = MG.alloc([D], BF16)
        g_mm(0)
        g_mm(1)
        g_s1a(0)
        g_s1b(0)
        for i in range(16):
            if i + 2 < 16:
                g_mm(i + 2)
            if i + 1 < 16:
                g_s1a(i + 1)
            g_s2a(i)
            if i + 1 < 16:
                g_s1b(i + 1)
            g_s2b(i)
            if i % 4 == 3:
                g4 = i // 4
                for j in range(8):
                    pb = 4 + j // 2
                    c0 = (j % 2) * 512
                    evac(pb, h2T[:, j, g4 * 512:(g4 + 1) * 512], PBH(pb)[:, c0:c0 + 512], [('ps', pb)], [('h2T', j, g4)],
                         bias=sfcol[:, j, 0:1])
        dump('xres0', xres[:, 0, :], [('xres', 0)])
        dump('h2T', h2T[:, 0, 0:512], [('h2T', 0, 0)])
        if stage == 5:
            finalize()
            return nc

        S.barrier()
        MH = Mem(arena, off_uT, base=off_r1)
        hT = MH.alloc([NFC, 1024], BF16)
        wgs = [MH.alloc([8, 256], BF16) for _ in range(2)]
        wus = [MH.alloc([8, 256], BF16) for _ in range(2)]
        MHc = Mem(arena, off_g1e, base=off_abc)
        wds = [MHc.alloc([2, 512], BF16) for _ in range(4)]
        sg = [MHc.alloc([512], BF16) for _ in range(2)]
        junkh = MHc.alloc([512], BF16)
        outs = [xt2[0], xt2[1], MG.alloc([D], F32)]
        wg_v = wg_d.rearrange("(k p) n -> p k n", p=128)
        wu_v = wu_d.rearrange("(k p) n -> p k n", p=128)
        for hh in range(2):
            for fp in range(NFC // 2):
                sl = fp % 2
                S.add('pool', lambda e, fp=fp, sl=sl: e.dma_start(out=wgs[sl], in_=wg_v[:, :, fp * 256:(fp + 1) * 256]),
                      writes=[('wgs', sl)], dma=('wgs', sl))
                S.add('pool', lambda e, fp=fp, sl=sl: e.dma_start(out=wus[sl], in_=wu_v[:, :, fp * 256:(fp + 1) * 256]),
                      writes=[('wus', sl)], dma=('wus', sl))
                for fi in range(2):
                    fc = fp * 2 + fi
                    for tg in range(2):
                        tok0 = hh * 1024 + tg * 512
                        gb, ub = (0, 1) if tg == 0 else (2, 3)
                        for k in range(8):
                            S.add('pe', lambda e, k=k, sl=sl, gb=gb, tok0=tok0, fi=fi: e.matmul(
                                PB(gb), lhsT=wgs[sl][:, k, fi * 128:(fi + 1) * 128], rhs=h2T[:, k, tok0:tok0 + 512],
                                start=(k == 0), stop=(k == 7)),
                                  reads=[('wgs', sl), ('h2T', k, hh)], writes=[('ps', gb)])
                        for k in range(8):
                            S.add('pe', lambda e, k=k, sl=sl, ub=ub, tok0=tok0, fi=fi: e.matmul(
                                PB(ub), lhsT=wus[sl][:, k, fi * 128:(fi + 1) * 128], rhs=h2T[:, k, tok0:tok0 + 512],
                                start=(k == 0), stop=(k == 7)),
                                  reads=[('wus', sl), ('h2T', k, hh)], writes=[('ps', ub)])
                        S.add('act', lambda e, tg=tg, gb=gb: e.activation(out=sg[tg], in_=PB(gb), func=AF.Silu), reads=[('ps', gb)], writes=[('sg', tg)])
                        S.add('dve', lambda e, tg=tg, ub=ub, fc=fc: e.tensor_tensor(out=hT[:, fc, tg * 512:(tg + 1) * 512], in0=sg[tg], in1=PB(ub),
                                                                                   op=ALU.mult),
                              reads=[('sg', tg), ('ps', ub)], writes=[('hT', fc, tg)])
            for nh in range(2):
                for fp in range(NFC // 2):
                    sl = fp % 4
                    S.add('pool', lambda e, fp=fp, sl=sl, nh=nh: e.dma_start(
                        out=wds[sl], in_=wd_d[fp * 256:(fp + 1) * 256, nh * 512:(nh + 1) * 512].rearrange("(a p) n -> p a n", p=128)),
                          writes=[('wds', sl)], dma=('wds', sl))
                    for fi in range(2):
                        fc = fp * 2 + fi
                        for t in range(8):
                            S.add('pe', lambda e, fc=fc, sl=sl, t=t, fi=fi: e.matmul(
                                PB(t), lhsT=hT[:, fc, t * 128:(t + 1) * 128], rhs=wds[sl][:, fi, :],
                                start=(fc == 0), stop=(fc == NFC - 1)),
                                  reads=[('wds', sl), ('hT', fc, t // 4)], writes=[('ps', t)])
                for t in range(8):
                    i = hh * 8 + t
                    S.add('act', lambda e, nh=nh, t=t, i=i: e.activation(out=junkh, in_=PB(t), func=AF.Square,
                                                                         accum_out=ssc[:, i, nh:nh + 1]),
                          reads=[('ps', t)], writes=['junkh', ('ssc', i, nh)])
                    tq = h2T[:, t, hh * 1024:(hh + 1) * 1024].bitcast(F32)
                    if nh == 0:
                        S.add('dve', lambda e, t=t, tq=tq: e.tensor_tensor(out=tq, in0=PB(t), in1=G2[:, 0:512], op=ALU.mult),
                              reads=[('ps', t)], writes=[('h2T', t, hh)])
                        continue
                    ob_i = i % 3
                    ot = outs[ob_i]
                    S.add('dve', lambda e, t=t, ot=ot: e.tensor_tensor(out=ot[:, 512:1024], in0=PB(t), in1=G2[:, 512:1024], op=ALU.mult),
                          reads=[('ps', t)], writes=[('outt', ob_i)])
                    S.add('dve', lambda e, i=i: e.tensor_tensor(out=ssc[:, i, 2:3], in0=ssc[:, i, 0:1], in1=ssc[:, i, 1:2], op=ALU.add),
                          reads=[('ssc', i, 0), ('ssc', i, 1)], writes=[('ssc', i, 2)])
                    S.add('act', lambda e, i=i: e.activation(out=ssc[:, i, 3:4], in_=ssc[:, i, 2:3], func=AF.Ln, scale=1.0 / D, bias=epsb),
                          reads=[('ssc', i, 2)], writes=[('ssc', i, 3)])
                    S.add('act', lambda e, i=i: e.activation(out=ssc[:, i, 3:4], in_=ssc[:, i, 3:4], func=AF.Exp, scale=-0.5),
                          reads=[('ssc', i, 3)], writes=[('ssc', i, 3)])
                    S.add('dve', lambda e, i=i, ot=ot: e.scalar_tensor_tensor(
                        out=ot[:, 512:1024], in0=ot[:, 512:1024], scalar=ssc[:, i, 3:4], in1=xres[:, i, 512:1024],
                        op0=ALU.mult, op1=ALU.add), reads=[('outt', ob_i), ('ssc', i, 3)], writes=[('outt', ob_i)])
                    S.add('dve', lambda e, i=i, tq=tq, ot=ot: e.scalar_tensor_tensor(
                        out=ot[:, 0:512], in0=tq, scalar=ssc[:, i, 3:4], in1=xres[:, i, 0:512],
                        op0=ALU.mult, op1=ALU.add), reads=[('h2T', t, hh), ('ssc', i, 3)], writes=[('outt', ob_i)])
                    S.add('sp', lambda e, i=i, ot=ot: e.dma_start(out=out_d[i * 128:(i + 1) * 128, :], in_=ot), reads=[('outt', ob_i)],
                          writes=[('out', i)], dma=('outd', ob_i))

        finalize()
    return nc


def _consts():
    t = np.arange(T, dtype=np.float64)
    ang = 2.0 * np.pi * np.outer(t, t) / T
    dftc = np.cos(ang).astype(np.float32).astype(ml_dtypes.bfloat16)
    dfts = (-np.sin(ang)).astype(np.float32).astype(ml_dtypes.bfloat16)
    m = np.arange(64, dtype=np.float64)
    a64 = 2.0 * np.pi * np.outer(m, m) / 64
    sc = 1.0 / math.sqrt(T * 64)
    cs = np.zeros((256, 512), np.float64)
    for g in range(4):
        cs[g * 64:(g + 1) * 64, g * 64:(g + 1) * 64] = np.cos(a64) * sc
        cs[g * 64:(g + 1) * 64, 256 + g * 64:256 + (g + 1) * 64] = np.sin(a64) * sc
    return dftc, dfts, cs.astype(np.float32).astype(ml_dtypes.bfloat16)


def make_in_maps(inp):
    f = lambda a: np.ascontiguousarray(np.asarray(a, dtype=np.float32))
    dftc, dfts, cs64 = _consts()
    shared = {
        "w_ada": f(inp["w_ada"][0]), "b_ada": f(inp["b_ada"][0]).reshape(1, -1),
        "g_pre_mix": f(inp["g_pre_mix"][0]).reshape(1, -1), "g_post_mix": f(inp["g_post_mix"][0]).reshape(1, -1),
        "g_pre_ffn": f(inp["g_pre_ffn"][0]).reshape(1, -1), "g_post_ffn": f(inp["g_post_ffn"][0]).reshape(1, -1),
        "w_in": f(inp["w_in"][0]), "conv_w": f(inp["conv_w"][0]), "conv_b": f(inp["conv_b"][0]).reshape(-1, 1),
        "dt_bias": f(inp["dt_bias"][0]).reshape(24, 1), "a_log": f(inp["a_log"][0]).reshape(1, 24),
        "d_skip": f(inp["d_skip"][0]).reshape(1, 12), "g_ssd": f(inp["g_ssd"][0]).reshape(-1, 1),
        "w_out": f(inp["w_out"][0]), "w_gate": f(inp["w_gate"][0]), "w_up": f(inp["w_up"][0]), "w_down": f(inp["w_down"][0]),
        "dftc": dftc, "dfts": dfts, "cs64": cs64,
    }
    x = f(inp["x"])
    c = f(inp["c"])
    ctx = f(inp["ctx"])
    cc = f(inp["c_ctx"])
    maps = []
    for b in range(8):
        m = dict(shared)
        m["x"] = x[b]
        m["ctx"] = ctx[b]
        m["c2"] = np.ascontiguousarray(np.stack([c[b], cc], axis=0))
        maps.append(m)
    return maps


def kernel(**inputs):
    nc = build_nc()
    maps = make_in_maps(inputs)
    res = run_bass_kernel_spmd(nc, maps, core_ids=list(range(8)))
    return np.stack([np.asarray(r["out"], dtype=np.float32) for r in res.results], axis=0)
```

```python
import math
from contextlib import ExitStack

import numpy as np
import ml_dtypes

import concourse.bass as bass
import concourse.mybir as mybir
from concourse.bass_utils import run_bass_kernel_spmd

F32 = mybir.dt.float32
BF16 = mybir.dt.bfloat16
U8 = mybir.dt.uint8
AF = mybir.ActivationFunctionType
ALU = mybir.AluOpType

D = 1024
T = 2048
CT = 256
NCH = 18
INW = 2840
DFF = 2816
NFC = 22
EPS = 1e-6
ENGS = ['pe', 'act', 'dve', 'pool', 'sp']


class Op:
    __slots__ = ('eng', 'fn', 'deps', 'sig', 'sigidx', 'dma', 'dsem', 'dval')

    def __init__(self, eng, fn, dma):
        self.eng = eng
        self.fn = fn
        self.deps = {}
        self.sig = False
        self.sigidx = 0
        self.dma = dma
        self.dsem = None
        self.dval = 0


class Sched:
    def __init__(self, nc, stack):
        self.nc = nc
        self.stack = stack
        self.ops = {e: [] for e in ENGS}
        self.reg = {}
        self.dsems = {}
        self.pending = {e: None for e in ENGS}
        self.last_dma = {}

    def add(self, eng, fn, reads=(), writes=(), dma=None):
        op = Op(eng, fn, dma is not None)
        deps = op.deps
        psr = [k for k in reads if isinstance(k, tuple) and k[0] == 'ps']
        if psr:
            reads = [k for k in reads if k not in psr]
            writes = list(writes) + psr
        for k in reads:
            st = self.reg.setdefault(k, [None, []])
            if st[0] is not None:
                deps[st[0]] = 'raw'
            st[1].append(op)
        for k in writes:
            st = self.reg.setdefault(k, [None, []])
            if st[0] is not None and st[0] is not op:
                deps.setdefault(st[0], 'waw')
            for r in st[1]:
                if r is not op:
                    deps.setdefault(r, 'war')
            st[0] = op
            st[1] = []
        if self.pending[eng] is not None:
            for d in self.pending[eng]:
                deps[d] = 'raw'
            self.pending[eng] = None
        if dma is not None:
            ent = self.dsems.get(dma)
            if ent is None or ent[1] >= 224:
                self.nsem = getattr(self, 'nsem', 0) + 1
                sem = self.stack.enter_context(self.nc.semaphore('d%d' % self.nsem))
                ent = [sem, 0]
                self.dsems[dma] = ent
            ent[1] += 16
            op.dsem = ent[0]
            op.dval = ent[1]
            self.last_dma[dma] = op
        self.ops[eng].append(op)
        return op

    def barrier(self):
        last = []
        for e in ENGS:
            for o in reversed(self.ops[e]):
                if not o.dma:
                    last.append(o)
                    break
        self.pending['sp'] = list(last) + list(self.last_dma.values())
        self.reg = {}
        nop = self.add('sp', lambda e: e.nop(nofuse=True))
        import os
        for e in ENGS:
            self.pending[e] = list(last) + ([] if os.environ.get('BAR_SPONLY') else [nop])
        self.pending['sp'] = None

    @staticmethod
    def _keep(op, d, kind):
        if d.dma or op.dma:
            return True
        if d.eng != op.eng:
            return True
        if op.eng == 'pe':
            return False
        return True

    def emit(self):
        nc = self.nc
        for e in ENGS:
            for op in self.ops[e]:
                for d, kind in op.deps.items():
                    if self._keep(op, d, kind):
                        d.sig = True
        esem = {}
        for e in ENGS:
            esem[e] = self.stack.enter_context(nc.semaphore('e_' + e))
            c = 0
            for op in self.ops[e]:
                if op.sig and not op.dma:
                    c += 1
                    op.sigidx = c

        def run(e, eng):
            waited = {}
            for op in self.ops[e]:
                for d, kind in op.deps.items():
                    if not self._keep(op, d, kind):
                        continue
                    if d.dma:
                        sem, val = d.dsem, d.dval
                    else:
                        sem, val = esem[d.eng], d.sigidx
                    key = id(sem)
                    if waited.get(key, 0) >= val:
                        continue
                    eng.wait_ge(sem, val)
                    waited[key] = val
                ins = op.fn(eng)
                if op.dma:
                    ins.then_inc(op.dsem, 16)
                elif op.sig:
                    ins.then_inc(esem[e], 1)

        with nc.Block() as block:
            @block.tensor
            def _(eng):
                run('pe', eng)

            @block.scalar
            def _(eng):
                run('act', eng)

            @block.vector
            def _(eng):
                run('dve', eng)

            @block.gpsimd
            def _(eng):
                run('pool', eng)

            @block.sync
            def _(eng):
                run('sp', eng)


class Mem:
    def __init__(self, arena, size, base=0):
        self.arena = arena
        self.size = size
        self.top = base

    def alloc(self, free, dtype):
        n = 1
        for s in free:
            n *= s
        esz = 4 if dtype == F32 else 2
        nb = n * esz
        off = self.top
        self.top = (off + nb + 63) // 64 * 64
        assert self.top <= self.size, ("SBUF arena overflow", self.top, self.size)
        ap = self.arena[:, off:off + nb].bitcast(dtype)
        if len(free) == 2:
            ap = ap.rearrange("p (a b) -> p a b", a=free[0])
        elif len(free) == 3:
            ap = ap.rearrange("p (a b c) -> p a b c", a=free[0], b=free[1])
        return ap


def bc_mid(ap2, n):
    P, Fd = ap2.shape
    return ap2.rearrange("p (o f) -> p o f", o=1).to_broadcast([P, n, Fd])


def bc_last(ap2, n):
    P, A = ap2.shape
    return ap2.rearrange("p (a o) -> p a o", o=1).to_broadcast([P, A, n])


def build_nc(debug=(), stage=99):
    nc = bass.Bass("TRN2", target_bir_lowering=False)

    def din(name, shape, dt=F32):
        return nc.dram_tensor(name, list(shape), dt, kind="ExternalInput").ap()

    x_d = din("x", [T, D])
    ctx_d = din("ctx", [CT, D])
    c2_d = din("c2", [2, D])
    wada_d = din("w_ada", [D, 6 * D])
    bada_d = din("b_ada", [1, 6 * D])
    gpm_d = din("g_pre_mix", [1, D])
    gqm_d = din("g_post_mix", [1, D])
    gpf_d = din("g_pre_ffn", [1, D])
    gqf_d = din("g_post_ffn", [1, D])
    win_d = din("w_in", [D, INW])
    convw_d = din("conv_w", [1792, 7])
    convb_d = din("conv_b", [1792, 1])
    dtb_d = din("dt_bias", [24, 1])
    alog_d = din("a_log", [1, 24])
    dsk_d = din("d_skip", [1, 12])
    gssd_d = din("g_ssd", [768, 1])
    wout_d = din("w_out", [D, D])
    wg_d = din("w_gate", [D, DFF])
    wu_d = din("w_up", [D, DFF])
    wd_d = din("w_down", [DFF, D])
    dftc_d = din("dftc", [T, T], BF16)
    dfts_d = din("dfts", [T, T], BF16)
    cs64_d = din("cs64", [256, 512], BF16)
    out_d = nc.dram_tensor("out", [T, D], F32, kind="ExternalOutput").ap()
    dbg_d = {}
    for name, shape, dt_ in debug:
        dbg_d[name] = nc.dram_tensor("dbg_" + name, list(shape), dt_, kind="ExternalOutput").ap()

    with ExitStack() as st:
        ARENA = 212480
        arena = st.enter_context(nc.sbuf_tensor("arena", [128, ARENA], U8))
        M = Mem(arena, ARENA)
        banks = [st.enter_context(nc.psum_tensor("ps%d" % i, [128, 512], F32)) for i in range(8)]
        S = Sched(nc, st)

        def PB(i):
            return banks[i][:, :]

        def PBH(i):
            return banks[i][:, :].bitcast(BF16)

        def dump(name, ap, rkeys):
            if name in dbg_d:
                tgt = dbg_d[name]
                S.add('sp', lambda e: e.dma_start(out=tgt, in_=ap), reads=rkeys, writes=[('dbg', name)], dma=('dbg', name))

        def finalize():
            fin = [('dbg', n) for n in dbg_d] + [('out', i) for i in range(16)]
            S.add('sp', lambda e: e.nop(), reads=fin)
            S.emit()

        ident = M.alloc([128], BF16)
        ident32 = M.alloc([128], F32)
        triF = M.alloc([128], F32)
        triB = M.alloc([128], F32)
        ones32 = M.alloc([128], F32)
        maskF3 = M.alloc([3, 128], BF16)
        maskB3 = M.alloc([3, 128], BF16)
        Sel = M.alloc([24, 128], BF16)
        Dsk = M.alloc([12, 128], BF16)
        epsb = M.alloc([1], F32)
        off_abc = M.top
        a_bc_x = M.alloc([D], F32)
        a_bc_c = M.alloc([D], F32)
        a2_bc = M.alloc([D], F32)
        G1 = M.alloc([D], F32)
        off_g1e = M.top
        G2 = M.alloc([D], F32)
        smcol = M.alloc([8, 2], F32)
        sfcol = M.alloc([8, 2], F32)
        convw_col = M.alloc([14, 7], F32)
        convb_col = M.alloc([14], F32)
        dtb_col = M.alloc([1], F32)
        gssd_col = M.alloc([6], F32)
        nA_bc = M.alloc([24], F32)
        dsk_bc = M.alloc([12], F32)
        rstd0 = M.alloc([NCH], F32)
        ssq0 = M.alloc([NCH], F32)
        base_top = M.top

        pool_c = 'pool'
        S.add(pool_c, lambda e: e.memset(epsb, EPS), writes=['epsb'])
        S.add(pool_c, lambda e: e.memset(ident, 0.0), writes=['ident'])
        S.add(pool_c, lambda e: e.affine_select(out=ident, in_=ident, compare_op=ALU.not_equal, fill=1.0, base=0,
                                                pattern=[[-1, 128]], channel_multiplier=1), reads=['ident'], writes=['ident'])
        S.add(pool_c, lambda e: e.memset(ident32, 0.0), writes=['ident32'])
        S.add(pool_c, lambda e: e.affine_select(out=ident32, in_=ident32, compare_op=ALU.not_equal, fill=1.0, base=0,
                                                pattern=[[-1, 128]], channel_multiplier=1), reads=['ident32'], writes=['ident32'])
        S.add(pool_c, lambda e: e.memset(ones32, 1.0), writes=['ones32'])
        S.add(pool_c, lambda e: e.memset(triF, 1.0), writes=['triF'])
        S.add(pool_c, lambda e: e.affine_select(out=triF, in_=triF, compare_op=ALU.is_ge, fill=0.0, base=0,
                                                pattern=[[1, 128]], channel_multiplier=-1), reads=['triF'], writes=['triF'])
        S.add(pool_c, lambda e: e.memset(triB, 1.0), writes=['triB'])
        S.add(pool_c, lambda e: e.affine_select(out=triB, in_=triB, compare_op=ALU.is_ge, fill=0.0, base=0,
                                                pattern=[[-1, 128]], channel_multiplier=1), reads=['triB'], writes=['triB'])
        S.add(pool_c, lambda e: e.memset(maskF3, 0.0), writes=['maskF3'])
        S.add(pool_c, lambda e: e.affine_select(out=maskF3, in_=maskF3, compare_op=ALU.is_ge, fill=-30000.0, base=0,
                                                pattern=[[0, 3], [1, 128]], channel_multiplier=-1), reads=['maskF3'], writes=['maskF3'])
        S.add(pool_c, lambda e: e.memset(maskB3, 0.0), writes=['maskB3'])
        S.add(pool_c, lambda e: e.affine_select(out=maskB3, in_=maskB3, compare_op=ALU.is_ge, fill=-30000.0, base=0,
                                                pattern=[[0, 3], [-1, 128]], channel_multiplier=1), reads=['maskB3'], writes=['maskB3'])
        S.add(pool_c, lambda e: e.memset(Sel, 0.0), writes=['Sel'])
        S.add(pool_c, lambda e: e.affine_select(out=Sel, in_=Sel, compare_op=ALU.not_equal, fill=1.0, base=0,
                                                pattern=[[-1, 24], [0, 128]], channel_multiplier=1), reads=['Sel'], writes=['Sel'])
        S.add(pool_c, lambda e: e.affine_select(out=Sel, in_=Sel, compare_op=ALU.not_equal, fill=1.0, base=-24,
                                                pattern=[[-1, 24], [0, 128]], channel_multiplier=1), reads=['Sel'], writes=['Sel'])

        S.add('sp', lambda e: e.dma_start(out=convw_col, in_=convw_d.rearrange("(c p) k -> p c k", p=128)),
              writes=['convw_col'], dma='convw_col')
        S.add('sp', lambda e: e.dma_start(out=convb_col.rearrange("p (c o) -> p c o", o=1),
                                          in_=convb_d.rearrange("(c p) o -> p c o", p=128), allow_slow_non_contiguous=True),
              writes=['convb_col'], dma='convb_col')
        S.add('sp', lambda e: e.dma_start(out=dtb_col[0:24, :], in_=dtb_d), writes=['dtb_col'], dma='dtb_col')
        S.add('sp', lambda e: e.dma_start(out=gssd_col.rearrange("p (c o) -> p c o", o=1),
                                          in_=gssd_d.rearrange("(c p) o -> p c o", p=128), allow_slow_non_contiguous=True),
              writes=['gssd_col'], dma='gssd_col')
        S.add('sp', lambda e: e.dma_start(out=nA_bc, in_=alog_d.broadcast_to([128, 24])), writes=['nA_bc'], dma='nA_bc')
        S.add('sp', lambda e: e.dma_start(out=dsk_bc, in_=dsk_d.broadcast_to([128, 12])), writes=['dsk_bc'], dma='dsk_bc')

        if stage == 0.5:
            dump('a_bc_x', G1, ['Dsk', 'Sel', 'maskF3', 'maskB3', 'triF', 'triB', 'convw_col', 'convb_col', 'dtb_col', 'gssd_col', 'nA_bc'])
            finalize()
            return nc
        mark_a = M.top
        M.top = mark_a + (8 * T + 8 * CT) * 2
        c2t = M.alloc([D], F32)
        sc32 = M.alloc([D], F32)
        scT = M.alloc([8, 2], BF16)
        mrow = M.alloc([2 * D], F32)
        bada2 = M.alloc([2 * D], F32)
        wa = [M.alloc([8, 512], BF16) for _ in range(4)]
        xall = M.alloc([NCH, D], F32)
        xnp = [M.alloc([D], BF16) for _ in range(2)]
        sel0 = M.alloc([128], F32)
        sel1 = M.alloc([128], F32)

        S.add('sp', lambda e: e.dma_start(out=c2t[0:2, :], in_=c2_d), writes=['c2t'], dma='c2t')
        S.add('sp', lambda e: e.dma_start(out=bada2[0:2, :], in_=bada_d[0:1, 0:2 * D].broadcast_to([2, 2 * D])), writes=['bada2'], dma='bada2')
        S.add('pool', lambda e: e.memset(sel0[0:2, :], 1.0), writes=['sel0'])
        S.add('pool', lambda e: e.affine_select(out=sel0[0:2, :], in_=sel0[0:2, :], compare_op=ALU.is_equal, fill=0.0, base=0,
                                                pattern=[[0, 128]], channel_multiplier=1), reads=['sel0'], writes=['sel0'])
        S.add('pool', lambda e: e.memset(sel1[0:2, :], 1.0), writes=['sel1'])
        S.add('pool', lambda e: e.affine_select(out=sel1[0:2, :], in_=sel1[0:2, :], compare_op=ALU.not_equal, fill=0.0, base=0,
                                                pattern=[[0, 128]], channel_multiplier=1), reads=['sel1'], writes=['sel1'])
        S.add('act', lambda e: e.activation(out=nA_bc, in_=nA_bc, func=AF.Exp), reads=['nA_bc'], writes=['nA_bc'])
        S.add('dve', lambda e: e.tensor_scalar(out=nA_bc, in0=nA_bc, scalar1=-1.0, scalar2=None, op0=ALU.mult),
              reads=['nA_bc'], writes=['nA_bc'])
        S.add('act', lambda e: e.activation(out=sc32[0:2, :], in_=c2t[0:2, :], func=AF.Silu), reads=['c2t'], writes=['sc32'])
        for j in range(8):
            S.add('pe', lambda e, j=j: e.matmul(PB(7)[:, 2 * j:2 * j + 2], lhsT=sc32[0:2, j * 128:(j + 1) * 128],
                                                rhs=ident32[0:2, 0:2], start=True, stop=True),
                  reads=['sc32', 'ident32'], writes=[('ps', 7)])
        S.add('dve', lambda e: e.tensor_copy(out=scT.rearrange("p a b -> p (a b)"), in_=PB(7)[:, 0:16]),
              reads=[('ps', 7)], writes=['scT'])
        screp = Dsk[:, 0:8, :]
        S.add('dve', lambda e: e.tensor_copy(out=screp, in_=scT[:, :, 0:1].to_broadcast([128, 8, 128])), reads=['scT'], writes=['Dsk'])
        if stage == 0.7:
            dump('a_bc_x', G1, ['scT', 'sel0', 'sel1', 'nA_bc'])
            finalize()
            return nc
        wada_v = wada_d.rearrange("(k p) n -> p k n", p=128)
        for t in range(4):
            sl = t % 4
            S.add('pool', lambda e, t=t, sl=sl: e.dma_start(out=wa[sl], in_=wada_v[:, :, t * 512:(t + 1) * 512]),
                  writes=[('wa', sl)], dma=('wa', sl))
            pb = t % 4
            for k in range(8):
                S.add('pe', lambda e, k=k, sl=sl, pb=pb: e.matmul(PB(pb)[0:2, :], lhsT=scT[:, k, :], rhs=wa[sl][:, k, :],
                                                                  start=(k == 0), stop=(k == 7)),
                      reads=['scT', ('wa', sl)], writes=[('ps', pb)])
            S.add('dve', lambda e, t=t, pb=pb: e.tensor_tensor(out=mrow[0:2, t * 512:(t + 1) * 512], in0=PB(pb)[0:2, :],
                                                               in1=bada2[0:2, t * 512:(t + 1) * 512], op=ALU.add),
                  reads=[('ps', pb), 'bada2'], writes=[('mrow', t)])

        if stage == 0.8:
            dump('a_bc_x', mrow[:, 0:1024], [('mrow', t) for t in range(4)])
            finalize()
            return nc

        def mrow_keys(v):
            return [('mrow', 2 * v), ('mrow', 2 * v + 1)]

        acol = a_bc_x[:, 0:16].rearrange("p (a b) -> p a b", a=8)
        gpmcol = a_bc_c[:, 0:8]
        S.add('sp', lambda e: e.dma_start(out=gpmcol, in_=gpm_d[0:1, :].rearrange("o (j p) -> p (o j)", p=128),
                                          allow_slow_non_contiguous=True), writes=['gpmcol'], dma='gpmcol')
        for v in (0, 1):
            for j in range(8):
                S.add('pe', lambda e, v=v, j=j: e.matmul(PB(7)[:, 2 * j:2 * j + 2],
                                                         lhsT=mrow[0:2, v * D + j * 128:v * D + (j + 1) * 128],
                                                         rhs=ident32[0:2, 0:2], start=True, stop=True),
                      reads=mrow_keys(v) + ['ident32'], writes=[('ps', 7)])
            if v == 0:
                S.add('dve', lambda e: e.tensor_copy(out=smcol.rearrange("p a b -> p (a b)"), in_=PB(7)[:, 0:16]),
                      reads=[('ps', 7)], writes=[('col', 0)])
            else:
                S.add('dve', lambda e: e.scalar_tensor_tensor(
                    out=acol, in0=PB(7)[:, 0:16].rearrange("p (a b) -> p a b", a=8), scalar=1.0, in1=bc_last(gpmcol, 2),
                    op0=ALU.add, op1=ALU.mult), reads=[('ps', 7), 'gpmcol'], writes=['acol'])
        dump('a_bc_x', a_bc_x, ['a_bc_x'])

        if stage == 1:
            finalize()
            return nc
        M.top = mark_a

        off_r1 = M.top
        hxT = M.alloc([8, T], BF16)
        hcT = M.alloc([8, CT], BF16)
        off_xbcX = M.top
        xbcX = M.alloc([6, T], BF16)
        off_uT = M.top
        uT = M.alloc([2, T], BF16)
        off_zcm = M.top
        zcm = M.alloc([6, T], BF16)
        xbcBC = M.alloc([8, T], BF16)
        off_xbcc = M.top
        xbcc = M.alloc([14, CT], BF16)
        dtr = M.alloc([T + CT], F32)
        mark_c = M.top
        off_xt = M.top
        xt = [M.alloc([D], F32) for _ in range(4)]
        off_xs = M.top
        xs = [M.alloc([D], BF16) for _ in range(2)]
        off_xs_end = M.top
        wst = [M.alloc([8, 128], BF16) for _ in range(3)]
        dg = [M.alloc([7, 128], BF16) for _ in range(2)]
        padl = [M.alloc([8, 70], BF16) for _ in range(2)]
        padc = M.alloc([CT + 6], BF16)

        def src_tile(i):
            return ctx_d[i * 128:(i + 1) * 128, :] if i < 2 else x_d[(i - 2) * 128:(i - 1) * 128, :]

        xops = []
        for i in range(NCH):
            xops.append(S.add('sp', lambda e, i=i: e.dma_start(out=xall[:, i, :], in_=src_tile(i)), writes=[('xall', i)],
                              dma=('xall', i // 3)))
            if i % 3 == 2:
                for o in xops[-3:]:
                    o.dval = xops[-1].dval
        for i in range(NCH):
            S.add('act', lambda e, i=i: e.activation(out=xnp[i % 2], in_=xall[:, i, :], func=AF.Square, accum_out=ssq0[:, i:i + 1]),
                  reads=[('xall', i)], writes=[('xnp', i % 2), ('ssq0', i)])
        S.add('act', lambda e: e.activation(out=rstd0, in_=ssq0, func=AF.Ln, scale=1.0 / D, bias=epsb),
              reads=[('ssq0', i) for i in range(NCH)] + ['epsb'], writes=['rstd0'])
        S.add('act', lambda e: e.activation(out=rstd0, in_=rstd0, func=AF.Exp, scale=-0.5), reads=['rstd0'], writes=['rstd0'])
        dump('rstd0', rstd0, ['rstd0'])
        if stage == 1.2:
            finalize()
            return nc
        groups = [(0, 2)] + [(2 + 4 * g, 4) for g in range(4)]
        for gidx, (t0, nt) in enumerate(groups):
            if stage == 1.3 and gidx >= 1:
                break
            for ii in range(nt):
                i = t0 + ii
                sl = i % 2
                S.add('dve', lambda e, i=i, sl=sl: e.tensor_scalar(
                    out=xnp[sl], in0=xall[:, i, :], scalar1=rstd0[:, i:i + 1], scalar2=None, op0=ALU.mult),
                      reads=[('xall', i), 'rstd0'], writes=[('xnp', sl)])
                for j in range(8):
                    pb = 4 + j // 2
                    col = (j % 2) * 512 + ii * 128
                    S.add('pe', lambda e, j=j, sl=sl, pb=pb, col=col: e.transpose(
                        out=PBH(pb)[:, col:col + 128], in_=xnp[sl][:, j * 128:(j + 1) * 128], identity=ident),
                          reads=[('xnp', sl), 'ident'], writes=[('ps', pb)])
            for j in range(8):
                pb = 4 + j // 2
                c0 = (j % 2) * 512
                cix = 1 if gidx == 0 else 0
                if gidx == 0:
                    dst = hcT[:, j, :]
                    dkey = ('hcT', j)
                else:
                    dst = hxT[:, j, (gidx - 1) * 512:gidx * 512]
                    dkey = ('hxT', j, gidx - 1)
                if pb % 2 == 0:
                    S.add('act', lambda e, pb=pb, c0=c0, nt=nt, dst=dst, j=j, cix=cix: e.activation(
                        out=dst, in_=PBH(pb)[:, c0:c0 + nt * 128], func=AF.Identity, scale=acol[:, j, cix:cix + 1],
                        bias=smcol[:, j, cix:cix + 1]), reads=[('ps', pb), ('col', 0), 'acol'], writes=[dkey])
                else:
                    S.add('dve', lambda e, pb=pb, c0=c0, nt=nt, dst=dst, j=j, cix=cix: e.tensor_scalar(
                        out=dst, in0=PBH(pb)[:, c0:c0 + nt * 128], scalar1=acol[:, j, cix:cix + 1], scalar2=smcol[:, j, cix:cix + 1],
                        op0=ALU.mult, op1=ALU.add), reads=[('ps', pb), ('col', 0), 'acol'], writes=[dkey])
        if stage != 1.3:
            import os
            _dj = int(os.environ.get('DJ', '0')); _dg = int(os.environ.get('DG', '0'))
            dump('hxT', hxT[:, _dj, _dg * 512:(_dg + 1) * 512], [('hxT', _dj, _dg)])

        if stage in (1.5, 1.3):
            finalize()
            return nc
        cmax = {1.6: 2, 1.7: 8, 1.8: 22}.get(stage, 23)
        S.barrier()
        for p_ in padl:
            S.add('pool', lambda e, p_=p_: e.memset(p_, 0.0), writes=[('pad', id(p_))])
        S.add('pool', lambda e: e.memset(padc, 0.0), writes=[('padc',)])

        win_v = win_d.rearrange("(k p) n -> p k n", p=128)
        items = []
        for c in range(cmax):
            for gidx, (t0, nt) in enumerate(groups):
                if gidx == 0 and c < 8:
                    continue
                items.append((c, gidx, nt))
        seen_c = set()

        def c_mm(n):
            c, gidx, nt = items[n]
            col0 = 128 * c
            wc = 128 if c < 22 else 24
            sl = c % 3
            dsl = c % 2
            cc = c - 8
            if c not in seen_c:
                seen_c.add(c)
                S.add('pool', lambda e, sl=sl, col0=col0, wc=wc: e.dma_start(out=wst[sl][:, :, 0:wc], in_=win_v[:, :, col0:col0 + wc]),
                      writes=[('wst', sl)], dma=('wst', sl))
                if 8 <= c < 22:
                    for k in range(7):
                        S.add('dve', lambda e, k=k, dsl=dsl, cc=cc: e.tensor_scalar(
                            out=dg[dsl][:, k, :], in0=ident, scalar1=convw_col[:, cc, k:k + 1], scalar2=None, op0=ALU.mult),
                              reads=['ident', 'convw_col'], writes=[('dg', dsl)])
            ntok = nt * 128
            if gidx == 0:
                rhs_of = lambda k: hcT[:, k, :]
                rkeys = [('hcT', k) for k in range(8)]
            else:
                rhs_of = lambda k, g=gidx - 1: hxT[:, k, g * 512:(g + 1) * 512]
                rkeys = [('hxT', k, gidx - 1) for k in range(8)]
            pa = n % 2
            for k in range(8):
                S.add('pe', lambda e, k=k, pa=pa, sl=sl, wc=wc, ntok=ntok, rhs_of=rhs_of: e.matmul(
                    PB(pa)[0:wc, 0:ntok], lhsT=wst[sl][:, k, 0:wc], rhs=rhs_of(k), start=(k == 0), stop=(k == 7)),
                      reads=[('wst', sl)] + rkeys, writes=[('ps', pa)])

        def c_post(n):
            c, gidx, nt = items[n]
            ntok = nt * 128
            pa = n % 2
            conv = 8 <= c < 22
            cc = c - 8
            dsl = c % 2
            tok0 = (gidx - 1) * 512
            if c < 2:
                S.add('dve', lambda e, pa=pa, c=c, tok0=tok0: e.tensor_copy(out=uT[:, c, tok0:tok0 + 512], in_=PB(pa)),
                      reads=[('ps', pa)], writes=[('uT', c, gidx)])
            elif c < 8:
                S.add('act', lambda e, pa=pa, c=c, tok0=tok0: e.activation(out=zcm[:, c - 2, tok0:tok0 + 512], in_=PB(pa),
                                                                            func=AF.Silu),
                      reads=[('ps', pa)], writes=[('zcm', c - 2, gidx)])
            elif conv:
                pbb = 2 + pa
                if gidx == 0:
                    S.add('dve', lambda e, pa=pa: e.tensor_copy(out=padc[:, 3:3 + CT], in_=PB(pa)[:, 0:CT]),
                          reads=[('ps', pa)], writes=[('padc',)])
                    for k in range(7):
                        S.add('pe', lambda e, k=k, pbb=pbb, dsl=dsl: e.matmul(
                            PB(pbb)[:, 0:CT], lhsT=dg[dsl][:, k, :], rhs=padc[:, k:k + CT], start=(k == 0), stop=(k == 6)),
                              reads=[('dg', dsl), ('padc',)], writes=[('ps', pbb)])
                    S.add('act', lambda e, pbb=pbb, cc=cc: e.activation(out=xbcc[:, cc, :], in_=PB(pbb)[:, 0:CT], func=AF.Silu,
                                                                        bias=convb_col[:, cc:cc + 1]),
                          reads=[('ps', pbb), 'convb_col'], writes=[('xbcc', cc)])
                else:
                    pl = padl[n % 2]
                    pkey = ('pad', id(pl))
                    S.add('dve', lambda e, pa=pa, pl=pl: e.tensor_copy(
                        out=pl[:, :, 3:67], in_=PB(pa).rearrange("p (r w) -> p r w", r=8)),
                          reads=[('ps', pa)], writes=[pkey])
                    for k in range(7):
                        S.add('pe', lambda e, k=k, pbb=pbb, dsl=dsl, pl=pl: e.matmul(
                            PB(pbb).rearrange("p (r w) -> p r w", r=8), lhsT=dg[dsl][:, k, :], rhs=pl[:, :, k:k + 64],
                            start=(k == 0), stop=(k == 6)),
                              reads=[('dg', dsl), pkey], writes=[('ps', pbb)])
                    S.add('act', lambda e, pbb=pbb, cc=cc, tok0=tok0: e.activation(
                        out=(xbcX[:, cc, tok0:tok0 + 512] if cc < 6 else xbcBC[:, cc - 6, tok0:tok0 + 512]),
                        in_=PB(pbb), func=AF.Silu, bias=convb_col[:, cc:cc + 1]),
                          reads=[('ps', pbb), 'convb_col'], writes=[('xbc', cc, gidx - 1)])
            else:
                d0 = 0 if gidx == 0 else CT + tok0
                S.add('act', lambda e, pa=pa, d0=d0, ntok=ntok: e.activation(
                    out=dtr[0:24, d0:d0 + ntok], in_=PB(pa)[0:24, 0:ntok], func=AF.Identity, bias=dtb_col[0:24, :]),
                      reads=[('ps', pa), 'dtb_col'], writes=[('dtr', gidx)])

        MW = Mem(arena, off_xs, base=off_xt)
        wa2 = [MW.alloc([8, 512], BF16) for _ in range(2)]
        MB = Mem(arena, off_xs_end, base=off_xs)
        btile = MB.alloc([512], F32)
        gtile = MB.alloc([512], F32)
        ada_dst = {2: (G1, gqm_d, False, 'G1'), 4: (a2_bc, gpf_d, True, 'a2_bc'), 5: (G2, gqf_d, False, 'G2')}

        def ada_dma(j):
            t = 4 + j
            sl = j % 2
            S.add('pool', lambda e, t=t, sl=sl: e.dma_start(out=wa2[sl], in_=wada_v[:, :, t * 512:(t + 1) * 512]),
                  writes=[('xt', 2 * sl), ('xt', 2 * sl + 1), ('wa2', sl)], dma=('wa2', sl))

        def ada_mm(j):
            t = 4 + j
            v, half = t // 2, t % 2
            sl = j % 2
            bk = 4 + j % 2
            if v == 3:
                S.add('sp', lambda e, t=t: e.dma_start(out=btile[:, 0:4], in_=bada_d[0:1, t * 512:(t + 1) * 512].rearrange(
                    "o (j p) -> p (o j)", p=128), allow_slow_non_contiguous=True), writes=[('xs', 0), 'btile'], dma='btile')
                for n4 in range(4):
                    for k in range(8):
                        S.add('pe', lambda e, n4=n4, k=k, sl=sl, bk=bk: e.matmul(
                            PB(bk)[:, n4:n4 + 1], lhsT=wa2[sl][:, k, n4 * 128:(n4 + 1) * 128], rhs=screp[:, k, 0:1],
                            start=(k == 0), stop=(k == 7)), reads=[('wa2', sl)], writes=[('ps', bk)])
                S.add('dve', lambda e, half=half, bk=bk: e.tensor_tensor(out=sfcol[:, half * 4:(half + 1) * 4, 0], in0=PB(bk)[:, 0:4],
                                                                         in1=btile[:, 0:4], op=ALU.add),
                      reads=[('ps', bk), 'btile'], writes=[('sfcol', half)])
                return
            dst, gain_d, plus1, key = ada_dst[v]
            S.add('sp', lambda e, t=t: e.dma_start(out=btile, in_=bada_d[0:1, t * 512:(t + 1) * 512].broadcast_to([128, 512])),
                  writes=[('xs', 0), 'btile'], dma='btile')
            S.add('sp', lambda e, half=half, gain_d=gain_d: e.dma_start(
                out=gtile, in_=gain_d[0:1, half * 512:(half + 1) * 512].broadcast_to([128, 512])),
                  writes=[('xs', 1), 'gtile'], dma='gtile')
            for k in range(8):
                S.add('pe', lambda e, k=k, sl=sl, bk=bk: e.matmul(PB(bk), lhsT=screp[:, k, :], rhs=wa2[sl][:, k, :],
                                                                  start=(k == 0), stop=(k == 7)),
                      reads=[('wa2', sl)], writes=[('ps', bk)])
            dh = dst[:, half * 512:(half + 1) * 512]
            if plus1:
                S.add('dve', lambda e, dh=dh, bk=bk: e.scalar_tensor_tensor(out=dh, in0=PB(bk), scalar=1.0, in1=btile,
                                                                           op0=ALU.add, op1=ALU.add),
                      reads=[('ps', bk), 'btile'], writes=[(key, half)])
            else:
                S.add('dve', lambda e, dh=dh, bk=bk: e.tensor_tensor(out=dh, in0=PB(bk), in1=btile, op=ALU.add),
                      reads=[('ps', bk), 'btile'], writes=[(key, half)])
            S.add('dve', lambda e, dh=dh: e.tensor_tensor(out=dh, in0=dh, in1=gtile, op=ALU.mult),
                  reads=[(key, half), 'gtile'], writes=[(key, half)])

        ada_at = {}
        if cmax == 23:
            for j in range(8):
                ada_at[2 + 12 * j] = ('dma', j)
                ada_at[2 + 12 * j + 8] = ('mm', j)
        if items:
            c_mm(0)
        for n in range(len(items)):
            if n + 1 < len(items):
                c_mm(n + 1)
            c_post(n)
            if n in ada_at:
                kind, j = ada_at[n]
                (ada_dma if kind == 'dma' else ada_mm)(j)
        dump('G1', G1, [('G1', 0), ('G1', 1)])
        dump('uT', uT[:, 0, 0:512], [('uT', 0, 1)])
        dump('zcm', zcm[:, 0, 0:512], [('zcm', 0, 1)])
        dump('xbc', xbcX[:, 0, 0:512], [('xbc', 0, 0)])
        dump('xbcc', xbcc[:, 13, :], [('xbcc', 13)])
        dump('dtr', dtr[0:24, 0:512], [('dtr', 0), ('dtr', 1)])

        if stage == 1.9:
            finalize()
            return nc

        def evac(bank, out_ap, in_ap, rkeys, wkeys, bias=None):
            if bank % 2 == 0:
                if bias is None:
                    S.add('act', lambda e: e.activation(out=out_ap, in_=in_ap, func=AF.Identity), reads=rkeys, writes=wkeys)
                else:
                    S.add('act', lambda e: e.activation(out=out_ap, in_=in_ap, func=AF.Identity, bias=bias), reads=rkeys, writes=wkeys)
            else:
                if bias is None:
                    S.add('dve', lambda e: e.tensor_copy(out=out_ap, in_=in_ap), reads=rkeys, writes=wkeys)
                else:
                    S.add('dve', lambda e: e.tensor_scalar(out=out_ap, in0=in_ap, scalar1=bias, scalar2=None, op0=ALU.add),
                          reads=rkeys, writes=wkeys)

        S.barrier()
        ME = Mem(arena, ARENA, base=mark_c)
        A_tok = ME.alloc([16, 512], BF16)
        cs64t = ME.alloc([2, 512], BF16)
        dft = [ME.alloc([2, 1024], BF16) for _ in range(3)]
        MR1 = Mem(arena, off_xbcX, base=off_r1)
        YfT = MR1.alloc([2, T], BF16)
        x_tok = MR1.alloc([NCH, 768], BF16)
        S.add('sp', lambda e: e.dma_start(out=cs64t, in_=cs64_d.rearrange("(k p) n -> p k n", p=128)), writes=['cs64t'], dma='cs64t')
        def a_tok(i):
            pb = i % 4
            for kc in range(2):
                S.add('pe', lambda e, i=i, kc=kc, pb=pb: e.matmul(PB(pb), lhsT=uT[:, kc, i * 128:(i + 1) * 128], rhs=cs64t[:, kc, :],
                                                                  start=(kc == 0), stop=(kc == 1)),
                      reads=['cs64t'], writes=[('ps', pb)])
            evac(pb, A_tok[:, i, :], PB(pb), [('ps', pb)], [('A_tok', i)])

        a_tok(0)
        a_tok(1)
        for kh in range(2):
            for i in range(16):
                if kh == 0 and i + 2 < 16:
                    a_tok(i + 2)
                sl = (kh * 16 + i) % 3
                S.add('sp', lambda e, i=i, sl=sl, kh=kh: e.dma_start(out=dft[sl][:, 0, :],
                                                                   in_=dftc_d[i * 128:(i + 1) * 128, kh * 1024:(kh + 1) * 1024]),
                      writes=[('dftc', sl)], dma=('dftc', sl))
                S.add('sp', lambda e, i=i, sl=sl, kh=kh: e.dma_start(out=dft[sl][:, 1, :],
                                                                   in_=dfts_d[i * 128:(i + 1) * 128, kh * 1024:(kh + 1) * 1024]),
                      writes=[('dfts', sl)], dma=('dfts', sl))
                for jc in range(2):
                    for kt in range(2):
                        bank = 4 + jc * 2 + kt
                        S.add('pe', lambda e, i=i, sl=sl, jc=jc, kt=kt, bank=bank: e.matmul(
                            PB(bank), lhsT=A_tok[:, i, jc * 128:(jc + 1) * 128], rhs=dft[sl][:, 0, kt * 512:(kt + 1) * 512],
                            start=(i == 0), stop=False), reads=[('A_tok', i), ('dftc', sl)], writes=[('ps', bank)])
                        S.add('pe', lambda e, i=i, sl=sl, jc=jc, kt=kt, bank=bank: e.matmul(
                            PB(bank), lhsT=A_tok[:, i, 256 + jc * 128:256 + (jc + 1) * 128], rhs=dft[sl][:, 1, kt * 512:(kt + 1) * 512],
                            start=False, stop=(i == 15)), reads=[('A_tok', i), ('dfts', sl)], writes=[('ps', bank)])
            for jc in range(2):
                for kt in range(2):
                    bank = 4 + jc * 2 + kt
                    c0 = kh * 1024 + kt * 512
                    evac(bank, YfT[:, jc, c0:c0 + 512], PB(bank), [('ps', bank)], [('YfT', jc, kh, kt)])
        dump('YfT', YfT[:, 0, 0:512], [('YfT', 0, 0, 0)])
        if stage == 2:
            finalize()
            return nc

        S.barrier()
        MD = Mem(arena, ARENA, base=mark_c)
        ea = MD.alloc([NCH, 24], F32)
        wstt = MD.alloc([NCH, 24], F32)
        eaend = MD.alloc([NCH, 24], F32)
        acsT_hl = MD.alloc([NCH * 128], BF16)
        uT_hl = MD.alloc([NCH * 128], BF16)
        mark_f = MD.top
        dtl = MD.alloc([NCH, 24], F32)
        dA = MD.alloc([NCH, 24], F32)
        lndt = MD.alloc([NCH, 24], F32)
        acs = MD.alloc([NCH, 24], F32)
        uu = MD.alloc([NCH, 24], F32)
        tmpw = MD.alloc([NCH, 24], F32)
        hlA = MD.alloc([NCH, 48], BF16)
        hlU = MD.alloc([NCH, 48], BF16)
        fl = lambda t: t.rearrange("p a b -> p (a b)")
        NW = NCH * 24
        for cc in range(NCH):
            S.add('pe', lambda e, cc=cc: e.matmul(PB(0)[:, cc * 24:(cc + 1) * 24], lhsT=dtr[0:24, cc * 128:(cc + 1) * 128],
                                                  rhs=ident32[0:24, 0:24], start=True, stop=True), reads=[], writes=[('ps', 0)])
        S.add('act', lambda e: e.activation(out=fl(dtl), in_=PB(0)[:, 0:NW], func=AF.Identity), reads=[('ps', 0)], writes=['dtl'])
        S.add('act', lambda e: e.activation(out=fl(tmpw), in_=fl(dtl), func=AF.Exp), reads=['dtl'], writes=['tmpw'])
        S.add('act', lambda e: e.activation(out=fl(dtl), in_=fl(tmpw), func=AF.Ln, bias=1.0), reads=['tmpw'], writes=['dtl'])
        S.add('act', lambda e: e.activation(out=fl(lndt), in_=fl(dtl), func=AF.Ln), reads=['dtl'], writes=['lndt'])
        S.add('dve', lambda e: e.tensor_tensor(out=dA, in0=dtl, in1=bc_mid(nA_bc, NCH), op=ALU.mult), reads=['dtl'], writes=['dA'])
        acs_ps = PB(1)[:, 0:NW].rearrange("p (c h) -> p c h", c=NCH)
        for c0 in range(0, NCH, 9):
            S.add('pe', lambda e, c0=c0: e.matmul(acs_ps[:, c0:c0 + 9, 0:12], lhsT=triF, rhs=dA[:, c0:c0 + 9, 0:12], start=True, stop=True),
                  reads=['dA'], writes=[('ps', 1)])
            S.add('pe', lambda e, c0=c0: e.matmul(acs_ps[:, c0:c0 + 9, 12:24], lhsT=triB, rhs=dA[:, c0:c0 + 9, 12:24], start=True, stop=True),
                  reads=['dA'], writes=[('ps', 1)])
        for c0 in range(0, NCH, 4):
            n = min(4, NCH - c0)
            S.add('pe', lambda e, c0=c0, n=n: e.matmul(PB(2)[:, c0 * 24:(c0 + n) * 24], lhsT=ones32,
                                                       rhs=dA[:, c0:c0 + n, :].rearrange("p c h -> p (c h)"), start=True, stop=True),
                  reads=['dA'], writes=[('ps', 2)])
        S.add('dve', lambda e: e.tensor_copy(out=fl(acs), in_=PB(1)[:, 0:NW]), reads=[('ps', 1)], writes=['acs'])
        S.add('act', lambda e: e.activation(out=fl(ea), in_=fl(acs), func=AF.Exp), reads=['acs'], writes=['ea'])
        S.add('act', lambda e: e.activation(out=fl(eaend), in_=PB(2)[:, 0:NW], func=AF.Exp), reads=[('ps', 2)], writes=['eaend'])
        S.add('dve', lambda e: e.tensor_tensor(out=fl(uu), in0=fl(lndt), in1=fl(acs), op=ALU.subtract), reads=['lndt', 'acs'], writes=['uu'])
        S.add('dve', lambda e: e.tensor_tensor(out=fl(tmpw), in0=PB(2)[:, 0:NW], in1=fl(uu), op=ALU.add), reads=[('ps', 2), 'uu'], writes=['tmpw'])
        S.add('act', lambda e: e.activation(out=fl(wstt), in_=fl(tmpw), func=AF.Exp), reads=['tmpw'], writes=['wstt'])
        for src, hl, key in ((acs, hlA, 'hlA'), (uu, hlU, 'hlU')):
            S.add('dve', lambda e, src=src, hl=hl: e.tensor_copy(out=hl[:, :, 0:24], in_=src), reads=['acs', 'uu'], writes=[key])
            S.add('dve', lambda e, src=src, hl=hl: e.tensor_tensor(out=hl[:, :, 24:48], in0=src, in1=hl[:, :, 0:24], op=ALU.subtract),
                  reads=['acs', 'uu', key], writes=[key])
        for hl, dstT, key in ((hlA, acsT_hl, 'hlA'), (hlU, uT_hl, 'hlU')):
            for cc in range(NCH):
                bank = 3 + cc // 8
                col = (cc % 8) * 128
                S.add('pe', lambda e, hl=hl, cc=cc, bank=bank, col=col: e.transpose(out=PBH(bank)[0:48, col:col + 128], in_=hl[:, cc, :],
                                                                                   identity=ident), reads=[key], writes=[('ps', bank)])
            for bank in (3, 4, 5):
                c0 = (bank - 3) * 8
                n = min(8, NCH - c0)
                evac(bank, dstT[0:48, c0 * 128:(c0 + n) * 128], PBH(bank)[0:48, 0:n * 128], [('ps', bank)], [key + 'T'])
        for cc in range(NCH):
            bank = 6 + cc % 2
            for j in range(6):
                srcx = xbcc[:, j, cc * 128:(cc + 1) * 128] if cc < 2 else xbcX[:, j, (cc - 2) * 128:(cc - 1) * 128]
                S.add('pe', lambda e, j=j, srcx=srcx, bank=bank: e.transpose(out=PBH(bank)[:, j * 128:(j + 1) * 128], in_=srcx, identity=ident),
                      reads=[], writes=[('ps', bank)])
            evac(bank, x_tok[:, cc, :], PBH(bank)[:, 0:768], [('ps', bank)], [('x_tok', cc)])
        dump('ea', fl(ea), ['ea'])
        dump('wstt', fl(wstt), ['wstt'])
        dump('eaend', fl(eaend), ['eaend'])
        dump('acsT', acsT_hl[0:48, 0:512], ['hlAT'])
        dump('x_tok', x_tok[:, 2, :], [('x_tok', 2)])
        if stage == 3:
            finalize()
            return nc

        S.barrier()
        MF3 = Mem(arena, ARENA, base=mark_f)
        MF4 = Mem(arena, off_zcm, base=off_uT)
        MF5 = Mem(arena, mark_c, base=off_xbcc)
        MX = Mem(arena, off_uT, base=off_xbcX)
        hinB = MX.alloc([16, 768], BF16)
        t1 = MF3.alloc([768], F32)
        t2 = MF3.alloc([768], F32)
        yv = MF3.alloc([768], F32)
        hF = MF3.alloc([768], F32)
        hB = MF3.alloc([768], F32)
        tmpS = MF3.alloc([768], F32)
        E_s = [[MF5.alloc([3, 128], BF16) for _ in range(2)] for _ in range(2)]
        Es = [MF5.alloc([3, 128], BF16) for _ in range(2)]
        Wt = [MF5.alloc([3, 128], BF16) for _ in range(2)]
        xw = [MF5.alloc([768], BF16) for _ in range(2)]
        Btk = [MF5.alloc([512], BF16) for _ in range(2)]
        yn = MF5.alloc([768], BF16)
        junky = MF5.alloc([768], BF16)
        hF_bf = MF5.alloc([768], BF16)
        ssqy = MF4.alloc([16], F32)
        rstdy = MF4.alloc([16], F32)
        TR, SC, EF, EB, Y0, Y1, ST0 = 0, 1, 2, 3, 4, 5, 6
        hv = lambda t: t.rearrange("p (h d) -> p h d", h=12)
        for h in range(12):
            S.add('dve', lambda e, h=h: e.tensor_scalar(out=Dsk[:, h, :], in0=ident, scalar1=dsk_bc[:, h:h + 1], scalar2=None,
                                                        op0=ALU.mult), reads=[], writes=['Dsk'])
        S.add('pool', lambda e: e.memset(hF, 0.0), writes=['hF'])
        S.add('pool', lambda e: e.memset(hB, 0.0), writes=['hB'])
        cnt = [0]

        def st_a(cc, d):
            slot = cnt[0] % 2
            cnt[0] += 1
            for g in range(4):
                srcb = xbcc[:, 6 + g, cc * 128:(cc + 1) * 128] if cc < 2 else xbcBC[:, g, (cc - 2) * 128:(cc - 1) * 128]
                S.add('pe', lambda e, g=g, srcb=srcb: e.transpose(out=PBH(TR)[:, g * 128:(g + 1) * 128], in_=srcb, identity=ident),
                      reads=[], writes=[('ps', TR)])
            evac(TR, Btk[slot], PBH(TR)[:, 0:512], [('ps', TR)], [('Btk', slot)])
            S.add('pool', lambda e, slot=slot, cc=cc, d=d: e.tensor_tensor(
                out=hv(xw[slot]), in0=hv(x_tok[:, cc, :]), in1=bc_last(wstt[:, cc, 12 * d:12 * d + 12], 64), op=ALU.mult),
                  reads=[], writes=[('xw', slot)])
            return slot

        def st_mm(slot, b0):
            for g in range(4):
                bank = b0 + g // 2
                col = (g % 2) * 192
                S.add('pe', lambda e, g=g, bank=bank, col=col, slot=slot: e.matmul(
                    PB(bank)[:, col:col + 192], lhsT=Btk[slot][:, g * 128:(g + 1) * 128], rhs=xw[slot][:, g * 192:(g + 1) * 192],
                    start=True, stop=True), reads=[('Btk', slot), ('xw', slot)], writes=[('ps', bank)])

        def st_rec(cc, d, hst, hkey, b0, meng):
            S.add(meng, lambda e, cc=cc, d=d, hst=hst: e.tensor_tensor(
                out=hv(tmpS), in0=hv(hst), in1=bc_last(eaend[:, cc, 12 * d:12 * d + 12], 64), op=ALU.mult),
                  reads=[hkey], writes=['tmpS'])
            for half in range(2):
                S.add('dve', lambda e, half=half, hst=hst: e.tensor_tensor(
                    out=hst[:, half * 384:(half + 1) * 384], in0=PB(b0 + half)[:, 0:384], in1=tmpS[:, half * 384:(half + 1) * 384],
                    op=ALU.add), reads=[('ps', b0 + half), 'tmpS'], writes=[hkey])

        def st_b(cc, d, hst, hkey, slot, meng='pool'):
            st_mm(slot, ST0)
            st_rec(cc, d, hst, hkey, ST0, meng)

        def st_step(cc, d, hst, hkey, meng='pool'):
            st_b(cc, d, hst, hkey, st_a(cc, d), meng)

        order_b = [1, 0] + list(range(17, 1, -1))
        slot_n = st_a(order_b[0], 1)
        for n, cc in enumerate(order_b):
            slot_c = slot_n
            b0 = ST0 if n % 2 == 0 else Y0
            st_mm(slot_c, b0)
            if n + 1 < len(order_b):
                slot_n = st_a(order_b[n + 1], 1)
            if cc >= 2:
                S.add('act', lambda e, cc=cc: e.activation(out=hinB[:, cc - 2, :], in_=hB, func=AF.Identity),
                      reads=['hB'], writes=[('hinB', cc - 2)])
            st_rec(cc, 1, hB, 'hB', b0, 'dve')
        dump('hinB0', hinB[:, 0, :], [('hinB', 0)])
        if stage == 3.5:
            finalize()
            return nc

        def E_mm(cc, g):
            for d in range(2):
                bank = EF if d == 0 else EB
                mk = maskF3 if d == 0 else maskB3
                S.add('pe', lambda e, bank=bank, mk=mk: e.matmul(PB(bank)[:, 0:384], lhsT=ident, rhs=mk.rearrange("p a b -> p (a b)"),
                                                                 start=True, stop=False), reads=[], writes=[('ps', bank)])
                for hh in range(3):
                    hd = 12 * d + 3 * g + hh
                    S.add('pe', lambda e, bank=bank, hh=hh, hd=hd, cc=cc: e.matmul(
                        PB(bank)[:, hh * 128:(hh + 1) * 128], lhsT=Sel[0:48, hd, :], rhs=acsT_hl[0:48, cc * 128:(cc + 1) * 128],
                        start=False, stop=False), reads=[], writes=[('ps', bank)])
                    S.add('pe', lambda e, bank=bank, hh=hh, hd=hd, cc=cc: e.matmul(
                        PB(bank)[:, hh * 128:(hh + 1) * 128], lhsT=uT_hl[0:48, cc * 128:(cc + 1) * 128], rhs=Sel[0:48, hd, :],
                        start=False, stop=(hh == 2)), reads=[], writes=[('ps', bank)])

        def E_exp(g):
            esl = g % 2
            for d in range(2):
                bank = EF if d == 0 else EB
                S.add('act', lambda e, bank=bank, d=d, esl=esl: e.activation(
                    out=E_s[d][esl].rearrange("p a b -> p (a b)"), in_=PB(bank)[:, 0:384], func=AF.Exp),
                      reads=[('ps', bank)], writes=[('E_s', d, esl)])

        Wd = [Wt, Es]

        def W_y(cc, g):
            esl = g % 2
            for d in range(2):
                S.add('dve', lambda e, esl=esl, g=g, d=d: e.tensor_tensor(
                    out=Wd[d][esl], in0=E_s[d][esl], in1=bc_mid(PB(SC)[:, g * 128:(g + 1) * 128], 3), op=ALU.mult),
                      reads=[('E_s', d, esl), ('ps', SC)], writes=[('Wd', d, esl)])
            for hh in range(3):
                h = 3 * g + hh
                yb, yc = (Y0, h * 64) if h < 8 else (Y1, (h - 8) * 64)
                for d in range(2):
                    S.add('pe', lambda e, esl=esl, hh=hh, h=h, yb=yb, yc=yc, cc=cc, d=d: e.matmul(
                        PB(yb)[:, yc:yc + 64], lhsT=Wd[d][esl][:, hh, :], rhs=x_tok[:, cc, h * 64:(h + 1) * 64],
                        start=(d == 0), stop=False), reads=[('Wd', d, esl)], writes=[('ps', yb)])
                S.add('pe', lambda e, h=h, yb=yb, yc=yc, cc=cc: e.matmul(
                    PB(yb)[:, yc:yc + 64], lhsT=Dsk[:, h, :], rhs=x_tok[:, cc, h * 64:(h + 1) * 64], start=False, stop=True),
                      reads=['Dsk'], writes=[('ps', yb)])

        yvs = [yv, MF4.alloc([768], F32)]

        def head(cc):
            l = cc - 2
            t0 = l * 128
            S.add('act', lambda e: e.activation(out=hF_bf, in_=hF, func=AF.Identity), reads=['hF'], writes=['hF_bf'])
            for g in range(4):
                S.add('pe', lambda e, g=g, t0=t0: e.matmul(PB(SC)[:, g * 128:(g + 1) * 128], lhsT=xbcBC[:, g, t0:t0 + 128],
                                                           rhs=xbcBC[:, 4 + g, t0:t0 + 128], start=True, stop=True),
                      reads=[], writes=[('ps', SC)])
            E_mm(cc, 0)
            E_exp(0)
            for d, tdst, hsrc, hk in ((0, t1, hF_bf, 'hF_bf'), (1, t2, hinB[:, l, :], ('hinB', l))):
                for g in range(4):
                    bank = ST0 + g // 2
                    col = (g % 2) * 192
                    S.add('pe', lambda e, g=g, bank=bank, col=col, hsrc=hsrc, t0=t0: e.matmul(
                        PB(bank)[:, col:col + 192], lhsT=xbcBC[:, 4 + g, t0:t0 + 128], rhs=hsrc[:, g * 192:(g + 1) * 192],
                        start=True, stop=True), reads=[hk], writes=[('ps', bank)])
                for half in range(2):
                    S.add('dve', lambda e, half=half, tdst=tdst, d=d, cc=cc: e.tensor_tensor(
                        out=tdst[:, half * 384:(half + 1) * 384].rearrange("p (h d) -> p h d", h=6),
                        in0=PB(ST0 + half)[:, 0:384].rearrange("p (h d) -> p h d", h=6),
                        in1=bc_last(ea[:, cc, 12 * d + 6 * half:12 * d + 6 * half + 6], 64), op=ALU.mult),
                          reads=[('ps', ST0 + half)], writes=[('t', d)])
                if d == 0:
                    E_mm(cc, 1)
                    E_exp(1)
            S.add('pool', lambda e: e.tensor_tensor(out=t1, in0=t1, in1=t2, op=ALU.add), reads=[('t', 0), ('t', 1)], writes=[('t', 0)])

        def body(cc, mid_hook=None):
            yvc = yvs[cc % 2]
            ykey = ('yv', cc % 2)
            W_y(cc, 0)
            E_mm(cc, 2)
            E_exp(2)
            slot_f = st_a(cc, 0)
            W_y(cc, 1)
            if mid_hook is not None:
                mid_hook()
            E_mm(cc, 3)
            E_exp(3)
            W_y(cc, 2)
            st_b(cc, 0, hF, 'hF', slot_f)
            W_y(cc, 3)
            S.add('dve', lambda e: e.tensor_tensor(out=yvc[:, 0:512], in0=PB(Y0), in1=t1[:, 0:512], op=ALU.add),
                  reads=[('ps', Y0), ('t', 0)], writes=[ykey])
            S.add('dve', lambda e: e.tensor_tensor(out=yvc[:, 512:768], in0=PB(Y1)[:, 0:256], in1=t1[:, 512:768], op=ALU.add),
                  reads=[('ps', Y1), ('t', 0)], writes=[ykey])

        def tail2a(cc):
            l = cc - 2
            t0 = l * 128
            yvc = yvs[cc % 2]
            ykey = ('yv', cc % 2)
            for j in range(6):
                S.add('pe', lambda e, j=j, t0=t0: e.transpose(out=PBH(TR)[:, j * 128:(j + 1) * 128], in_=zcm[:, j, t0:t0 + 128], identity=ident),
                      reads=[], writes=[('ps', TR)])
            S.add('dve', lambda e: e.tensor_tensor(out=yvc, in0=yvc, in1=PBH(TR)[:, 0:768], op=ALU.mult), reads=[ykey, ('ps', TR)], writes=[ykey])
            S.add('act', lambda e, l=l: e.activation(out=junky, in_=yvc, func=AF.Square, accum_out=ssqy[:, l:l + 1]),
                  reads=[ykey], writes=['junky', ('ssqy', l)])
            S.add('act', lambda e, l=l: e.activation(out=rstdy[:, l:l + 1], in_=ssqy[:, l:l + 1], func=AF.Ln, scale=1.0 / 768, bias=epsb),
                  reads=[('ssqy', l)], writes=[('rstdy', l)])
            S.add('act', lambda e, l=l: e.activation(out=rstdy[:, l:l + 1], in_=rstdy[:, l:l + 1], func=AF.Exp, scale=-0.5),
                  reads=[('rstdy', l)], writes=[('rstdy', l)])
            S.add('act', lambda e, l=l: e.activation(out=yn, in_=yvc, func=AF.Copy, scale=rstdy[:, l:l + 1]),
                  reads=[ykey, ('rstdy', l)], writes=['yn'])

        def tail2b(cc):
            l = cc - 2
            for j in range(6):
                S.add('pe', lambda e, j=j: e.transpose(out=PBH(TR)[:, j * 128:(j + 1) * 128], in_=yn[:, j * 128:(j + 1) * 128], identity=ident),
                      reads=['yn'], writes=[('ps', TR)])
            evac(TR, hinB[:, l, :], PBH(TR)[:, 0:768], [('ps', TR)], [('hinB', l)])

        for cc in (0, 1):
            st_step(cc, 0, hF, 'hF')
        head(2)
        body(2)
        for cc in range(3, NCH):
            head(cc)
            tail2a(cc - 1)
            body(cc, mid_hook=(lambda c=cc - 1: tail2b(c)))
        tail2a(NCH - 1)
        tail2b(NCH - 1)
        dump('ynT0', hinB[:, 0, :], [('hinB', 0)])
        dump('ynT15', hinB[:, 15, :], [('hinB', 15)])
        if stage == 4:
            finalize()
            return nc

        S.barrier()
        MG1 = Mem(arena, off_xbcX, base=off_r1 + 2 * T * 2)
        wout = MG1.alloc([8, D], BF16)
        tt = MG1.alloc([D], F32)
        xs2 = [MG1.alloc([D], BF16) for _ in range(2)]
        junkg = MG1.alloc([D], BF16)
        MG = Mem(arena, ARENA, base=off_uT)
        xres = MG.alloc([16, D], F32)
        h2T = MG.alloc([8, T], BF16)
        xt2 = [MG.alloc([D], F32) for _ in range(2)]
        ssa = MG.alloc([16, 4], F32)
        ssb = MG.alloc([16, 2], F32)
        ssc = MG.alloc([16, 4], F32)
        S.add('pool', lambda e: e.dma_start(out=wout, in_=wout_d.rearrange("(k p) n -> p k n", p=128)), writes=['wout'], dma='wout')
        for j in range(6):
            S.add('dve', lambda e, j=j: e.tensor_scalar(out=wout[:, 2 + j, :], in0=wout[:, 2 + j, :], scalar1=gssd_col[:, j:j + 1], scalar2=None,
                                                        op0=ALU.mult), reads=['wout'], writes=['wout'])
        def g_mm(i):
            sl = i % 2
            S.add('sp', lambda e, i=i, sl=sl: e.dma_start(out=xt2[sl], in_=x_d[i * 128:(i + 1) * 128, :]), writes=[('xt2', sl)], dma=('xt2', sl))
            ob = (0, 1) if i % 2 == 0 else (2, 3)
            for nh in range(2):
                for k in range(8):
                    lt = YfT[:, k, i * 128:(i + 1) * 128] if k < 2 else hinB[:, i, (k - 2) * 128:(k - 1) * 128]
                    S.add('pe', lambda e, nh=nh, k=k, lt=lt, ob=ob: e.matmul(PB(ob[nh]), lhsT=lt, rhs=wout[:, k, nh * 512:(nh + 1) * 512],
                                                                            start=(k == 0), stop=(k == 7)),
                          reads=['wout'], writes=[('ps', ob[nh])])

        def g_s1a(i):
            ob = (0, 1) if i % 2 == 0 else (2, 3)
            for nh in range(2):
                S.add('act', lambda e, nh=nh, ob=ob, i=i: e.activation(out=junkg[:, 0:512], in_=PB(ob[nh]), func=AF.Square,
                                                                       accum_out=ssa[:, i, nh:nh + 1]),
                      reads=[('ps', ob[nh])], writes=['junkg', ('ssa', i, nh)])
            for nh in range(2):
                S.add('dve', lambda e, nh=nh, ob=ob: e.tensor_tensor(out=tt[:, nh * 512:(nh + 1) * 512], in0=PB(ob[nh]),
                                                                     in1=G1[:, nh * 512:(nh + 1) * 512], op=ALU.mult),
                      reads=[('ps', ob[nh])], writes=['tt'])
            S.add('dve', lambda e, i=i: e.tensor_tensor(out=ssa[:, i, 2:3], in0=ssa[:, i, 0:1], in1=ssa[:, i, 1:2], op=ALU.add),
                  reads=[('ssa', i, 0), ('ssa', i, 1)], writes=[('ssa', i, 2)])

        def g_s1b(i):
            sl = i % 2
            S.add('act', lambda e, i=i: e.activation(out=ssa[:, i, 3:4], in_=ssa[:, i, 2:3], func=AF.Ln, scale=1.0 / D, bias=epsb),
                  reads=[('ssa', i, 2)], writes=[('ssa', i, 3)])
            S.add('act', lambda e, i=i: e.activation(out=ssa[:, i, 3:4], in_=ssa[:, i, 3:4], func=AF.Exp, scale=-0.5),
                  reads=[('ssa', i, 3)], writes=[('ssa', i, 3)])
            S.add('dve', lambda e, i=i, sl=sl: e.scalar_tensor_tensor(out=xres[:, i, :], in0=tt, scalar=ssa[:, i, 3:4], in1=xt2[sl],
                                                                      op0=ALU.mult, op1=ALU.add),
                  reads=['tt', ('ssa', i, 3), ('xt2', sl)], writes=[('xres', i)])

        def g_s2a(i):
            S.add('act', lambda e, i=i: e.activation(out=junkg2, in_=xres[:, i, :], func=AF.Square, accum_out=ssb[:, i, 0:1]),
                  reads=[('xres', i)], writes=['junkg2', ('ssb', i, 0)])
            S.add('act', lambda e, i=i: e.activation(out=ssb[:, i, 1:2], in_=ssb[:, i, 0:1], func=AF.Ln, scale=1.0 / D, bias=epsb),
                  reads=[('ssb', i, 0)], writes=[('ssb', i, 1)])
            S.add('act', lambda e, i=i: e.activation(out=ssb[:, i, 1:2], in_=ssb[:, i, 1:2], func=AF.Exp, scale=-0.5),
                  reads=[('ssb', i, 1)], writes=[('ssb', i, 1)])

        def g_s2b(i):
            sl = i % 2
            ii = i % 4
            S.add('dve', lambda e, i=i, sl=sl: e.scalar_tensor_tensor(out=xs2[sl], in0=xres[:, i, :], scalar=ssb[:, i, 1:2], in1=a2_bc,
                                                                      op0=ALU.mult, op1=ALU.mult),
                  reads=[('xres', i), ('ssb', i, 1)], writes=[('xs2', sl)])
            for j in range(8):
                pb = 4 + j // 2
                col = (j % 2) * 512 + ii * 128
                S.add('pe', lambda e, j=j, sl=sl, pb=pb, col=col: e.transpose(out=PBH(pb)[:, col:col + 128],
                                                                             in_=xs2[sl][:, j * 128:(j + 1) * 128], identity=ident),
                      reads=[('xs2', sl)], writes=[('ps', pb)])

        junkg2 = MG.alloc([D], BF16)
        g_mm(0)
        g_mm(1)
        g_s1a(0)
        g_s1b(0)
        for i in range(16):
            if i + 2 < 16:
                g_mm(i + 2)
            if i + 1 < 16:
                g_s1a(i + 1)
            g_s2a(i)
            if i + 1 < 16:
                g_s1b(i + 1)
            g_s2b(i)
            if i % 4 == 3:
                g4 = i // 4
                for j in range(8):
                    pb = 4 + j // 2
                    c0 = (j % 2) * 512
                    evac(pb, h2T[:, j, g4 * 512:(g4 + 1) * 512], PBH(pb)[:, c0:c0 + 512], [('ps', pb)], [('h2T', j, g4)],
                         bias=sfcol[:, j, 0:1])
        dump('xres0', xres[:, 0, :], [('xres', 0)])
        dump('h2T', h2T[:, 0, 0:512], [('h2T', 0, 0)])
        if stage == 5:
            finalize()
            return nc

        S.barrier()
        MH = Mem(arena, off_uT, base=off_r1)
        hT = MH.alloc([NFC, 1024], BF16)
        wgs = [MH.alloc([8, 256], BF16) for _ in range(2)]
        wus = [MH.alloc([8, 256], BF16) for _ in range(2)]
        MHc = Mem(arena, off_g1e, base=off_abc)
        wds = [MHc.alloc([2, 512], BF16) for _ in range(4)]
        sg = [MHc.alloc([512], BF16) for _ in range(2)]
        junkh = MHc.alloc([512], BF16)
        outs = [xt2[0], xt2[1], MG.alloc([D], F32)]
        wg_v = wg_d.rearrange("(k p) n -> p k n", p=128)
        wu_v = wu_d.rearrange("(k p) n -> p k n", p=128)
        for hh in range(2):
            for fp in range(NFC // 2):
                sl = fp % 2
                S.add('pool', lambda e, fp=fp, sl=sl: e.dma_start(out=wgs[sl], in_=wg_v[:, :, fp * 256:(fp + 1) * 256]),
                      writes=[('wgs', sl)], dma=('wgs', sl))
                S.add('pool', lambda e, fp=fp, sl=sl: e.dma_start(out=wus[sl], in_=wu_v[:, :, fp * 256:(fp + 1) * 256]),
                      writes=[('wus', sl)], dma=('wus', sl))
                for fi in range(2):
                    fc = fp * 2 + fi
                    for tg in range(2):
                        tok0 = hh * 1024 + tg * 512
                        gb, ub = (0, 1) if tg == 0 else (2, 3)
                        for k in range(8):
                            S.add('pe', lambda e, k=k, sl=sl, gb=gb, tok0=tok0, fi=fi: e.matmul(
                                PB(gb), lhsT=wgs[sl][:, k, fi * 128:(fi + 1) * 128], rhs=h2T[:, k, tok0:tok0 + 512],
                                start=(k == 0), stop=(k == 7)),
                                  reads=[('wgs', sl), ('h2T', k, hh)], writes=[('ps', gb)])
                        for k in range(8):
                            S.add('pe', lambda e, k=k, sl=sl, ub=ub, tok0=tok0, fi=fi: e.matmul(
                                PB(ub), lhsT=wus[sl][:, k, fi * 128:(fi + 1) * 128], rhs=h2T[:, k, tok0:tok0 + 512],
                                start=(k == 0), stop=(k == 7)),
                                  reads=[('wus', sl), ('h2T', k, hh)], writes=[('ps', ub)])
                        S.add('act', lambda e, tg=tg, gb=gb: e.activation(out=sg[tg], in_=PB(gb), func=AF.Silu), reads=[('ps', gb)], writes=[('sg', tg)])
                        S.add('dve', lambda e, tg=tg, ub=ub, fc=fc: e.tensor_tensor(out=hT[:, fc, tg * 512:(tg + 1) * 512], in0=sg[tg], in1=PB(ub),
                                                                                   op=ALU.mult),
                              reads=[('sg', tg), ('ps', ub)], writes=[('hT', fc, tg)])
            for nh in range(2):
                for fp in range(NFC // 2):
                    sl = fp % 4
                    S.add('pool', lambda e, fp=fp, sl=sl, nh=nh: e.dma_start(
                        out=wds[sl], in_=wd_d[fp * 256:(fp + 1) * 256, nh * 512:(nh + 1) * 512].rearrange("(a p) n -> p a n", p=128)),
                          writes=[('wds', sl)], dma=('wds', sl))
                    for fi in range(2):
                        fc = fp * 2 + fi
                        for t in range(8):
                            S.add('pe', lambda e, fc=fc, sl=sl, t=t, fi=fi: e.matmul(
                                PB(t), lhsT=hT[:, fc, t * 128:(t + 1) * 128], rhs=wds[sl][:, fi, :],
                                start=(fc == 0), stop=(fc == NFC - 1)),
                                  reads=[('wds', sl), ('hT', fc, t // 4)], writes=[('ps', t)])
                for t in range(8):
                    i = hh * 8 + t
                    S.add('act', lambda e, nh=nh, t=t, i=i: e.activation(out=junkh, in_=PB(t), func=AF.Square,
                                                                         accum_out=ssc[:, i, nh:nh + 1]),
                          reads=[('ps', t)], writes=['junkh', ('ssc', i, nh)])
                    tq = h2T[:, t, hh * 1024:(hh + 1) * 1024].bitcast(F32)
                    if nh == 0:
                        S.add('dve', lambda e, t=t, tq=tq: e.tensor_tensor(out=tq, in0=PB(t), in1=G2[:, 0:512], op=ALU.mult),
                              reads=[('ps', t)], writes=[('h2T', t, hh)])
                        continue
                    ob_i = i % 3
                    ot = outs[ob_i]
                    S.add('dve', lambda e, t=t, ot=ot: e.tensor_tensor(out=ot[:, 512:1024], in0=PB(t), in1=G2[:, 512:1024], op=ALU.mult),
                          reads=[('ps', t)], writes=[('outt', ob_i)])
                    S.add('dve', lambda e, i=i: e.tensor_tensor(out=ssc[:, i, 2:3], in0=ssc[:, i, 0:1], in1=ssc[:, i, 1:2], op=ALU.add),
                          reads=[('ssc', i, 0), ('ssc', i, 1)], writes=[('ssc', i, 2)])
                    S.add('act', lambda e, i=i: e.activation(out=ssc[:, i, 3:4], in_=ssc[:, i, 2:3], func=AF.Ln, scale=1.0 / D, bias=epsb),
                          reads=[('ssc', i, 2)], writes=[('ssc', i, 3)])
                    S.add('act', lambda e, i=i: e.activation(out=ssc[:, i, 3:4], in_=ssc[:, i, 3:4], func=AF.Exp, scale=-0.5),
                          reads=[('ssc', i, 3)], writes=[('ssc', i, 3)])
                    S.add('dve', lambda e, i=i, ot=ot: e.scalar_tensor_tensor(
                        out=ot[:, 512:1024], in0=ot[:, 512:1024], scalar=ssc[:, i, 3:4], in1=xres[:, i, 512:1024],
                        op0=ALU.mult, op1=ALU.add), reads=[('outt', ob_i), ('ssc', i, 3)], writes=[('outt', ob_i)])
                    S.add('dve', lambda e, i=i, tq=tq, ot=ot: e.scalar_tensor_tensor(
                        out=ot[:, 0:512], in0=tq, scalar=ssc[:, i, 3:4], in1=xres[:, i, 0:512],
                        op0=ALU.mult, op1=ALU.add), reads=[('h2T', t, hh), ('ssc', i, 3)], writes=[('outt', ob_i)])
                    S.add('sp', lambda e, i=i, ot=ot: e.dma_start(out=out_d[i * 128:(i + 1) * 128, :], in_=ot), reads=[('outt', ob_i)],
                          writes=[('out', i)], dma=('outd', ob_i))

        finalize()
    return nc


def _consts():
    t = np.arange(T, dtype=np.float64)
    ang = 2.0 * np.pi * np.outer(t, t) / T
    dftc = np.cos(ang).astype(np.float32).astype(ml_dtypes.bfloat16)
    dfts = (-np.sin(ang)).astype(np.float32).astype(ml_dtypes.bfloat16)
    m = np.arange(64, dtype=np.float64)
    a64 = 2.0 * np.pi * np.outer(m, m) / 64
    sc = 1.0 / math.sqrt(T * 64)
    cs = np.zeros((256, 512), np.float64)
    for g in range(4):
        cs[g * 64:(g + 1) * 64, g * 64:(g + 1) * 64] = np.cos(a64) * sc
        cs[g * 64:(g + 1) * 64, 256 + g * 64:256 + (g + 1) * 64] = np.sin(a64) * sc
    return dftc, dfts, cs.astype(np.float32).astype(ml_dtypes.bfloat16)


def make_in_maps(inp):
    f = lambda a: np.ascontiguousarray(np.asarray(a, dtype=np.float32))
    dftc, dfts, cs64 = _consts()
    shared = {
        "w_ada": f(inp["w_ada"][0]), "b_ada": f(inp["b_ada"][0]).reshape(1, -1),
        "g_pre_mix": f(inp["g_pre_mix"][0]).reshape(1, -1), "g_post_mix": f(inp["g_post_mix"][0]).reshape(1, -1),
        "g_pre_ffn": f(inp["g_pre_ffn"][0]).reshape(1, -1), "g_post_ffn": f(inp["g_post_ffn"][0]).reshape(1, -1),
        "w_in": f(inp["w_in"][0]), "conv_w": f(inp["conv_w"][0]), "conv_b": f(inp["conv_b"][0]).reshape(-1, 1),
        "dt_bias": f(inp["dt_bias"][0]).reshape(24, 1), "a_log": f(inp["a_log"][0]).reshape(1, 24),
        "d_skip": f(inp["d_skip"][0]).reshape(1, 12), "g_ssd": f(inp["g_ssd"][0]).reshape(-1, 1),
        "w_out": f(inp["w_out"][0]), "w_gate": f(inp["w_gate"][0]), "w_up": f(inp["w_up"][0]), "w_down": f(inp["w_down"][0]),
        "dftc": dftc, "dfts": dfts, "cs64": cs64,
    }
    x = f(inp["x"])
    c = f(inp["c"])
    ctx = f(inp["ctx"])
    cc = f(inp["c_ctx"])
    maps = []
    for b in range(8):
        m = dict(shared)
        m["x"] = x[b]
        m["ctx"] = ctx[b]
        m["c2"] = np.ascontiguousarray(np.stack([c[b], cc], axis=0))
        maps.append(m)
    return maps


def kernel(**inputs):
    nc = build_nc()
    maps = make_in_maps(inputs)
    res = run_bass_kernel_spmd(nc, maps, core_ids=list(range(8)))
    return np.stack([np.asarray(r["out"], dtype=np.float32) for r in res.results], axis=0)
```
